# Optimizing a Trainium2 kernel written in Bass

```python
import math
import jax
import jax.numpy as jnp
from jax import lax
import numpy as np

D_MODEL = 4096
BATCH = 2
SEQ = 4096
DEPTH = 4

GRID_W = 64
CTX_LEN = 256
MIX_WIDTH = D_MODEL
NA_WIDTH = MIX_WIDTH // 2
GLA_WIDTH = MIX_WIDTH - NA_WIDTH
NA_HEAD_DIM = 128
NA_HEADS = NA_WIDTH // NA_HEAD_DIM
NA_KH = 8
NA_KW = 16
NA_QB = 16
NA_KSPAN = 32
NA_NCB = GRID_W // NA_QB
GLA_HEADS = 4
GLA_DV = GLA_WIDTH // GLA_HEADS
GLA_DK = GLA_DV // 2
GLA_KEY_WIDTH = GLA_HEADS * GLA_DK
GLA_GATE_RANK = 16
GLA_GATE_NORM = 16.0
GLA_CHUNK = 64
ROPE_BASE = 10000.0
EPS = 1e-6
NEG_INF = -1e30

IN_SIZES = (NA_WIDTH, NA_WIDTH, NA_WIDTH, NA_WIDTH,
            GLA_KEY_WIDTH, GLA_KEY_WIDTH, GLA_WIDTH, GLA_WIDTH,
            GLA_GATE_RANK, GLA_GATE_RANK)
IN_DIM = sum(IN_SIZES)
IN_SPLITS = tuple(int(s) for s in np.cumsum(IN_SIZES)[:-1])

kernel_name = 'hybrid_natten_gla_prefix_dit'


def rmsnorm(x, g):
    xf = x.astype(jnp.float32)
    y = xf * lax.rsqrt(jnp.mean(xf * xf, axis=-1, keepdims=True) + EPS)
    return (y * g.astype(jnp.float32)).astype(x.dtype)


def _rotate(x, pos):
    nf = x.shape[-1] // 2
    inv = ROPE_BASE ** (-jnp.arange(nf, dtype=jnp.float32) / nf)
    ang = pos.astype(jnp.float32)[:, None] * inv[None, :]
    cos = jnp.cos(ang)[None, :, None, :].astype(x.dtype)
    sin = jnp.sin(ang)[None, :, None, :].astype(x.dtype)
    x1, x2 = x[..., :nf], x[..., nf:]
    return jnp.concatenate([x1 * cos - x2 * sin, x1 * sin + x2 * cos], axis=-1)


def axial_rope(x, rows, cols):
    half = x.shape[-1] // 2
    return jnp.concatenate([_rotate(x[..., :half], rows), _rotate(x[..., half:], cols)], axis=-1)


def neighborhood_attention(q, k, v, k_ctx, v_ctx, rpb):
    B, L, H, Dh = q.shape
    rows = L // GRID_W
    kh = min(NA_KH, rows)
    scale = Dh ** -0.5
    qg = q.reshape(B, rows, NA_NCB, NA_QB, H, Dh)
    kg = k.reshape(B, rows, GRID_W, H, Dh)
    vg = v.reshape(B, rows, GRID_W, H, Dh)
    qcol = np.arange(GRID_W).reshape(NA_NCB, NA_QB)
    kstart = np.clip(np.arange(NA_NCB) * NA_QB - NA_KW // 2, 0, GRID_W - NA_KSPAN)
    kcol = kstart[:, None] + np.arange(NA_KSPAN)[None, :]
    cstart = np.clip(qcol - NA_KW // 2, 0, GRID_W - NA_KW)
    col_ok = (kcol[:, None, :] >= cstart[:, :, None]) & (kcol[:, None, :] < cstart[:, :, None] + NA_KW)
    dx_idx = np.clip(kcol[:, None, :] - qcol[:, :, None] + NA_KW - 1, 0, 2 * NA_KW - 2)
    rpb_x = rpb[:, :, dx_idx]
    n_loc = kh * NA_KSPAN

    def row_block(r):
        rs = jnp.clip(r - kh // 2, 0, rows - kh)
        k_blk = lax.dynamic_slice_in_dim(kg, rs, kh, axis=1)[:, :, kcol]
        v_blk = lax.dynamic_slice_in_dim(vg, rs, kh, axis=1)[:, :, kcol]
        q_r = lax.dynamic_index_in_dim(qg, r, axis=1, keepdims=False)
        s_loc = jnp.einsum('bnqhd,bynkhd->bhnqyk', q_r, k_blk).astype(jnp.float32) * scale
        dy_idx = rs + jnp.arange(kh) - r + NA_KH - 1
        bias = jnp.take(rpb_x, dy_idx, axis=1).transpose(0, 2, 3, 1, 4)
        s_loc = jnp.where(col_ok[:, :, None, :], s_loc + bias[None].astype(jnp.float32), NEG_INF)
        s_ctx = jnp.einsum('bnqhd,bchd->bhnqc', q_r, k_ctx).astype(jnp.float32) * scale
        s = jnp.concatenate([s_loc.reshape(B, H, NA_NCB, NA_QB, n_loc), s_ctx], axis=-1)
        p = jax.nn.softmax(s, axis=-1).astype(v.dtype)
        p_loc = p[..., :n_loc].reshape(B, H, NA_NCB, NA_QB, kh, NA_KSPAN)
        p_ctx = p[..., n_loc:]
        return (jnp.einsum('bhnqyk,bynkhd->bnqhd', p_loc, v_blk)
                + jnp.einsum('bhnqc,bchd->bnqhd', p_ctx, v_ctx))

    o = lax.map(row_block, jnp.arange(rows))
    return o.transpose(1, 0, 2, 3, 4, 5).reshape(B, L, H * Dh)


def context_attention(q, k, v):
    B, CL, H, Dh = q.shape
    s = jnp.einsum('bqhd,bkhd->bhqk', q, k).astype(jnp.float32) * (Dh ** -0.5)
    p = jax.nn.softmax(s, axis=-1).astype(v.dtype)
    return jnp.einsum('bhqk,bkhd->bqhd', p, v).reshape(B, CL, H * Dh)


def gla_chunk_scan(q, k, v, g, s0):
    B, L, H, DK = q.shape
    DV = v.shape[-1]
    C = GLA_CHUNK
    nc = L // C

    def chunks(t):
        return t.reshape(B, nc, C, H, t.shape[-1]).transpose(1, 0, 3, 2, 4)

    tril = jnp.tril(jnp.ones((C, C), dtype=bool))

    def step(state, inp):
        qc, kc, vc, gc = inp
        bc = jnp.cumsum(gc, axis=2)
        b_last = bc[:, :, -1]
        o_inter = jnp.einsum('bhtd,bhde->bhte', qc * jnp.exp(bc), state)
        diff = bc[:, :, :, None, :] - bc[:, :, None, :, :]
        decay = jnp.where(tril[:, :, None], jnp.exp(jnp.minimum(diff, 0.0)), 0.0)
        attn = jnp.einsum('bhtd,bhsd,bhtsd->bhts', qc, kc, decay)
        o_intra = jnp.einsum('bhts,bhse->bhte', attn, vc)
        k_dec = kc * jnp.exp(b_last[:, :, None, :] - bc)
        state = state * jnp.exp(b_last)[..., None] + jnp.einsum('bhsd,bhse->bhde', k_dec, vc)
        return state, o_inter + o_intra

    s_final, o = lax.scan(step, s0, (chunks(q), chunks(k), chunks(v), chunks(g)))
    return o.transpose(1, 0, 3, 2, 4).reshape(B, L, H, DV), s_final


def gla_final_state(k, v, g):
    b = jnp.cumsum(g, axis=1)
    return jnp.einsum('blhd,blhe->bhde', k * jnp.exp(b[:, -1:] - b), v)


def gla_bidirectional(q, k, v, g_fwd, g_bwd, q_c, k_c, v_c, g_fwd_c, g_bwd_c, with_ctx_out):
    f32 = jnp.float32
    q, k, v, q_c, k_c, v_c = (t.astype(f32) for t in (q, k, v, q_c, k_c, v_c))
    flip = lambda t: t[:, ::-1]
    B, _, H, DK = q.shape
    DV = v.shape[-1]
    if with_ctx_out:
        s0 = jnp.zeros((B, H, DK, DV), f32)
        o_cf, s_f = gla_chunk_scan(q_c, k_c, v_c, g_fwd_c, s0)
        o_cb, s_b = gla_chunk_scan(flip(q_c), flip(k_c), flip(v_c), flip(g_bwd_c), s0)
        o_ctx = o_cf + flip(o_cb)
    else:
        s_f = gla_final_state(k_c, v_c, g_fwd_c)
        s_b = gla_final_state(flip(k_c), flip(v_c), flip(g_bwd_c))
        o_ctx = None
    o_f, _ = gla_chunk_scan(q, k, v, g_fwd, s_f)
    o_b, _ = gla_chunk_scan(flip(q), flip(k), flip(v), flip(g_bwd), s_b)
    return o_f + flip(o_b), o_ctx


def gla_merge(o, gla_g, dtype):
    B, L, H, DV = o.shape
    return rmsnorm(o, gla_g).reshape(B, L, H * DV).astype(dtype)


def hybrid_layer(x, xc, mod, mod_c, norm_g, w_in, rpb, w_decay, b_decay, gla_g, w_out,
                 rows, cols, with_ctx_out):
    shift, scale, gate = jnp.split(mod, 3, axis=-1)
    shift_c, scale_c, gate_c = jnp.split(mod_c, 3, axis=-1)
    h = rmsnorm(x, norm_g) * (1.0 + scale[:, None]) + shift[:, None]
    hc = rmsnorm(xc, norm_g) * (1.0 + scale_c) + shift_c
    naq, nak, nav, nag, gq, gk, gv, gg, gf, gb = jnp.split(h @ w_in, IN_SPLITS, axis=-1)
    naq_c, nak_c, nav_c, nag_c, gq_c, gk_c, gv_c, gg_c, gf_c, gb_c = jnp.split(hc @ w_in, IN_SPLITS, axis=-1)

    nh = lambda t: t.reshape(t.shape[0], t.shape[1], NA_HEADS, NA_HEAD_DIM)
    kh = lambda t: t.reshape(t.shape[0], t.shape[1], GLA_HEADS, GLA_DK)
    vh = lambda t: t.reshape(t.shape[0], t.shape[1], GLA_HEADS, GLA_DV)

    def decay(lr, d):
        logit = (lr @ w_decay[d] + b_decay[d]).astype(jnp.float32)
        return kh(jax.nn.log_sigmoid(logit) / GLA_GATE_NORM)

    q_scale = GLA_DK ** -0.5
    na_lat = neighborhood_attention(nh(naq), nh(nak), nh(nav), nh(nak_c), nh(nav_c), rpb)
    gla_lat, gla_ctx = gla_bidirectional(
        axial_rope(kh(gq) * q_scale, rows, cols), axial_rope(kh(gk), rows, cols), vh(gv),
        decay(gf, 0), decay(gb, 1),
        kh(gq_c) * q_scale, kh(gk_c), vh(gv_c), decay(gf_c, 0), decay(gb_c, 1),
        with_ctx_out)
    y = jnp.concatenate([na_lat * jax.nn.silu(nag),
                         gla_merge(gla_lat, gla_g, x.dtype) * jax.nn.silu(gg)], axis=-1) @ w_out
    x = x + gate[:, None] * y
    if with_ctx_out:
        na_ctx = context_attention(nh(naq_c), nh(nak_c), nh(nav_c))
        yc = jnp.concatenate([na_ctx * jax.nn.silu(nag_c),
                              gla_merge(gla_ctx, gla_g, xc.dtype) * jax.nn.silu(gg_c)], axis=-1) @ w_out
        xc = xc + gate_c * yc
    return x, xc


def setup_inputs(seed: int = 0) -> dict:
    key = jax.random.key(seed)
    ks = jax.random.split(key, 14)
    f32 = jnp.float32
    D = D_MODEL
    nrm = lambda k, shape: jax.random.normal(k, shape, f32)
    return {
        'x': nrm(ks[0], (BATCH, SEQ, D)),
        'c': nrm(ks[1], (BATCH, D)),
        'ctx': nrm(ks[2], (BATCH, CTX_LEN, D)),
        'c_ctx': nrm(ks[3], (D,)),
        'ada_w': nrm(ks[4], (DEPTH, D, 3 * D)) * D ** -0.5,
        'ada_b': 0.01 * nrm(ks[5], (DEPTH, 3 * D)),
        'norm_g': 1.0 + 0.02 * nrm(ks[6], (DEPTH, D)),
        'w_in': nrm(ks[7], (DEPTH, D, IN_DIM)) * D ** -0.5,
        'na_rpb': 0.02 * nrm(ks[8], (DEPTH, NA_HEADS, 2 * NA_KH - 1, 2 * NA_KW - 1)),
        'gla_w_decay': nrm(ks[9], (DEPTH, 2, GLA_GATE_RANK, GLA_KEY_WIDTH)) * GLA_GATE_RANK ** -0.5,
        'gla_b_decay': 0.1 * nrm(ks[10], (DEPTH, 2, GLA_KEY_WIDTH)),
        'gla_norm_g': 1.0 + 0.02 * nrm(ks[11], (DEPTH, GLA_DV)),
        'w_out': nrm(ks[12], (DEPTH, MIX_WIDTH, D)) * MIX_WIDTH ** -0.5,
        'final_norm_g': 1.0 + 0.02 * nrm(ks[13], (D,)),
    }


def reference(x, c, ctx, c_ctx, ada_w, ada_b, norm_g, w_in, na_rpb, gla_w_decay, gla_b_decay,
              gla_norm_g, w_out, final_norm_g):
    L = x.shape[1]
    t = jnp.arange(L)
    rows, cols = t // GRID_W, t % GRID_W
    silu_c = jax.nn.silu(c)
    silu_cc = jax.nn.silu(c_ctx)
    xc = ctx
    for l in range(DEPTH):
        mod = silu_c @ ada_w[l] + ada_b[l]
        mod_c = silu_cc @ ada_w[l] + ada_b[l]
        x, xc = hybrid_layer(x, xc, mod, mod_c, norm_g[l], w_in[l], na_rpb[l], gla_w_decay[l],
                             gla_b_decay[l], gla_norm_g[l], w_out[l], rows, cols,
                             with_ctx_out=(l < DEPTH - 1))
    return rmsnorm(x, final_norm_g)
```

```python
import numpy as np
import ml_dtypes
import concourse.bass as bass
import concourse.mybir as mybir
from concourse.bass_utils import run_bass_kernel_spmd

F32 = mybir.dt.float32
BF16 = mybir.dt.bfloat16
AF = mybir.ActivationFunctionType
ALU = mybir.AluOpType
NPBF = ml_dtypes.bfloat16

D = 4096
DEPTH = 4
SEQ = 4096
CTX = 256
NT = SEQ + CTX
GRID_W = 64
EPS = 1e-6
NEG = -1e30


class Buf:
    __slots__ = ("name", "w", "r")

    def __init__(self, name):
        self.name = name
        self.w = None
        self.r = []


class Sched:
    EPOCH = 20000

    def __init__(self, nc, n_dma_sems=40):
        self.nc = nc
        self.eng = {"pe": nc.tensor, "act": nc.scalar, "dve": nc.vector, "pool": nc.gpsimd, "sp": nc.sync}
        self.ops = []
        self.n_dma_sems = n_dma_sems
        self.done = 0
        self.ev = []
        self.deps = []
        self.cnt = {e: 0 for e in self.eng}
        self.n_dma = 0
        self.dma_prev = {}
        self.sems = {}
        self.seen = {e: {} for e in self.eng}
        self.last = {}
        self.bufs = []

    def buf(self, name):
        b = Buf(name)
        self.bufs.append(b)
        return b

    def op(self, eng, fn, reads=(), writes=()):
        self.ops.append((eng, fn, tuple(reads), tuple(writes), False))

    def dma(self, q, out, in_, reads=(), writes=()):
        self.ops.append((q, lambda e, o=out, i=in_: e.dma_start(out=o, in_=i), tuple(reads), tuple(writes), True))

    def cc(self, groups, cin, cout, reads=(), writes=()):
        def fn(e, cin=cin, cout=cout):
            return e.collective_compute("AllGather", ALU.bypass, replica_groups=groups,
                                        ins=[cin.ap().opt()], outs=[cout.ap().opt()])
        self.ops.append(("pool", fn, tuple(reads), tuple(writes), "cc"))

    def _sem(self, sk):
        if sk not in self.sems:
            self.sems[sk] = self.nc.semaphore("s_%s_%s" % (sk[0], sk[1])).__enter__()
        return self.sems[sk]

    def flush(self, barrier=False):
        ops = self.ops
        n = len(ops)
        for k in range(self.done, n):
            eng, fn, reads, writes, is_dma = ops[k]
            d = set()
            for b in reads:
                if b.w is not None:
                    d.add(b.w)
            for b in writes:
                if b.w is not None:
                    d.add(b.w)
                d.update(b.r)
            if is_dma == "cc":
                self.n_cc = getattr(self, "n_cc", 0) + 1
                evk = (("cc", 0), self.n_cc)
            elif is_dma:
                s = self.n_dma % self.n_dma_sems
                if s in self.dma_prev:
                    d.add(self.dma_prev[s])
                self.dma_prev[s] = k
                evk = (("dma", s), 16 * (self.n_dma // self.n_dma_sems + 1))
                self.n_dma += 1
            else:
                c = self.cnt[eng]
                evk = ((eng, c // self.EPOCH), c % self.EPOCH + 1)
                self.cnt[eng] = c + 1
            self.ev.append(evk)
            d.discard(k)
            for b in reads:
                b.r.append(k)
            for b in writes:
                b.w = k
                b.r = []
            e = self.eng[eng]
            need = {}
            for dk in d:
                if (not ops[dk][4]) and ops[dk][0] == eng and eng == "pe":
                    continue
                sk, v = self.ev[dk]
                if v > need.get(sk, 0):
                    need[sk] = v
            sn = self.seen[eng]
            for sk, v in need.items():
                if sn.get(sk, 0) < v:
                    e.wait_ge(self._sem(sk), v)
                    sn[sk] = v
            ins = fn(e)
            sk, v = evk
            ins.then_inc(self._sem(sk), 16 if is_dma is True else 1)
            if v > self.last.get(sk, 0):
                self.last[sk] = v
        self.done = n
        if barrier:
            for en, e in self.eng.items():
                sn = self.seen[en]
                for sk, v in self.last.items():
                    if sn.get(sk, 0) < v:
                        e.wait_ge(self._sem(sk), v)
                        sn[sk] = v
            for b in self.bufs:
                b.w = None
                b.r = []

    def emit(self):
        self.flush(False)
        for sk, v in self.last.items():
            self.nc.sync.wait_ge(self._sem(sk), v)
        self.n_ops = len(self.ops)


_UID = [0]


def _u(name):
    _UID[0] += 1
    return "%s_%d" % (name, _UID[0])


def _sb(nc, name, shape, dt):
    return nc.sbuf_tensor(_u(name), list(shape), dt).__enter__()


def _ps(nc, name, shape, dt=F32):
    return nc.psum_tensor(_u(name), list(shape), dt).__enter__()


def _din(nc, name, shape, dt):
    return nc.dram_tensor(name, list(shape), dt, kind="ExternalInput").ap()


def _dout(nc, name, shape, dt):
    return nc.dram_tensor(name, list(shape), dt, kind="ExternalOutput").ap()


def _dint(nc, name, shape, dt):
    return nc.dram_tensor(_u(name), list(shape), dt, kind="Internal").ap()


ADA_COLS = 3 * D // 8


def build_ada(depth=DEPTH):
    nc = bass.Bass("TRN2", target_bir_lowering=False)
    cT = _din(nc, "cT", [128, 32, 3], F32)
    adaw = _din(nc, "adaw", [depth, 32, 128, ADA_COLS], F32)
    adab = _din(nc, "adab", [depth, 3, ADA_COLS], F32)
    mod = _dout(nc, "mod", [depth, 3, ADA_COLS], F32)
    S = Sched(nc)
    ct = _sb(nc, "ct", [128, 32, 3], F32)
    st = _sb(nc, "st", [128, 32, 3], F32)
    NW = 3
    wt = [_sb(nc, "wt%d" % i, [128, ADA_COLS], F32) for i in range(NW)]
    bt = _sb(nc, "bt", [3, ADA_COLS], F32)
    rt = _sb(nc, "rt", [3, ADA_COLS], F32)
    pss = [_ps(nc, "ps%d" % i, [128, 512]) for i in range(3)]
    b_ct, b_st, b_bt, b_rt = Buf("ct"), Buf("st"), Buf("bt"), Buf("rt")
    b_wt = [Buf("wt%d" % i) for i in range(NW)]
    b_ps = [Buf("ps%d" % i) for i in range(3)]
    S.dma("sp", ct[:], cT[:, :, :], writes=[b_ct])
    S.op("act", lambda e: e.activation(out=st[:], in_=ct[:], func=AF.Silu), reads=[b_ct], writes=[b_st])
    it = 0
    for l in range(depth):
        S.dma("sp", bt[:], adab[l, :, :], writes=[b_bt])
        for kc in range(32):
            w = it % NW
            S.dma("sp", wt[w][:], adaw[l, kc, :, :], writes=[b_wt[w]])
            for j in range(3):
                S.op("pe", lambda e, w=w, j=j, kc=kc: e.matmul(
                    pss[j][0:3, :], lhsT=st[:, kc, :], rhs=wt[w][:, j * 512:(j + 1) * 512],
                    start=(kc == 0), stop=(kc == 31)), reads=[b_st, b_wt[w]], writes=[b_ps[j]])
            it += 1
        for j in range(3):
            S.op("dve", lambda e, j=j: e.tensor_tensor(
                out=rt[:, j * 512:(j + 1) * 512], in0=pss[j][0:3, :], in1=bt[:, j * 512:(j + 1) * 512], op=ALU.add),
                reads=[b_ps[j], b_bt], writes=[b_rt])
        S.dma("sp", mod[l, :, :], rt[:], reads=[b_rt])
    S.emit()
    return nc


PT = 256
NPT = NT // PT


def build_p(has_z, nt=NT):
    npt = nt // PT
    nc = bass.Bass("TRN2", target_bir_lowering=False)
    xT = _din(nc, "xT", [8, 128, nt], F32)
    gvec = _din(nc, "gvec", [128, 8], F32)
    scT = _din(nc, "scT", [128, 8, 2], F32)
    ones_in = _din(nc, "ones", [128, 128], F32)
    if has_z:
        zT = _din(nc, "zT", [32, 128, nt], BF16)
        wout = _din(nc, "wout", [32, 128, 1024], F32)
        gateT = _din(nc, "gateT", [128, 8, 2], F32)
        xoT = _dout(nc, "xoT", [8, 128, nt], F32)
    uT = _dout(nc, "uT", [8, 128, nt], BF16)
    ssq = _dout(nc, "ssq", [1, nt], F32)
    S = Sched(nc)
    gv = _sb(nc, "gv", [128, 8], F32)
    sc = _sb(nc, "sc", [128, 8, 2], F32)
    G = _sb(nc, "G", [128, 8, 2], F32)
    ones = _sb(nc, "ones_sb", [128, 128], F32)
    ssq_sb = _sb(nc, "ssq_sb", [1, nt], F32)
    xt = [_sb(nc, "xt%d" % i, [128, 8, PT], F32) for i in range(2)]
    ut = [_sb(nc, "ut%d" % i, [128, 8, PT], BF16) for i in range(2)]
    sq = [_sb(nc, "sq%d" % i, [128, PT], F32) for i in range(2)]
    b_gv, b_sc, b_G, b_ones, b_ssq = Buf("gv"), Buf("sc"), Buf("G"), Buf("ones"), Buf("ssq")
    b_xt = [Buf("xt0"), Buf("xt1")]
    b_ut = [Buf("ut0"), Buf("ut1")]
    b_sq = [Buf("sq0"), Buf("sq1")]
    pss = _ps(nc, "pss", [128, 512])
    b_pss = Buf("pss")
    if has_z:
        wbf = _sb(nc, "wbf", [128, 32, 1024], BF16)
        gt = _sb(nc, "gt", [128, 8, 2], F32)
        zt = [_sb(nc, "zt%d" % i, [128, 32, PT], BF16) for i in range(2)]
        xn = [_sb(nc, "xn%d" % i, [128, 8, PT], F32) for i in range(2)]
        b_wbf = [Buf("wbf%d" % k) for k in range(32)]
        b_gt = Buf("gt")
        b_zt = [Buf("zt0"), Buf("zt1")]
        b_xn = [Buf("xn0"), Buf("xn1")]
        psm = [_ps(nc, "psm%d" % i, [128, 512]) for i in range(2)]
        b_psm = [Buf("psm0"), Buf("psm1")]
    S.dma("sp", gv[:], gvec[:, :], writes=[b_gv])
    S.dma("sp", sc[:], scT[:, :, :], writes=[b_sc])
    S.dma("sp", ones[:], ones_in[:, :], writes=[b_ones])
    if has_z:
        S.dma("sp", gt[:], gateT[:, :, :], writes=[b_gt])
        for kc in range(32):
            S.dma("pool", wbf[:, kc, :], wout[kc, :, :], writes=[b_wbf[kc]])
    S.op("dve", lambda e: e.tensor_scalar(out=G[:], in0=sc[:], scalar1=1.0, scalar2=None, op0=ALU.add),
         reads=[b_sc], writes=[b_G])
    for s in range(2):
        S.op("dve", lambda e, s=s: e.tensor_tensor(out=G[:, :, s], in0=G[:, :, s], in1=gv[:], op=ALU.mult),
             reads=[b_G, b_gv], writes=[b_G])
    for tt in range(npt):
        c0 = tt * PT
        seg = 0 if tt == 0 else 1
        i = tt % 2
        S.dma("sp", xt[i][:], xT[:, :, c0:c0 + PT].rearrange("f p t -> p f t"), writes=[b_xt[i]])
        if has_z:
            S.dma("sp", zt[i][:], zT[:, :, c0:c0 + PT].rearrange("k p t -> p k t"), writes=[b_zt[i]])
        for fc in range(8):
            if has_z:
                pm = psm[fc % 2]
                for kc in range(32):
                    S.op("pe", lambda e, pm=pm, kc=kc, fc=fc, i=i: e.matmul(
                        pm[:, 0:PT], lhsT=wbf[:, kc, fc * 128:(fc + 1) * 128], rhs=zt[i][:, kc, :],
                        start=(kc == 0), stop=(kc == 31)),
                        reads=[b_wbf[kc], b_zt[i]], writes=[b_psm[fc % 2]])
                S.op("dve", lambda e, pm=pm, fc=fc, i=i, seg=seg: e.scalar_tensor_tensor(
                    out=xn[i][:, fc, :], in0=pm[:, 0:PT], scalar=gt[:, fc, seg:seg + 1], in1=xt[i][:, fc, :],
                    op0=ALU.mult, op1=ALU.add),
                    reads=[b_psm[fc % 2], b_gt, b_xt[i]], writes=[b_xn[i]])
                src, b_src = xn[i], b_xn[i]
            else:
                src, b_src = xt[i], b_xt[i]
            q = (tt * 8 + fc) % 2
            S.op("act", lambda e, src=src, fc=fc, q=q: e.activation(out=sq[q][:], in_=src[:, fc, :], func=AF.Square),
                 reads=[b_src], writes=[b_sq[q]])
            S.op("pe", lambda e, q=q, fc=fc: e.matmul(pss[0:1, 0:PT], lhsT=ones[:, 0:1], rhs=sq[q][:],
                                                      start=(fc == 0), stop=(fc == 7)),
                 reads=[b_ones, b_sq[q]], writes=[b_pss])
            S.op("dve", lambda e, src=src, fc=fc, i=i, seg=seg: e.tensor_scalar(
                out=ut[i][:, fc, :], in0=src[:, fc, :], scalar1=G[:, fc, seg:seg + 1], scalar2=None, op0=ALU.mult),
                reads=[b_src, b_G], writes=[b_ut[i]])
        S.op("dve", lambda e, c0=c0: e.tensor_copy(out=ssq_sb[:, c0:c0 + PT], in_=pss[0:1, 0:PT]),
             reads=[b_pss], writes=[b_ssq])
        if has_z:
            S.dma("sp", xoT[:, :, c0:c0 + PT].rearrange("f p t -> p f t"), xn[i][:], reads=[b_xn[i]])
        S.dma("sp", uT[:, :, c0:c0 + PT].rearrange("f p t -> p f t"), ut[i][:], reads=[b_ut[i]])
    S.dma("sp", ssq[:, :], ssq_sb[:], reads=[b_ssq])
    S.emit()
    return nc


def build_f(nl=SEQ):
    nc = bass.Bass("TRN2", target_bir_lowering=False)
    xT = _din(nc, "xT", [8, 128, nl], F32)
    ssq4 = _din(nc, "ssq4", [4, nl], F32)
    fg = _din(nc, "fg", [128, 8], F32)
    ones_in = _din(nc, "ones", [128, 128], F32)
    oT = _dout(nc, "oT", [8, 128, nl], F32)
    S = Sched(nc)
    TT = 512
    ones = _sb(nc, "ones_sb", [128, 128], F32)
    g = _sb(nc, "g", [128, 8], F32)
    s4 = _sb(nc, "s4", [4, nl], F32)
    rstd = _sb(nc, "rstd", [128, nl], F32)
    xt = [_sb(nc, "xt%d" % i, [128, 8, TT], F32) for i in range(2)]
    ot = [_sb(nc, "ot%d" % i, [128, 8, TT], F32) for i in range(2)]
    ps = [_ps(nc, "ps%d" % i, [128, 512]) for i in range(2)]
    b_ones, b_g, b_s4, b_rstd = Buf("ones"), Buf("g"), Buf("s4"), Buf("rstd")
    b_xt, b_ot, b_ps = [Buf("x0"), Buf("x1")], [Buf("o0"), Buf("o1")], [Buf("p0"), Buf("p1")]
    S.dma("sp", ones[:], ones_in[:, :], writes=[b_ones])
    S.dma("sp", g[:], fg[:, :], writes=[b_g])
    S.dma("sp", s4[:], ssq4[:, :], writes=[b_s4])
    epsb = _sb(nc, "epsb", [128, 1], F32)
    b_eps = Buf("eps")
    S.op("dve", lambda e: e.memset(epsb[:], EPS), writes=[b_eps])
    for tt in range(nl // TT):
        c0 = tt * TT
        i = tt % 2
        S.op("pe", lambda e, i=i, c0=c0: e.matmul(ps[i][:, :], lhsT=ones[0:4, :], rhs=s4[:, c0:c0 + TT],
                                                 start=True, stop=True), reads=[b_ones, b_s4], writes=[b_ps[i]])
        S.op("act", lambda e, i=i, c0=c0: e.activation(out=rstd[:, c0:c0 + TT], in_=ps[i][:, :], func=AF.Sqrt,
                                                     scale=1.0 / D, bias=epsb[:, 0:1]),
             reads=[b_ps[i], b_eps], writes=[b_rstd])
        S.op("dve", lambda e, c0=c0: e.reciprocal(out=rstd[:, c0:c0 + TT], in_=rstd[:, c0:c0 + TT]),
             reads=[b_rstd], writes=[b_rstd])
        S.dma("sp", xt[i][:], xT[:, :, c0:c0 + TT].rearrange("f p t -> p f t"), writes=[b_xt[i]])
        for fc in range(8):
            S.op("dve", lambda e, i=i, fc=fc, c0=c0: e.scalar_tensor_tensor(
                out=ot[i][:, fc, :], in0=xt[i][:, fc, :], scalar=g[:, fc:fc + 1], in1=rstd[:, c0:c0 + TT],
                op0=ALU.mult, op1=ALU.mult), reads=[b_xt[i], b_g, b_rstd], writes=[b_ot[i]])
        S.dma("sp", oT[:, :, c0:c0 + TT].rearrange("f p t -> p f t"), ot[i][:], reads=[b_ot[i]])
    S.emit()
    return nc


NA_QS = 128.0 ** -0.5
GLA_QS = 256.0 ** -0.5


def na_pairs(rows):
    kh = min(8, rows)
    out = []
    for m in range(rows // 2):
        prs = set()
        for r in (2 * m, 2 * m + 1):
            rs = min(max(r - kh // 2, 0), rows - kh)
            for y in range(rs, rs + kh):
                prs.add(y // 2)
        out.append(sorted(prs))
    return out


def a_cols(g):
    r = np.arange
    naq, nak, nav, nag = 0, 2048, 4096, 6144
    gq, gk, gv, gg, gf, gb = 8192, 9216, 10240, 12288, 14336, 14352
    p0 = np.concatenate([naq + 512 * g + r(512), nak + 512 * g + r(512)])
    p1 = np.concatenate([nag + 512 * g + r(512), gq + 256 * g + r(256), gk + 256 * g + r(256)])
    p2 = np.concatenate([gg + 512 * g + r(512), gf + r(16), gb + r(16)])
    p3 = np.concatenate([nav + 512 * g + r(512), gv + 512 * g + r(512)])
    return [p0, p1, p2, p3]


def build_a(rows=64, nctx=CTX, dbg=False, stop=9, env=None):
    if env is not None:
        return _build_a(rows, nctx, stop, env)
    global _dint
    _dint_saved = _dint
    if dbg:
        _dint = _dout
    try:
        return _build_a(rows, nctx, stop)
    finally:
        _dint = _dint_saved


def _build_a(rows, nctx, stop=9, env=None):
    nl = rows * GRID_W
    nt = nctx + nl
    nch = nt // 64
    nck = nctx // 128
    ntile128 = nt // 128
    pairs = na_pairs(rows)
    npair = rows // 2
    nbias = 25
    if env is None:
        nc = bass.Bass("TRN2", target_bir_lowering=False)
        uT = _din(nc, "uT", [32, 128, nt], BF16)
        ssq4 = _din(nc, "ssq4", [4, nt], F32)
        shT = _din(nc, "shT", [128, 32, 2], F32)
        wp = [_din(nc, "wp0", [32, 128, 1024], F32), _din(nc, "wp1", [32, 128, 1024], F32),
              _din(nc, "wp2", [32, 128, 544], F32), _din(nc, "wp3", [32, 128, 1024], F32)]
        biasT = _din(nc, "biasT", [128, 4 * nbias, 128], F32)
        wd = _din(nc, "wd", [17, 2, 256], F32)
        gng = _din(nc, "gng", [128, 4], F32)
        ropeR = _din(nc, "ropeR", [128, 2, rows], F32)
        ropeC = _din(nc, "ropeC", [128, 2, 64], F32)
        cf = _din(nc, "cf", [128, 5, 128], F32)
        cb = _din(nc, "cb", [128, 4, 128], BF16)
        zT = _dout(nc, "zT", [8, 128, nt], BF16)
        naS = _dint(nc, "naS", [12, 128, nt], BF16)
        glS = _dint(nc, "glS", [8, 128, nt], F32)
        lrS = _dint(nc, "lrS", [32, nt], F32)
        vS = _dint(nc, "vS", [nt, 1024], BF16)
        gpS = _dint(nc, "gpS", [2, 3, 128, 2, nt], BF16)
        oS = _dint(nc, "oS", [2, 128, 4, nt], F32)
        S = Sched(nc)
        B = S.buf
        banks = [_ps(nc, "bank%d" % i, [128, 512]) for i in range(8)]
        b_bank = [B("bank%d" % i) for i in range(8)]
    else:
        nc, S = env["nc"], env["S"]
        B = S.buf
        banks, b_bank = env["banks"], env["b_bank"]
        uT, ssq4, shT = env["uT"], env["ssq4"], None
        wp, biasT, wd, gng = env["wp"], env["biasT"], env["wd"], env["gng"]
        ropeR, ropeC, cf, cb = env["ropeR"], env["ropeC"], env["cf"], env["cb"]
        zT = env["zT"]
        naS, glS, lrS, vS, gpS, oS = (env[k_] for k_ in ("naS", "glS", "lrS", "vS", "gpS", "oS"))

    ttiles = [(0, nctx, 0)] + [(nctx + 512 * i, 512, 1) for i in range(nl // 512)]

    if env is not None and "consts" in env:
        cft, cbt, epsb, onec, b_cf, b_cb, b_eps = env["consts"]
    else:
        cft = _sb(nc, "cft", [128, 5, 128], F32)
        cbt = _sb(nc, "cbt", [128, 4, 128], BF16)
        epsb = _sb(nc, "epsb", [128, 1], F32)
        onec = _sb(nc, "onec", [128, 1], F32)
        b_cf, b_cb, b_eps = B("cf"), B("cb"), B("eps")
        S.dma("sp", cft[:], cf[:, :, :], writes=[b_cf])
        S.dma("sp", cbt[:], cb[:, :, :], writes=[b_cb])
        S.op("dve", lambda e: e.memset(epsb[:], EPS), writes=[b_eps])
        S.op("dve", lambda e: e.memset(onec[:], 1.0), writes=[b_eps])
        if env is not None:
            env["consts"] = (cft, cbt, epsb, onec, b_cf, b_cb, b_eps)
    b_rstd, b_rcol = B("rstd"), B("rcol")
    ones_f = cft[:, 0, :]
    perm_f = cft[:, 1, :]
    tri_f = [cft[:, 2, :], cft[:, 3, :]]
    ident_f = cft[:, 4, :]
    ident_b = cbt[:, 0, :]
    ones_b = cbt[:, 1, :]
    mask_b = [cbt[0:64, 2, 0:64], cbt[0:64, 3, 0:64]]

    ph = []

    def alloc(name, shape, dt):
        t = nc.sbuf_tensor(_u(name), list(shape), dt)
        ph.append(t)
        return t.__enter__()

    def free_all():
        S.flush(barrier=True)
        while ph:
            ph.pop().__exit__(None, None, None)

    rstd = alloc("rstd", [128, nt], F32)
    rcol = alloc("rcol", [128, ntile128], F32)
    s4 = alloc("s4", [4, nt], F32)
    sh = alloc("sh", [128, 32, 2], F32)
    shb = alloc("shb", [128, 32, 2], BF16)
    shrep = alloc("shrep", [128, 32, 2, 128], BF16)
    wbf = alloc("wbf", [128, 32, 1024], BF16)
    ut = [alloc("ut%d" % i, [128, 32, 512], BF16) for i in range(2)]
    tmp = [alloc("tmp%d" % i, [128, 512], F32) for i in range(2)]
    ob = [alloc("ob%d" % i, [128, 512], BF16) for i in range(3)]
    of = [alloc("of%d" % i, [128, 512], F32) for i in range(3)]
    sbias = alloc("sbias", [128, 9, 2], F32)
    sbias2 = alloc("sbias2", [128, 9, 2], F32)
    srow = alloc("srow", [128, 2, 2, 512], F32)
    b_s4, b_sh, b_shb, b_shrep = B("s4"), B("sh"), B("shb"), B("shrep")
    b_wbf = [B("wbf%d" % k) for k in range(32)]
    b_ut = [B("ut0"), B("ut1")]
    b_tmp = [B("tmp0"), B("tmp1")]
    b_ob = [B("ob%d" % i) for i in range(3)]
    b_of = [B("of%d" % i) for i in range(3)]
    b_sbias, b_srow = B("sbias"), B("srow")

    S.dma("sp", s4[:], ssq4[:, :], writes=[b_s4])
    if env is None:
        S.dma("sp", sh[:], shT[:, :, :], writes=[b_sh])
        S.op("dve", lambda e: e.tensor_copy(out=shb[:], in_=sh[:]), reads=[b_sh], writes=[b_shb])
    else:
        S.dma("sp", shb[:], uT[:, :, nt:nt + 2].rearrange("k p t -> p k t"), writes=[b_shb])
    S.op("dve", lambda e: e.tensor_copy(out=shrep[:], in_=shb[:].unsqueeze(3).broadcast_to([128, 32, 2, 128])),
         reads=[b_shb], writes=[b_shrep])
    for ti, (c0, n, seg) in enumerate(ttiles):
        bk = ti % 2
        S.op("pe", lambda e, bk=bk, c0=c0, n=n: e.matmul(banks[bk][:, 0:n], lhsT=ones_f[0:4, :], rhs=s4[:, c0:c0 + n],
                                                        start=True, stop=True),
             reads=[b_cf, b_s4], writes=[b_bank[bk]])
        S.op("act", lambda e, bk=bk, c0=c0, n=n: e.activation(out=rstd[:, c0:c0 + n], in_=banks[bk][:, 0:n], func=AF.Sqrt,
                                                            scale=1.0 / D, bias=epsb[:, 0:1]),
             reads=[b_bank[bk], b_eps], writes=[b_rstd])
        S.op("dve", lambda e, c0=c0, n=n: e.reciprocal(out=rstd[:, c0:c0 + n], in_=rstd[:, c0:c0 + n]),
             reads=[b_rstd], writes=[b_rstd])
    for t in range(ntile128):
        S.op("pe", lambda e, t=t: e.matmul(banks[2][:, t:t + 1], lhsT=s4[:, t * 128:(t + 1) * 128], rhs=ones_f[0:4, 0:1],
                                         start=True, stop=True), reads=[b_cf, b_s4], writes=[b_bank[2]])
    S.op("act", lambda e: e.activation(out=rcol[:], in_=banks[2][:, 0:ntile128], func=AF.Sqrt, scale=1.0 / D,
                                       bias=epsb[:, 0:1]), reads=[b_bank[2], b_eps], writes=[b_rcol])
    S.op("dve", lambda e: e.reciprocal(out=rcol[:], in_=rcol[:]), reads=[b_rcol], writes=[b_rcol])

    fm_pass = [
        [("id", NA_QS, naS, h, BF16) for h in range(4)] + [("id", 1.0, naS, 4 + h, BF16) for h in range(4)],
        [("silu", 1.0, naS, 8 + h, BF16) for h in range(4)] + [("id", GLA_QS, glS, 0, F32), ("id", GLA_QS, glS, 1, F32),
                                                               ("id", 1.0, glS, 2, F32), ("id", 1.0, glS, 3, F32)],
        [("silu", 1.0, glS, 4 + e_, F32) for e_ in range(4)] + [("lr", 1.0, lrS, 0, F32)],
    ]
    evn = 0
    for p in range(4):
        ncols = 544 if p == 2 else 1024
        for kc in range(32):
            S.dma("pool", wbf[:, kc, 0:ncols], wp[p][kc, :, :], reads=[], writes=[b_wbf[kc]])
        if p < 3:
            tiles = fm_pass[p]
            for ci, (kind, scl, dst, di, odt) in enumerate(tiles):
                m_ = 32 if kind == "lr" else 128
                for kc in range(32):
                    S.op("pe", lambda e, ci=ci, kc=kc, m_=m_: e.matmul(
                        banks[7][0:m_, ci * 2:ci * 2 + 2], lhsT=wbf[:, kc, ci * 128:ci * 128 + m_], rhs=shb[:, kc, :],
                        start=(kc == 0), stop=(kc == 31)), reads=[b_wbf[kc], b_shb], writes=[b_bank[7]])
            nci = len(tiles)
            S.op("dve", lambda e, nci=nci: e.tensor_copy(
                out=sbias[:, 0:nci, :], in_=banks[7][:, 0:2 * nci].rearrange("p (c s) -> p c s", s=2)),
                reads=[b_bank[7]], writes=[b_sbias])
            for ci, (kind, scl, dst, di, odt) in enumerate(tiles):
                S.op("dve", lambda e, ci=ci, scl=scl: e.tensor_scalar(
                    out=sbias2[:, ci, :], in0=sbias[:, ci, :], scalar1=float(scl), scalar2=None, op0=ALU.mult),
                    reads=[b_sbias], writes=[b_sbias])
            for ti, (c0, n, seg) in enumerate(ttiles):
                ui = ti % 2
                S.dma("sp", ut[ui][:, :, 0:n], uT[:, :, c0:c0 + n].rearrange("k p t -> p k t"), writes=[b_ut[ui]])
                for ci, (kind, scl, dst, di, odt) in enumerate(tiles):
                    m_ = 32 if kind == "lr" else 128
                    bk = evn % 4
                    for kc in range(32):
                        S.op("pe", lambda e, bk=bk, ci=ci, kc=kc, m_=m_, ui=ui, n=n: e.matmul(
                            banks[bk][0:m_, 0:n], lhsT=wbf[:, kc, ci * 128:ci * 128 + m_], rhs=ut[ui][:, kc, 0:n],
                            start=(kc == 0), stop=(kc == 31)), reads=[b_wbf[kc], b_ut[ui]], writes=[b_bank[bk]])
                    ti2 = evn % 2
                    S.op("dve", lambda e, bk=bk, ti2=ti2, m_=m_, n=n, c0=c0: e.tensor_tensor(
                        out=tmp[ti2][0:m_, 0:n], in0=banks[bk][0:m_, 0:n], in1=rstd[0:m_, c0:c0 + n], op=ALU.mult),
                        reads=[b_bank[bk], b_rstd], writes=[b_tmp[ti2]])
                    oi = evn % 3
                    otile, b_ot = (ob[oi], b_ob[oi]) if odt == BF16 else (of[oi], b_of[oi])
                    func = AF.Silu if kind == "silu" else AF.Identity
                    S.op("act", lambda e, otile=otile, ti2=ti2, m_=m_, n=n, func=func, scl=scl, ci=ci, seg=seg: e.activation(
                        out=otile[0:m_, 0:n], in_=tmp[ti2][0:m_, 0:n], func=func, scale=float(scl),
                        bias=sbias2[0:m_, ci, seg:seg + 1]), reads=[b_tmp[ti2], b_sbias], writes=[b_ot])
                    if kind == "lr":
                        S.dma("sp", dst[:, c0:c0 + n], otile[0:32, 0:n], reads=[b_ot])
                    else:
                        S.dma("sp", dst[di, :, c0:c0 + n], otile[:, 0:n], reads=[b_ot])
                    evn += 1
        else:
            for seg in range(2):
                for hf in range(2):
                    bk = 4 + (seg * 2 + hf) % 2
                    for kc in range(32):
                        S.op("pe", lambda e, bk=bk, kc=kc, seg=seg, hf=hf: e.matmul(
                            banks[bk][:, :], lhsT=shrep[:, kc, seg, :], rhs=wbf[:, kc, hf * 512:(hf + 1) * 512],
                            start=(kc == 0), stop=(kc == 31)), reads=[b_wbf[kc], b_shrep], writes=[b_bank[bk]])
                    S.op("dve", lambda e, bk=bk, seg=seg, hf=hf: e.tensor_copy(out=srow[:, seg, hf, :], in_=banks[bk][:, :]),
                         reads=[b_bank[bk]], writes=[b_srow])
            for ti, (c0, n, seg) in enumerate(ttiles):
                ui = ti % 2
                S.dma("sp", ut[ui][:, :, 0:n], uT[:, :, c0:c0 + n].rearrange("k p t -> p k t"), writes=[b_ut[ui]])
                for st in range(n // 128):
                    tg = (c0 + st * 128) // 128
                    for hf in range(2):
                        bk = evn % 4
                        for kc in range(32):
                            S.op("pe", lambda e, bk=bk, kc=kc, ui=ui, st=st, hf=hf: e.matmul(
                                banks[bk][:, :], lhsT=ut[ui][:, kc, st * 128:(st + 1) * 128],
                                rhs=wbf[:, kc, hf * 512:(hf + 1) * 512], start=(kc == 0), stop=(kc == 31)),
                                reads=[b_wbf[kc], b_ut[ui]], writes=[b_bank[bk]])
                        oi = evn % 3
                        S.op("dve", lambda e, bk=bk, oi=oi, tg=tg, seg=seg, hf=hf: e.scalar_tensor_tensor(
                            out=ob[oi][:, :], in0=banks[bk][:, :], scalar=rcol[:, tg:tg + 1], in1=srow[:, seg, hf, :],
                            op0=ALU.mult, op1=ALU.add), reads=[b_bank[bk], b_rcol, b_srow], writes=[b_ob[oi]])
                        S.dma("sp", vS[tg * 128:(tg + 1) * 128, hf * 512:(hf + 1) * 512], ob[oi][:, :], reads=[b_ob[oi]])
                        evn += 1
    free_all()
    if stop <= 1:
        S.emit()
        return nc, S, locals()

    NB = 25

    def cls_of(m):
        if m < 2:
            return m
        if m >= npair - 2:
            return 3 + (m - (npair - 2))
        return 2

    biasb = alloc("biasb", [128, 4 * NB, 128], BF16)
    b_biasb = B("biasb")
    for i in range(0, 4 * NB, 10):
        S.dma("pool", biasb[:, i:i + 10, :], biasT[:, i:i + 10, :], writes=[b_biasb])
    hk = [alloc("hk%d" % i, [128, nt], BF16) for i in range(2)]
    hq = [alloc("hq%d" % i, [128, nt], BF16) for i in range(2)]
    hg = [alloc("hg%d" % i, [128, nt], BF16) for i in range(2)]
    hv = [alloc("hv%d" % i, [128, ntile128, 128], BF16) for i in range(2)]
    pT = [alloc("pT%d" % i, [128, 1024], BF16) for i in range(2)]
    rec = [alloc("rec%d" % i, [128, 256], F32) for i in range(2)]
    t1 = [alloc("t1_%d" % i, [128, 256], F32) for i in range(2)]
    zst = [alloc("zst%d" % i, [128, 256], BF16) for i in range(2)]
    b_hk, b_hq, b_hg, b_hv = ([B("h%s%d" % (c_, i)) for i in range(2)] for c_ in "kqgv")
    b_pT, b_rec, b_t1, b_zst = ([B("%s%d" % (c_, i)) for i in range(2)] for c_ in ("pT", "rec", "t1", "zst"))
    un = 0
    for h in range(4):
        hi = h % 2
        S.dma("sp", hq[hi][:], naS[h, :, :], writes=[b_hq[hi]])
        S.dma("sp", hk[hi][:], naS[4 + h, :, :], writes=[b_hk[hi]])
        S.dma("sp", hg[hi][:], naS[8 + h, :, :], writes=[b_hg[hi]])
        S.dma("sp", hv[hi][:], vS[:, h * 128:(h + 1) * 128].rearrange("(c p) d -> p c d", p=128), writes=[b_hv[hi]])
        units = [(0, nctx, [(128 * c, None) for c in range(nck)])]
        for m in range(npair):
            ch = [(nctx + 128 * p_, h * NB + cls_of(m) * 5 + j) for j, p_ in enumerate(pairs[m])]
            ch += [(128 * c, None) for c in range(nck)]
            units.append((nctx + 128 * m, 128, ch))
        for (q0, N, ch) in units:
            u_ = un % 2
            per = 512 // N
            sb_ = [2 * u_, 2 * u_ + 1]
            bo, bs = 4 + 2 * u_, 5 + 2 * u_
            for i, (k0, bi) in enumerate(ch):
                bk = sb_[i // per]
                o0 = (i % per) * N
                S.op("pe", lambda e, bk=bk, o0=o0, N=N, k0=k0, q0=q0, hi=hi, bi=bi: e.matmul(
                    banks[bk][:, o0:o0 + N], lhsT=hk[hi][:, k0:k0 + 128], rhs=hq[hi][:, q0:q0 + N],
                    start=True, stop=(bi is None)), reads=[b_hk[hi], b_hq[hi]], writes=[b_bank[bk]])
                if bi is not None:
                    S.op("pe", lambda e, bk=bk, o0=o0, N=N, bi=bi: e.matmul(
                        banks[bk][:, o0:o0 + N], lhsT=ident_b, rhs=biasb[:, bi, :], start=False, stop=True),
                        reads=[b_cb, b_biasb], writes=[b_bank[bk]])
            ng = (len(ch) + per - 1) // per
            for gi in range(ng):
                cnt = min(per, len(ch) - gi * per)
                S.op("act", lambda e, gi=gi, cnt=cnt, N=N, u_=u_, bk=sb_[gi], per=per: e.activation(
                    out=pT[u_][:, gi * per * N:gi * per * N + cnt * N], in_=banks[bk][:, 0:cnt * N], func=AF.Exp),
                    reads=[b_bank[sb_[gi]]], writes=[b_pT[u_]])
            for i, (k0, bi) in enumerate(ch):
                S.op("pe", lambda e, i=i, k0=k0, N=N, bo=bo, hi=hi, u_=u_, last=(i == len(ch) - 1): e.matmul(
                    banks[bo][:, 0:N], lhsT=hv[hi][:, k0 // 128, :], rhs=pT[u_][:, i * N:(i + 1) * N],
                    start=(i == 0), stop=last), reads=[b_hv[hi], b_pT[u_]], writes=[b_bank[bo]])
            for i, (k0, bi) in enumerate(ch):
                S.op("pe", lambda e, i=i, N=N, bs=bs, u_=u_, last=(i == len(ch) - 1): e.matmul(
                    banks[bs][:, 0:N], lhsT=ones_b, rhs=pT[u_][:, i * N:(i + 1) * N],
                    start=(i == 0), stop=last), reads=[b_cb, b_pT[u_]], writes=[b_bank[bs]])
            S.op("dve", lambda e, N=N, bs=bs, u_=u_: e.reciprocal(out=rec[u_][:, 0:N], in_=banks[bs][:, 0:N]),
                 reads=[b_bank[bs]], writes=[b_rec[u_]])
            S.op("dve", lambda e, N=N, bo=bo, u_=u_: e.tensor_tensor(out=t1[u_][:, 0:N], in0=banks[bo][:, 0:N],
                                                                    in1=rec[u_][:, 0:N], op=ALU.mult),
                 reads=[b_bank[bo], b_rec[u_]], writes=[b_t1[u_]])
            S.op("dve", lambda e, N=N, u_=u_, hi=hi, q0=q0: e.tensor_tensor(out=zst[u_][:, 0:N], in0=t1[u_][:, 0:N],
                                                                        in1=hg[hi][:, q0:q0 + N], op=ALU.mult),
                 reads=[b_t1[u_], b_hg[hi]], writes=[b_zst[u_]])
            S.dma("sp", zT[h, :, q0:q0 + N], zst[u_][:, 0:N], reads=[b_zst[u_]])
            un += 1
    free_all()
    if stop <= 2:
        S.emit()
        return nc, S, locals()

    wdt = alloc("wdt", [17, 2, 256], F32)
    gngt = alloc("gngt", [128, 4], F32)
    rR = alloc("rR", [128, 2, rows], F32)
    rC = alloc("rC", [128, 2, 64], F32)
    dec = alloc("dec", [128, 2, 2, nch], F32)
    b_wdt, b_gng, b_rope, b_dec = B("wdt"), B("gng"), B("rope"), B("dec")
    S.dma("sp", wdt[:], wd[:, :, :], writes=[b_wdt])
    S.dma("sp", gngt[:], gng[:, :], writes=[b_gng])
    S.dma("sp", rR[:], ropeR[:, :, :], writes=[b_rope])
    S.dma("sp", rC[:], ropeC[:, :, :], writes=[b_rope])
    sub = []

    def salloc(name, shape, dt):
        t = nc.sbuf_tensor(_u(name), list(shape), dt)
        sub.append(t)
        return t.__enter__()

    def sfree():
        S.flush(barrier=True)
        while sub:
            sub.pop().__exit__(None, None, None)

    qk = [salloc("qk%d" % i, [128, 4, 512], F32) for i in range(2)]
    qr = [salloc("qr%d" % i, [128, 4, 512], F32) for i in range(2)]
    rt = salloc("rt", [128, 512], F32)
    lrt = [salloc("lrt%d" % i, [17, 512], F32) for i in range(2)]
    e1 = [salloc("e1_%d" % i, [128, 256], F32) for i in range(2)]
    Lt = [salloc("Lt%d" % i, [128, 256], F32) for i in range(2)]
    bl = [salloc("bl%d" % i, [128, 2, 8], F32) for i in range(2)]
    E = [salloc("E%d" % i, [128, 3, 2, 512], F32) for i in range(2)]
    gp = [salloc("gp%d" % i, [128, 3, 2, 512], BF16) for i in range(2)]
    b_qk, b_qr, b_lrt, b_e1, b_Lt, b_bl, b_E, b_gp = ([B("%s%d" % (c_, i)) for i in range(2)]
                                                      for c_ in ("qk", "qr", "lrt", "e1", "Lt", "bl", "E", "gp"))
    b_rt = B("rt")
    for dr in range(2):
        S.op("dve", lambda e, dr=dr: e.memset(lrt[dr][:], 1.0), writes=[b_lrt[dr]])
    it = 0
    for ti, (c0, n, seg) in enumerate(ttiles):
        i2 = ti % 2
        S.dma("sp", qk[i2][:, :, 0:n], glS[0:4, :, c0:c0 + n].rearrange("j p t -> p j t"), writes=[b_qk[i2]])
        if seg == 1:
            r0 = (c0 - nctx) // 64
            nr = n // 64
            for j in range(4):
                hf = j % 2
                if hf == 0:
                    cosap = rR[:, 0, r0:r0 + nr].unsqueeze(2).broadcast_to([128, nr, 64])
                    sinap = rR[:, 1, r0:r0 + nr].unsqueeze(2).broadcast_to([128, nr, 64])
                else:
                    cosap = rC[:, 0, :].unsqueeze(1).broadcast_to([128, nr, 64])
                    sinap = rC[:, 1, :].unsqueeze(1).broadcast_to([128, nr, 64])
                bk = j % 2
                S.op("pe", lambda e, bk=bk, i2=i2, j=j, n=n: e.matmul(banks[bk][:, 0:n], lhsT=perm_f, rhs=qk[i2][:, j, 0:n],
                                                                  start=True, stop=True),
                     reads=[b_cf, b_qk[i2]], writes=[b_bank[bk]])
                S.op("dve", lambda e, i2=i2, j=j, n=n, cosap=cosap: e.tensor_tensor(
                    out=qr[i2][:, j, 0:n].rearrange("p (r c) -> p r c", c=64),
                    in0=qk[i2][:, j, 0:n].rearrange("p (r c) -> p r c", c=64), in1=cosap, op=ALU.mult),
                    reads=[b_qk[i2], b_rope], writes=[b_qr[i2]])
                S.op("dve", lambda e, bk=bk, n=n, sinap=sinap: e.tensor_tensor(
                    out=rt[:, 0:n].rearrange("p (r c) -> p r c", c=64),
                    in0=banks[bk][:, 0:n].rearrange("p (r c) -> p r c", c=64), in1=sinap, op=ALU.mult),
                    reads=[b_bank[bk], b_rope], writes=[b_rt])
                S.op("dve", lambda e, i2=i2, j=j, n=n: e.tensor_tensor(out=qr[i2][:, j, 0:n], in0=qr[i2][:, j, 0:n],
                                                                   in1=rt[:, 0:n], op=ALU.add),
                     reads=[b_qr[i2], b_rt], writes=[b_qr[i2]])
            src, b_src = qr[i2], b_qr[i2]
        else:
            src, b_src = qk[i2], b_qk[i2]
        for dr in range(2):
            S.dma("sp", lrt[dr][0:16, 0:n], lrS[16 * dr:16 * dr + 16, c0:c0 + n], writes=[b_lrt[dr]])
            for st in range(n // 128):
                bg = 2 + it % 2
                ei = it % 2
                S.op("pe", lambda e, bg=bg, dr=dr, st=st: e.matmul(
                    banks[bg][:, 0:256], lhsT=lrt[dr][0:17, st * 128:(st + 1) * 128], rhs=wdt[0:17, dr, :],
                    start=True, stop=True), reads=[b_lrt[dr], b_wdt], writes=[b_bank[bg]])
                S.op("act", lambda e, bg=bg, ei=ei: e.activation(out=e1[ei][:], in_=banks[bg][:, 0:256], func=AF.Exp, scale=-1.0),
                     reads=[b_bank[bg]], writes=[b_e1[ei]])
                S.op("act", lambda e, ei=ei: e.activation(out=Lt[ei][:], in_=e1[ei][:], func=AF.Ln, bias=onec[:, 0:1]),
                     reads=[b_e1[ei], b_eps], writes=[b_Lt[ei]])
                for dh in range(2):
                    S.op("pe", lambda e, dh=dh, ei=ei, st=st, dr=dr: e.matmul(
                        banks[4 + dh][:, st * 128:(st + 1) * 128], lhsT=Lt[ei][:, dh * 128:(dh + 1) * 128], rhs=tri_f[dr],
                        start=True, stop=True), reads=[b_Lt[ei], b_cf], writes=[b_bank[4 + dh]])
                it += 1
            di = (ti * 2 + dr) % 2
            ncb = n // 64
            cb0 = c0 // 64
            off = 63 if dr == 0 else 0
            for dh in range(2):
                S.op("dve", lambda e, di=di, dh=dh, ncb=ncb, off=off, n=n: e.tensor_copy(
                    out=bl[di][:, dh, 0:ncb], in_=banks[4 + dh][:, 0:n].rearrange("p (c t) -> p c t", t=64)[:, :, off]),
                    reads=[b_bank[4 + dh]], writes=[b_bl[di]])
            S.op("act", lambda e, di=di, dr=dr, ncb=ncb, cb0=cb0: e.activation(
                out=dec[:, dr, :, cb0:cb0 + ncb], in_=bl[di][:, :, 0:ncb], func=AF.Exp),
                reads=[b_bl[di]], writes=[b_dec])
            for dh in range(2):
                S.op("act", lambda e, di=di, dh=dh, n=n: e.activation(out=E[di][:, 0, dh, 0:n], in_=banks[4 + dh][:, 0:n],
                                                                   func=AF.Exp), reads=[b_bank[4 + dh]], writes=[b_E[di]])
                S.op("act", lambda e, di=di, dh=dh, n=n: e.activation(out=E[di][:, 1, dh, 0:n], in_=banks[4 + dh][:, 0:n],
                                                                   func=AF.Exp, scale=-1.0),
                     reads=[b_bank[4 + dh]], writes=[b_E[di]])
                for c in range(ncb):
                    S.op("act", lambda e, di=di, dh=dh, c=c: e.activation(
                        out=E[di][:, 2, dh, c * 64:(c + 1) * 64], in_=banks[4 + dh][:, c * 64:(c + 1) * 64], func=AF.Exp,
                        scale=-1.0, bias=bl[di][:, dh, c:c + 1]), reads=[b_bank[4 + dh], b_bl[di]], writes=[b_E[di]])
            for dh in range(2):
                for j, sj in ((0, dh), (1, 2 + dh), (2, 2 + dh)):
                    S.op("dve", lambda e, di=di, dh=dh, j=j, sj=sj, n=n, src=src: e.tensor_tensor(
                        out=gp[di][:, j, dh, 0:n], in0=src[:, sj, 0:n], in1=E[di][:, j, dh, 0:n], op=ALU.mult),
                        reads=[b_src, b_E[di]], writes=[b_gp[di]])
            for j in range(3):
                S.dma("sp", gpS[dr, j, :, :, c0:c0 + n], gp[di][:, j, :, 0:n], reads=[b_gp[di]])
    sfree()
    if stop <= 3:
        free_all()
        S.emit()
        return nc, S, locals()

    Sf = [salloc("Sf%d" % i, [128, 2, 512], F32) for i in range(2)]
    Sb = [[salloc("Sb%d_%d" % (d_, i), [128, 2, 512], BF16) for i in range(2)] for d_ in range(2)]
    gpb = [[salloc("gpb%d_%d" % (d_, i), [128, 3, 2, 512], BF16) for i in range(2)] for d_ in range(2)]
    vb = [[salloc("vb%d_%d" % (d_, i), [64, 8, 512], BF16) for i in range(2)] for d_ in range(2)]
    am = [[salloc("am%d_%d" % (d_, i), [64, 64], BF16) for i in range(2)] for d_ in range(2)]
    kdt = [[salloc("kdt%d_%d" % (d_, i), [64, 256], BF16) for i in range(2)] for d_ in range(2)]
    och = [[salloc("och%d_%d" % (d_, i), [128, 4, 64], F32) for i in range(2)] for d_ in range(2)]
    b_Sf = [B("Sf0"), B("Sf1")]
    b_Sb, b_gpb, b_vb, b_am, b_kdt, b_och = ([[B("%s%d_%d" % (c_, d_, i)) for i in range(2)] for d_ in range(2)]
                                             for c_ in ("Sb", "gpb", "vb", "am", "kdt", "och"))
    b_A = [B("psA0"), B("psA1")]
    b_T = [B("psT0"), B("psT1")]
    nck64 = nctx // 64
    orders = [list(range(nch)), list(range(nck64 - 1, -1, -1)) + list(range(nch - 1, nck64 - 1, -1))]
    cur_blk, slot = [-1, -1], [1, 1]
    for dr in range(2):
        S.op("dve", lambda e, dr=dr: e.memset(Sf[dr][:], 0.0), reads=[], writes=[b_Sf[dr]])
        S.op("dve", lambda e, dr=dr: e.memset(Sb[dr][0][:], 0.0), reads=[], writes=[b_Sb[dr][0]])
    for step in range(nch):
        for dr in range(2):
            c = orders[dr][step]
            blk = c // 8
            if blk != cur_blk[dr]:
                cur_blk[dr] = blk
                slot[dr] ^= 1
                sl = slot[dr]
                t0b = blk * 512
                nb = min(512, nt - t0b)
                for j in range(3):
                    S.dma("sp", gpb[dr][sl][:, j, :, 0:nb], gpS[dr, j, :, :, t0b:t0b + nb], writes=[b_gpb[dr][sl]])
                S.dma("sp", vb[dr][sl][:, 0:nb // 64, :],
                      vS[t0b:t0b + nb, 512:1024].rearrange("(c p) e -> p c e", p=64), writes=[b_vb[dr][sl]])
            sl = slot[dr]
            g_, v_, bg_, bv_ = gpb[dr][sl], vb[dr][sl], b_gpb[dr][sl], b_vb[dr][sl]
            ci = c % 8
            cs = ci * 64
            pi = step % 2
            bAT = banks[4 * dr]
            bO = banks[4 * dr + 1]
            bOb = b_bank[4 * dr + 1]
            am_, kdt_, och_ = am[dr][pi], kdt[dr][pi], och[dr][pi]
            bam_, bkdt_, boch_ = b_am[dr][pi], b_kdt[dr][pi], b_och[dr][pi]
            Sb_r, bSb_r = Sb[dr][pi], b_Sb[dr][pi]
            Sb_w, bSb_w = Sb[dr][1 - pi], b_Sb[dr][1 - pi]
            for dh in range(2):
                S.op("pe", lambda e, dh=dh, g_=g_, cs=cs, bAT=bAT: e.matmul(
                    bAT[0:64, 0:64], lhsT=g_[:, 1, dh, cs:cs + 64], rhs=g_[:, 0, dh, cs:cs + 64],
                    start=(dh == 0), stop=(dh == 1)), reads=[bg_], writes=[b_A[dr]])
            S.op("dve", lambda e, am_=am_, bAT=bAT, dr=dr: e.tensor_tensor(out=am_[:], in0=bAT[0:64, 0:64], in1=mask_b[dr],
                                                                      op=ALU.mult), reads=[b_A[dr], b_cb], writes=[bam_])
            for dh in range(2):
                S.op("pe", lambda e, dh=dh, g_=g_, cs=cs, bAT=bAT: e.matmul(
                    bAT[0:64, 128 + dh * 128:128 + (dh + 1) * 128], lhsT=g_[:, 2, dh, cs:cs + 64], rhs=ident_b,
                    start=True, stop=True), reads=[bg_, b_cb], writes=[b_T[dr]])
            S.op("act", lambda e, kdt_=kdt_, bAT=bAT: e.activation(out=kdt_[:], in_=bAT[0:64, 128:384], func=AF.Identity),
                 reads=[b_T[dr]], writes=[bkdt_])
            for ec in range(4):
                S.op("pe", lambda e, ec=ec, bO=bO, v_=v_, ci=ci, am_=am_: e.matmul(
                    bO[:, ec * 64:(ec + 1) * 64], lhsT=v_[0:64, ci, ec * 128:(ec + 1) * 128], rhs=am_[:],
                    start=True, stop=False), reads=[bv_, bam_], writes=[bOb])
                for dh in range(2):
                    S.op("pe", lambda e, ec=ec, bO=bO, dh=dh, Sb_r=Sb_r, g_=g_, cs=cs: e.matmul(
                        bO[:, ec * 64:(ec + 1) * 64], lhsT=Sb_r[:, dh, ec * 128:(ec + 1) * 128],
                        rhs=g_[:, 0, dh, cs:cs + 64], start=False, stop=(dh == 1)),
                        reads=[bSb_r, bg_], writes=[bOb])
            S.op("act", lambda e, och_=och_, bO=bO: e.activation(out=och_[:].rearrange("p a t -> p (a t)"),
                                                              in_=bO[:, 0:256], func=AF.Identity),
                 reads=[bOb], writes=[boch_])
            S.dma("sp", oS[dr, :, :, c * 64:(c + 1) * 64], och_[:], reads=[boch_])
            for dh in range(2):
                bu = 4 * dr + 2 + dh
                S.op("pe", lambda e, bu=bu, dh=dh, kdt_=kdt_, v_=v_, ci=ci: e.matmul(
                    banks[bu][:, :], lhsT=kdt_[0:64, dh * 128:(dh + 1) * 128], rhs=v_[0:64, ci, :],
                    start=True, stop=True), reads=[bkdt_, bv_], writes=[b_bank[bu]])
                S.op("dve", lambda e, bu=bu, dh=dh, dr=dr, c=c: e.scalar_tensor_tensor(
                    out=Sf[dr][:, dh, :], in0=Sf[dr][:, dh, :], scalar=dec[:, dr, dh, c:c + 1], in1=banks[bu][:, :],
                    op0=ALU.mult, op1=ALU.add), reads=[b_Sf[dr], b_dec, b_bank[bu]], writes=[b_Sf[dr]])
            S.op("act", lambda e, Sb_w=Sb_w, dr=dr: e.activation(out=Sb_w[:], in_=Sf[dr][:], func=AF.Identity),
                 reads=[b_Sf[dr]], writes=[bSb_w])
    sfree()
    if stop <= 4:
        free_all()
        if env is None:
            S.emit()
        return nc, S, locals()

    ofb = [salloc("ofb%d" % i, [128, 4, 512], F32) for i in range(2)]
    obb = [salloc("obb%d" % i, [128, 4, 512], F32) for i in range(2)]
    sg = [salloc("sg%d" % i, [128, 4, 512], F32) for i in range(2)]
    sq3 = [salloc("sq3_%d" % i, [128, 4, 512], F32) for i in range(2)]
    rr = [salloc("rr%d" % i, [128, 512], F32) for i in range(2)]
    zt3 = [salloc("zt3_%d" % i, [128, 4, 512], BF16) for i in range(2)]
    b_ofb, b_obb, b_sg, b_sq3, b_rr, b_zt3 = ([B("%s%d" % (c_, i)) for i in range(2)]
                                              for c_ in ("ofb", "obb", "sg", "sq3", "rr", "zt3"))
    for ti, (c0, n, seg) in enumerate(ttiles):
        i2 = ti % 2
        S.dma("sp", ofb[i2][:, :, 0:n], oS[0, :, :, c0:c0 + n], writes=[b_ofb[i2]])
        S.dma("sp", obb[i2][:, :, 0:n], oS[1, :, :, c0:c0 + n], writes=[b_obb[i2]])
        S.dma("sp", sg[i2][:, :, 0:n], glS[4:8, :, c0:c0 + n].rearrange("j p t -> p j t"), writes=[b_sg[i2]])
        S.op("dve", lambda e, i2=i2, n=n: e.tensor_tensor(out=ofb[i2][:, :, 0:n], in0=ofb[i2][:, :, 0:n],
                                                        in1=obb[i2][:, :, 0:n], op=ALU.add),
             reads=[b_ofb[i2], b_obb[i2]], writes=[b_ofb[i2]])
        S.op("act", lambda e, i2=i2, n=n: e.activation(out=sq3[i2][:, :, 0:n], in_=ofb[i2][:, :, 0:n], func=AF.Square),
             reads=[b_ofb[i2]], writes=[b_sq3[i2]])
        bk = i2
        for ec in range(4):
            S.op("pe", lambda e, bk=bk, ec=ec, i2=i2, n=n: e.matmul(banks[bk][:, 0:n], lhsT=ones_f, rhs=sq3[i2][:, ec, 0:n],
                                                                 start=(ec == 0), stop=(ec == 3)),
                 reads=[b_cf, b_sq3[i2]], writes=[b_bank[bk]])
        S.op("act", lambda e, bk=bk, i2=i2, n=n: e.activation(out=rr[i2][:, 0:n], in_=banks[bk][:, 0:n], func=AF.Sqrt,
                                                            scale=1.0 / 512.0, bias=epsb[:, 0:1]),
             reads=[b_bank[bk], b_eps], writes=[b_rr[i2]])
        S.op("dve", lambda e, i2=i2, n=n: e.reciprocal(out=rr[i2][:, 0:n], in_=rr[i2][:, 0:n]),
             reads=[b_rr[i2]], writes=[b_rr[i2]])
        for ec in range(4):
            S.op("dve", lambda e, i2=i2, ec=ec, n=n: e.scalar_tensor_tensor(
                out=sq3[i2][:, ec, 0:n], in0=ofb[i2][:, ec, 0:n], scalar=gngt[:, ec:ec + 1], in1=rr[i2][:, 0:n],
                op0=ALU.mult, op1=ALU.mult), reads=[b_ofb[i2], b_gng, b_rr[i2], b_bank[bk]], writes=[b_sq3[i2]])
            S.op("dve", lambda e, i2=i2, ec=ec, n=n: e.tensor_tensor(out=zt3[i2][:, ec, 0:n], in0=sq3[i2][:, ec, 0:n],
                                                                  in1=sg[i2][:, ec, 0:n], op=ALU.mult),
                 reads=[b_sq3[i2], b_sg[i2]], writes=[b_zt3[i2]])
        S.dma("sp", zT[4:8, :, c0:c0 + n].rearrange("j p t -> p j t"), zt3[i2][:, :, 0:n], reads=[b_zt3[i2]])
    sfree()
    free_all()
    if env is None:
        S.emit()
    return nc, S, locals()


def const_f32():
    c = np.zeros((128, 5, 128), np.float32)
    c[:, 0, :] = 1.0
    i = np.arange(128)
    c[i, 1, (i + 64) % 128] = 1.0
    c[:, 1, :] = c[:, 1, :].T
    s_, t_ = np.meshgrid(i, i, indexing="ij")
    same = (s_ // 64) == (t_ // 64)
    c[:, 2, :] = np.where(same & (s_ <= t_), -1.0 / 16.0, 0.0)
    c[:, 3, :] = np.where(same & (s_ >= t_), -1.0 / 16.0, 0.0)
    c[i, 4, i] = 1.0
    return c


def const_bf16():
    c = np.zeros((128, 4, 128), np.float32)
    i = np.arange(128)
    c[i, 0, i] = 1.0
    c[:, 1, :] = 1.0
    s_, t_ = np.meshgrid(np.arange(64), np.arange(64), indexing="ij")
    c[0:64, 2, 0:64] = (s_ <= t_)
    c[0:64, 3, 0:64] = (s_ >= t_)
    return c.astype(NPBF)


def rope_tables(rows):
    i = np.arange(64, dtype=np.float32)
    inv = (np.float32(10000.0) ** (-i / np.float32(64.0))).astype(np.float32)

    def tab(npos):
        ang = (np.arange(npos, dtype=np.float32)[None, :] * inv[:, None]).astype(np.float32)
        cos = np.concatenate([np.cos(ang), np.cos(ang)], 0)
        sin = np.concatenate([-np.sin(ang), np.sin(ang)], 0)
        return np.ascontiguousarray(np.stack([cos, sin], 1).astype(np.float32))

    return tab(rows), tab(64)


def na_bias_table(rpb_l, g, rows):
    kh = min(8, rows)
    pairs = na_pairs(rows)
    npair = rows // 2
    reps = [0, 1, 2, npair - 2, npair - 1]
    out = np.full((128, 4, 25, 128), NEG, np.float32)
    loc = np.arange(128)
    yl, xl = loc // 64, loc % 64
    for ci, m in enumerate(reps):
        for j, p in enumerate(pairs[m]):
            ky = (2 * p + yl)[:, None]
            kx = xl[:, None]
            qy = (2 * m + yl)[None, :]
            qx = xl[None, :]
            rs = np.clip(qy - kh // 2, 0, rows - kh)
            okr = (ky >= rs) & (ky < rs + kh)
            cst = np.clip(qx - 8, 0, 64 - 16)
            okc = (kx >= cst) & (kx < cst + 16)
            dy = np.clip(ky - qy + 7, 0, 14)
            dx = np.clip(kx - qx + 15, 0, 30)
            ok = okr & okc
            for h in range(4):
                vals = rpb_l[4 * g + h][dy, dx]
                out[:, h, ci * 5 + j, :] = np.where(ok, vals, np.float32(NEG))
    return np.ascontiguousarray(out.reshape(128, 100, 128))


def pmaj(a):
    n = a.shape[0] // 128
    return np.ascontiguousarray(a.reshape(n, 128, *a.shape[1:]).swapaxes(0, 1))


def a_weight_inputs(w_in_l, rpb_l, wdec_l, bdec_l, gng_l, g, rows):
    cols = a_cols(g)
    im = {}
    for p in range(4):
        im["wp%d" % p] = np.ascontiguousarray(w_in_l[:, cols[p]]).reshape(32, 128, -1)
    im["biasT"] = na_bias_table(rpb_l, g, rows)
    wd = np.empty((17, 2, 256), np.float32)
    wd[0:16] = wdec_l[:, :, 256 * g:256 * g + 256].transpose(1, 0, 2)
    wd[16] = bdec_l[:, 256 * g:256 * g + 256]
    im["wd"] = wd
    im["gng"] = np.ascontiguousarray(gng_l.reshape(4, 128).T)
    rR, rC = rope_tables(rows)
    im["ropeR"], im["ropeC"] = rR, rC
    im["cf"], im["cb"] = const_f32(), const_bf16()
    return im


_PROG = {}
_DBG = None


def _prog(key, fn):
    if key not in _PROG:
        _PROG[key] = fn()
    return _PROG[key]


def _run(nc, in_maps):
    res = run_bass_kernel_spmd(nc, in_maps, core_ids=list(range(len(in_maps))))
    return res.results


def kernel_impl(x, c, ctx, c_ctx, ada_w, ada_b, norm_g, w_in, na_rpb, gla_w_decay, gla_b_decay, gla_norm_g, w_out,
                final_norm_g):
    f32 = np.float32
    x, c, ctx, c_ctx = (np.asarray(a, f32) for a in (x, c, ctx, c_ctx))
    ada_w, ada_b, norm_g, w_in, na_rpb = (np.asarray(a, f32) for a in (ada_w, ada_b, norm_g, w_in, na_rpb))
    gla_w_decay, gla_b_decay, gla_norm_g, w_out, final_norm_g = (
        np.asarray(a, f32) for a in (gla_w_decay, gla_b_decay, gla_norm_g, w_out, final_norm_g))
    depth = ada_w.shape[0]
    nb, nl, _ = x.shape
    nctx = ctx.shape[1]
    rows = nl // GRID_W
    nt = nctx + nl
    assert nb == 2
    cores = [(b, g) for b in range(2) for g in range(4)]
    ones = np.ones((128, 128), f32)

    nc_ada = _prog(("ada", depth), lambda: build_ada(depth))
    cvec = np.stack([c[0], c[1], c_ctx], 0)
    cT = np.ascontiguousarray(cvec.T.reshape(32, 128, 3).transpose(1, 0, 2))
    ims = []
    for j in range(8):
        cs = slice(ADA_COLS * j, ADA_COLS * (j + 1))
        ims.append({"cT": cT, "adaw": np.ascontiguousarray(ada_w[:, :, cs]).reshape(depth, 32, 128, ADA_COLS),
                    "adab": np.ascontiguousarray(np.broadcast_to(ada_b[:, None, cs], (depth, 3, ADA_COLS)))})
    r = _run(nc_ada, ims)
    mod = np.concatenate([r[j]["mod"] for j in range(8)], axis=-1)
    if _DBG is not None:
        _DBG['mod'] = mod
    shift, scale, gate = mod[:, :, 0:D], mod[:, :, D:2 * D], mod[:, :, 2 * D:3 * D]

    def seg2(v, l, b, sl):
        return pmaj(np.stack([v[l, 2, sl], v[l, b, sl]], -1))

    xs = []
    for (b, g) in cores:
        fs = slice(1024 * g, 1024 * (g + 1))
        xb = np.concatenate([ctx[b][:, fs], x[b][:, fs]], 0)
        xs.append(np.ascontiguousarray(xb.T).reshape(8, 128, nt))

    def gather_u(res):
        uT = [np.concatenate([res[4 * b + g]["uT"] for g in range(4)], 0) for b in range(2)]
        ssq4 = [np.concatenate([res[4 * b + g]["ssq"] for g in range(4)], 0) for b in range(2)]
        return uT, ssq4

    nc_p0 = _prog(("p0", nt), lambda: build_p(False, nt))
    ims = []
    for ci, (b, g) in enumerate(cores):
        fs = slice(1024 * g, 1024 * (g + 1))
        ims.append({"xT": xs[ci], "gvec": pmaj(norm_g[0, fs]), "scT": seg2(scale, 0, b, fs), "ones": ones})
    uT, ssq4 = gather_u(_run(nc_p0, ims))
    if _DBG is not None:
        _DBG['u0'] = uT
        _DBG['ssq0'] = ssq4

    nc_a = _prog(("a", rows, nctx), lambda: build_a(rows, nctx)[0])
    nc_p = _prog(("p", nt), lambda: build_p(True, nt))
    for l in range(depth):
        ims = []
        wcache = {}
        for ci, (b, g) in enumerate(cores):
            if g not in wcache:
                wcache[g] = a_weight_inputs(w_in[l], na_rpb[l], gla_w_decay[l], gla_b_decay[l], gla_norm_g[l], g, rows)
            im = {"uT": uT[b], "ssq4": ssq4[b], "shT": seg2(shift, l, b, slice(0, D))}
            im.update(wcache[g])
            ims.append(im)
        res = _run(nc_a, ims)
        del wcache
        zT = []
        for b in range(2):
            zf = np.empty((32, 128, nt), NPBF)
            for g in range(4):
                zf[4 * g:4 * g + 4] = res[4 * b + g]["zT"][0:4]
                zf[16 + 4 * g:16 + 4 * g + 4] = res[4 * b + g]["zT"][4:8]
            zT.append(zf)
        if _DBG is not None:
            _DBG['z%d' % l] = zT
        ln = min(l + 1, depth - 1)
        ims = []
        for ci, (b, g) in enumerate(cores):
            fs = slice(1024 * g, 1024 * (g + 1))
            ims.append({"xT": xs[ci], "zT": zT[b], "wout": np.ascontiguousarray(w_out[l][:, fs]).reshape(32, 128, 1024),
                        "gateT": seg2(gate, l, b, fs), "gvec": pmaj(norm_g[ln, fs]), "scT": seg2(scale, ln, b, fs),
                        "ones": ones})
        res = _run(nc_p, ims)
        xs = [res[ci]["xoT"] for ci in range(8)]
        uT, ssq4 = gather_u(res)
        if _DBG is not None:
            _DBG['x%d' % (l + 1)] = xs
            _DBG['u%d' % (l + 1)] = uT
            _DBG['ssq%d' % (l + 1)] = ssq4

    nc_f = _prog(("f", nl), lambda: build_f(nl))
    ims = []
    for ci, (b, g) in enumerate(cores):
        fs = slice(1024 * g, 1024 * (g + 1))
        ims.append({"xT": np.ascontiguousarray(xs[ci][:, :, nctx:]), "ssq4": np.ascontiguousarray(ssq4[b][:, nctx:]),
                    "fg": pmaj(final_norm_g[fs]), "ones": ones})
    res = _run(nc_f, ims)
    out = np.empty((2, nl, D), f32)
    for ci, (b, g) in enumerate(cores):
        out[b][:, 1024 * g:1024 * (g + 1)] = res[ci]["oT"].reshape(1024, nl).T
    return out


def kernel(**inputs):
    return kernel_fused(**inputs)


XPAD = 16
GROUPS4 = [[0, 1, 2, 3], [4, 5, 6, 7]]


def build_fused(depth=DEPTH, rows=64, nctx=CTX):
    nl = rows * GRID_W
    nt = nctx + nl
    ntx = nt + XPAD
    npt = nt // PT
    nc = bass.Bass("TRN2", target_bir_lowering=False)
    S = Sched(nc)
    B = S.buf
    banks = [_ps(nc, "bank%d" % i, [128, 512]) for i in range(8)]
    b_bank = [B("bank%d" % i) for i in range(8)]
    xT = _din(nc, "xT", [8, 128, nt], F32)
    cT2 = _din(nc, "cT2", [128, 32, 2], F32)
    adaw = _din(nc, "adaw", [depth, 32, 128, 3072], F32)
    adab = _din(nc, "adab", [depth, 2, 3072], F32)
    gvec = _din(nc, "gvec", [128, depth, 8], F32)
    fg = _din(nc, "fg", [128, 8], F32)
    wout = _din(nc, "wout", [depth, 32, 128, 1024], F32)
    wp = [[_din(nc, "wp%d_%d" % (p, l), [32, 128, 544 if p == 2 else 1024], F32) for p in range(4)] for l in range(depth)]
    biasT = [_din(nc, "biasT_%d" % l, [128, 100, 128], F32) for l in range(depth)]
    wd = [_din(nc, "wd_%d" % l, [17, 2, 256], F32) for l in range(depth)]
    gng = [_din(nc, "gng_%d" % l, [128, 4], F32) for l in range(depth)]
    ropeR = _din(nc, "ropeR", [128, 2, rows], F32)
    ropeC = _din(nc, "ropeC", [128, 2, 64], F32)
    cf = _din(nc, "cf", [128, 5, 128], F32)
    cb = _din(nc, "cb", [128, 4, 128], BF16)
    oT = _dout(nc, "oT", [8, 128, nl], F32)
    xS = _dint(nc, "xS", [8, 128, nt], F32)
    uloc = _dint(nc, "uloc", [8, 128, ntx], BF16)
    ufull = _dint(nc, "ufull", [32, 128, ntx], BF16)
    sloc = _dint(nc, "sloc", [1, nt], F32)
    half = 64 * ntx
    cin = [nc.dram_tensor(_u("cin"), [32, half // 32], BF16) for _ in range(16)]
    cout = [nc.dram_tensor(_u("cout"), [128, half // 32], BF16) for _ in range(16)]
    sin_ = nc.dram_tensor(_u("sin"), [32, nt // 32], F32)
    sout = nc.dram_tensor(_u("sout"), [128, nt // 32], F32)
    b_cin = [B("cin%d" % j) for j in range(16)]
    b_cout = [B("cout%d" % j) for j in range(16)]
    b_sin, b_sout = B("sin"), B("sout")
    ssq4 = sout.ap().rearrange("(g a) b -> g (a b)", g=4)

    def barrier():
        S.flush(barrier=True)

    def exchange(with_ssq):
        barrier()
        ufg = ufull.rearrange("(g i) p t -> g i p t", g=4)
        for i in range(8):
            for hf in range(2):
                j = 2 * i + hf
                S.dma("sp", cin[j].ap().rearrange("a (b t) -> (a b) t", b=2), uloc[i, 64 * hf:64 * hf + 64, :],
                      writes=[b_cin[j]])
        for j in range(16):
            S.cc(GROUPS4, cin[j], cout[j], reads=[b_cin[j]], writes=[b_cout[j]])
        for i in range(8):
            for hf in range(2):
                j = 2 * i + hf
                S.dma("sp", ufg[:, i, 64 * hf:64 * hf + 64, :].rearrange("g p t -> p g t"),
                      cout[j].ap().rearrange("(g a) (b t) -> (a b) g t", g=4, b=2), reads=[b_cout[j]])
        if with_ssq:
            S.dma("sp", sin_.ap().rearrange("a b -> (a b)").rearrange("(o n) -> o n", o=1), sloc[:, :], writes=[b_sin])
            S.cc(GROUPS4, sin_, sout, reads=[b_sin], writes=[b_sout])
        barrier()

    modT = _sb(nc, "modT", [128, depth, 24, 2], F32)
    gv = _sb(nc, "gv", [128, depth, 8], F32)
    b_mod, b_gv = B("modT"), B("gv")
    env = {"nc": nc, "S": S, "banks": banks, "b_bank": b_bank, "ropeR": ropeR, "ropeC": ropeC, "cf": cf, "cb": cb}
    for k_, shp, dt_ in (("naS", [12, 128, nt], BF16), ("glS", [8, 128, nt], F32), ("lrS", [32, nt], F32),
                         ("vS", [nt, 1024], BF16), ("gpS", [2, 3, 128, 2, nt], BF16), ("oS", [2, 128, 4, nt], F32)):
        env[k_] = _dint(nc, k_, shp, dt_)
    S.dma("sp", gv[:], gvec[:, :, :], writes=[b_gv])

    ph = []

    def alloc(name, shape, dt):
        t = nc.sbuf_tensor(_u(name), list(shape), dt)
        ph.append(t)
        return t.__enter__()

    def free_all():
        S.flush(barrier=True)
        while ph:
            ph.pop().__exit__(None, None, None)

    ct = alloc("ct", [128, 32, 2], F32)
    st = alloc("st", [128, 32, 2], F32)
    NW = 3
    wt = [alloc("wt%d" % i, [128, 3072], F32) for i in range(NW)]
    bt = alloc("bt", [2, 3072], F32)
    rt = alloc("rt", [2, 3072], F32)
    idf = alloc("idf", [128, 128], F32)
    b_ct, b_st, b_bt, b_rt, b_idf = B("ct"), B("st"), B("bt"), B("rt"), B("idf")
    b_wt = [B("wt%d" % i) for i in range(NW)]
    S.dma("sp", ct[:], cT2[:, :, :], writes=[b_ct])
    S.dma("sp", idf[:], cf[:, 4, :], writes=[b_idf])
    S.op("act", lambda e: e.activation(out=st[:], in_=ct[:], func=AF.Silu), reads=[b_ct], writes=[b_st])
    it = 0
    for l in range(depth):
        S.dma("sp", bt[:], adab[l, :, :], writes=[b_bt])
        for kc in range(32):
            w = it % NW
            S.dma("sp", wt[w][:], adaw[l, kc, :, :], writes=[b_wt[w]])
            for j in range(6):
                S.op("pe", lambda e, w=w, j=j, kc=kc: e.matmul(
                    banks[j][0:2, :], lhsT=st[:, kc, :], rhs=wt[w][:, j * 512:(j + 1) * 512],
                    start=(kc == 0), stop=(kc == 31)), reads=[b_st, b_wt[w]], writes=[b_bank[j]])
            it += 1
        for j in range(6):
            S.op("dve", lambda e, j=j: e.tensor_tensor(
                out=rt[:, j * 512:(j + 1) * 512], in0=banks[j][0:2, :], in1=bt[:, j * 512:(j + 1) * 512], op=ALU.add),
                reads=[b_bank[j], b_bt], writes=[b_rt])
        for j in range(24):
            S.op("pe", lambda e, j=j: e.matmul(banks[6][:, 2 * j:2 * j + 2], lhsT=rt[0:2, j * 128:(j + 1) * 128],
                                             rhs=idf[0:2, 0:2], start=True, stop=True),
                 reads=[b_rt, b_idf], writes=[b_bank[6]])
        S.op("dve", lambda e, l=l: e.tensor_copy(out=modT[:, l, :, :], in_=banks[6][:, 0:48].rearrange("p (j s) -> p j s", s=2)),
             reads=[b_bank[6]], writes=[b_mod])
    free_all()

    def emit_p(l_next, l_prev, first):
        has_z = not first
        xin = xT if first else xS
        G = alloc("G", [128, 8, 2], F32)
        shb2 = alloc("shb2", [128, 8, 2], BF16)
        ones = alloc("ones_sb", [128, 128], F32)
        ssq_sb = alloc("ssq_sb", [1, nt], F32)
        xt = [alloc("xt%d" % i, [128, 8, PT], F32) for i in range(2)]
        ut = [alloc("ut%d" % i, [128, 8, PT], BF16) for i in range(2)]
        sq = [alloc("sq%d" % i, [128, PT], F32) for i in range(2)]
        b_G, b_ones, b_ssq, b_shb2 = B("G"), B("ones"), B("ssq"), B("shb2")
        b_xt, b_ut, b_sq = [B("xt0"), B("xt1")], [B("ut0"), B("ut1")], [B("sq0"), B("sq1")]
        pss, b_pss = banks[7], b_bank[7]
        if has_z:
            wbf = alloc("wbf", [128, 32, 1024], BF16)
            zt = [alloc("zt%d" % i, [128, 32, PT], BF16) for i in range(2)]
            xn = [alloc("xn%d" % i, [128, 8, PT], F32) for i in range(2)]
            b_wbf = [B("wbf%d" % k) for k in range(32)]
            b_zt, b_xn = [B("zt0"), B("zt1")], [B("xn0"), B("xn1")]
            for kc in range(32):
                S.dma("pool", wbf[:, kc, :], wout[l_prev, kc, :, :], writes=[b_wbf[kc]])
        S.dma("sp", ones[:], cf[:, 0, :], writes=[b_ones])
        S.op("dve", lambda e: e.tensor_scalar(out=G[:], in0=modT[:, l_next, 8:16, :], scalar1=1.0, scalar2=None, op0=ALU.add),
             reads=[b_mod], writes=[b_G])
        for s_ in range(2):
            S.op("dve", lambda e, s_=s_: e.tensor_tensor(out=G[:, :, s_], in0=G[:, :, s_], in1=gv[:, l_next, :], op=ALU.mult),
                 reads=[b_G, b_gv], writes=[b_G])
        S.op("dve", lambda e: e.tensor_copy(out=shb2[:], in_=modT[:, l_next, 0:8, :]), reads=[b_mod], writes=[b_shb2])
        S.dma("sp", uloc[:, :, nt:nt + 2].rearrange("f p t -> p f t"), shb2[:], reads=[b_shb2])
        for tt in range(npt):
            c0 = tt * PT
            seg = 0 if tt == 0 else 1
            i = tt % 2
            S.dma("sp", xt[i][:], xin[:, :, c0:c0 + PT].rearrange("f p t -> p f t"), writes=[b_xt[i]])
            if has_z:
                S.dma("sp", zt[i][:], ufull[:, :, c0:c0 + PT].rearrange("k p t -> p k t"), writes=[b_zt[i]])
            for fc in range(8):
                if has_z:
                    pb = fc % 2
                    for kc in range(32):
                        S.op("pe", lambda e, pb=pb, kc=kc, fc=fc, i=i: e.matmul(
                            banks[pb][:, 0:PT], lhsT=wbf[:, kc, fc * 128:(fc + 1) * 128], rhs=zt[i][:, kc, :],
                            start=(kc == 0), stop=(kc == 31)), reads=[b_wbf[kc], b_zt[i]], writes=[b_bank[pb]])
                    S.op("dve", lambda e, pb=pb, fc=fc, i=i, seg=seg: e.scalar_tensor_tensor(
                        out=xn[i][:, fc, :], in0=banks[pb][:, 0:PT], scalar=modT[:, l_prev, 16 + fc, seg:seg + 1],
                        in1=xt[i][:, fc, :], op0=ALU.mult, op1=ALU.add),
                        reads=[b_bank[pb], b_mod, b_xt[i]], writes=[b_xn[i]])
                    src, b_src = xn[i], b_xn[i]
                else:
                    src, b_src = xt[i], b_xt[i]
                q = (tt * 8 + fc) % 2
                S.op("act", lambda e, src=src, fc=fc, q=q: e.activation(out=sq[q][:], in_=src[:, fc, :], func=AF.Square),
                     reads=[b_src], writes=[b_sq[q]])
                S.op("pe", lambda e, q=q, fc=fc: e.matmul(pss[0:1, 0:PT], lhsT=ones[:, 0:1], rhs=sq[q][:],
                                                          start=(fc == 0), stop=(fc == 7)),
                     reads=[b_ones, b_sq[q]], writes=[b_pss])
                S.op("dve", lambda e, src=src, fc=fc, i=i, seg=seg: e.tensor_scalar(
                    out=ut[i][:, fc, :], in0=src[:, fc, :], scalar1=G[:, fc, seg:seg + 1], scalar2=None, op0=ALU.mult),
                    reads=[b_src, b_G], writes=[b_ut[i]])
            S.op("dve", lambda e, c0=c0: e.tensor_copy(out=ssq_sb[:, c0:c0 + PT], in_=pss[0:1, 0:PT]),
                 reads=[b_pss], writes=[b_ssq])
            S.dma("sp", xS[:, :, c0:c0 + PT].rearrange("f p t -> p f t"), src[:], reads=[b_src])
            S.dma("sp", uloc[:, :, c0:c0 + PT].rearrange("f p t -> p f t"), ut[i][:], reads=[b_ut[i]])
        S.dma("sp", sloc[:, :], ssq_sb[:], reads=[b_ssq])
        free_all()

    emit_p(0, None, True)
    exchange(True)
    for l in range(depth):
        env.update({"uT": ufull, "ssq4": ssq4, "wp": wp[l], "biasT": biasT[l], "wd": wd[l], "gng": gng[l],
                    "zT": uloc})
        _build_a(rows, nctx, 9, env)
        exchange(False)
        emit_p(min(l + 1, depth - 1), l, False)
        exchange(True)

    TT = 512
    ones = alloc("ones_f", [128, 128], F32)
    g = alloc("g_f", [128, 8], F32)
    s4 = alloc("s4_f", [4, nt], F32)
    rstd = alloc("rstd_f", [128, nl], F32)
    epsb = alloc("epsb_f", [128, 1], F32)
    xt = [alloc("xtf%d" % i, [128, 8, TT], F32) for i in range(2)]
    ot = [alloc("otf%d" % i, [128, 8, TT], F32) for i in range(2)]
    b_ones, b_g, b_s4, b_rstd, b_eps = B("ones"), B("g"), B("s4"), B("rstd"), B("epsf")
    b_xt, b_ot = [B("x0"), B("x1")], [B("o0"), B("o1")]
    S.dma("sp", ones[:], cf[:, 0, :], writes=[b_ones])
    S.dma("sp", g[:], fg[:, :], writes=[b_g])
    S.dma("sp", s4[:], ssq4, writes=[b_s4])
    S.op("dve", lambda e: e.memset(epsb[:], EPS), writes=[b_eps])
    for tt in range(nl // TT):
        c0 = tt * TT
        i = tt % 2
        S.op("pe", lambda e, i=i, c0=c0: e.matmul(banks[i][:, :], lhsT=ones[0:4, :], rhs=s4[:, nctx + c0:nctx + c0 + TT],
                                                 start=True, stop=True), reads=[b_ones, b_s4], writes=[b_bank[i]])
        S.op("act", lambda e, i=i, c0=c0: e.activation(out=rstd[:, c0:c0 + TT], in_=banks[i][:, :], func=AF.Sqrt,
                                                     scale=1.0 / D, bias=epsb[:, 0:1]),
             reads=[b_bank[i], b_eps], writes=[b_rstd])
        S.op("dve", lambda e, c0=c0: e.reciprocal(out=rstd[:, c0:c0 + TT], in_=rstd[:, c0:c0 + TT]),
             reads=[b_rstd], writes=[b_rstd])
        S.dma("sp", xt[i][:], xS[:, :, nctx + c0:nctx + c0 + TT].rearrange("f p t -> p f t"), writes=[b_xt[i]])
        for fc in range(8):
            S.op("dve", lambda e, i=i, fc=fc, c0=c0: e.scalar_tensor_tensor(
                out=ot[i][:, fc, :], in0=xt[i][:, fc, :], scalar=g[:, fc:fc + 1], in1=rstd[:, c0:c0 + TT],
                op0=ALU.mult, op1=ALU.mult), reads=[b_xt[i], b_g, b_rstd], writes=[b_ot[i]])
        S.dma("sp", oT[:, :, c0:c0 + TT].rearrange("f p t -> p f t"), ot[i][:], reads=[b_ot[i]])
    free_all()
    S.emit()
    return nc, S


def z_perm():
    perm = np.empty(32, np.int64)
    for g in range(4):
        for i in range(8):
            perm[8 * g + i] = (4 * g + i) if i < 4 else (16 + 4 * g + (i - 4))
    return perm


def kernel_fused(x, c, ctx, c_ctx, ada_w, ada_b, norm_g, w_in, na_rpb, gla_w_decay, gla_b_decay, gla_norm_g, w_out,
                 final_norm_g):
    f32 = np.float32
    x, c, ctx, c_ctx = (np.asarray(a, f32) for a in (x, c, ctx, c_ctx))
    ada_w, ada_b, norm_g, w_in, na_rpb = (np.asarray(a, f32) for a in (ada_w, ada_b, norm_g, w_in, na_rpb))
    gla_w_decay, gla_b_decay, gla_norm_g, w_out, final_norm_g = (
        np.asarray(a, f32) for a in (gla_w_decay, gla_b_decay, gla_norm_g, w_out, final_norm_g))
    depth = ada_w.shape[0]
    nb, nl, _ = x.shape
    nctx = ctx.shape[1]
    rows = nl // GRID_W
    nt = nctx + nl
    nc = _prog(("fused", depth, rows, nctx), lambda: build_fused(depth, rows, nctx)[0])
    perm = z_perm()
    rR, rC = rope_tables(rows)
    cfa, cba = const_f32(), const_bf16()
    shared = {}
    ims = []
    for b in range(2):
        for g in range(4):
            fs = np.arange(1024 * g, 1024 * (g + 1))
            im = {}
            xb = np.concatenate([ctx[b][:, fs], x[b][:, fs]], 0)
            im["xT"] = np.ascontiguousarray(xb.T).reshape(8, 128, nt)
            cv = np.stack([c_ctx, c[b]], -1)
            im["cT2"] = pmaj(cv)
            if g not in shared:
                sh = {}
                acols = np.concatenate([fs, D + fs, 2 * D + fs])
                sh["adaw"] = np.ascontiguousarray(ada_w[:, :, acols]).reshape(depth, 32, 128, 3072)
                sh["adab"] = np.ascontiguousarray(np.broadcast_to(ada_b[:, None, acols], (depth, 2, 3072)))
                sh["gvec"] = np.ascontiguousarray(norm_g[:, fs].reshape(depth, 8, 128).transpose(2, 0, 1))
                sh["fg"] = pmaj(final_norm_g[fs])
                wo = w_out[:, :, fs].reshape(depth, 32, 128, 1024)
                sh["wout"] = np.ascontiguousarray(wo[:, perm])
                for l in range(depth):
                    aw = a_weight_inputs(w_in[l], na_rpb[l], gla_w_decay[l], gla_b_decay[l], gla_norm_g[l], g, rows)
                    for p in range(4):
                        sh["wp%d_%d" % (p, l)] = aw["wp%d" % p]
                    sh["biasT_%d" % l] = aw["biasT"]
                    sh["wd_%d" % l] = aw["wd"]
                    sh["gng_%d" % l] = aw["gng"]
                shared[g] = sh
            im.update(shared[g])
            im.update({"ropeR": rR, "ropeC": rC, "cf": cfa, "cb": cba})
            ims.append(im)
    res = _run(nc, ims)
    out = np.empty((2, nl, D), f32)
    for ci in range(8):
        b, g = ci // 4, ci % 4
        out[b][:, 1024 * g:1024 * (g + 1)] = res[ci]["oT"].reshape(1024, nl).T
    return out
```

```python
import numpy as np
import ml_dtypes
import concourse.bass as bass
import concourse.mybir as mybir
from concourse.bass_utils import run_bass_kernel_spmd

F32 = mybir.dt.float32
BF16 = mybir.dt.bfloat16
AF = mybir.ActivationFunctionType
ALU = mybir.AluOpType
NPBF = ml_dtypes.bfloat16

D = 4096
DEPTH = 4
SEQ = 4096
CTX = 256
NT = SEQ + CTX
GRID_W = 64
EPS = 1e-6
NEG = -1e30


class Buf:
    __slots__ = ("name", "w", "r")

    def __init__(self, name):
        self.name = name
        self.w = None
        self.r = []


class Sched:
    EPOCH = 20000

    def __init__(self, nc, n_dma_sems=40):
        self.nc = nc
        self.eng = {"pe": nc.tensor, "act": nc.scalar, "dve": nc.vector, "pool": nc.gpsimd, "sp": nc.sync}
        self.ops = []
        self.n_dma_sems = n_dma_sems
        self.done = 0
        self.ev = []
        self.deps = []
        self.cnt = {e: 0 for e in self.eng}
        self.n_dma = 0
        self.dma_prev = {}
        self.sems = {}
        self.seen = {e: {} for e in self.eng}
        self.last = {}
        self.bufs = []

    def buf(self, name):
        b = Buf(name)
        self.bufs.append(b)
        return b

    def op(self, eng, fn, reads=(), writes=()):
        self.ops.append((eng, fn, tuple(reads), tuple(writes), False))

    def dma(self, q, out, in_, reads=(), writes=()):
        self.ops.append((q, lambda e, o=out, i=in_: e.dma_start(out=o, in_=i), tuple(reads), tuple(writes), True))

    def cc(self, groups, cin, cout, reads=(), writes=()):
        def fn(e, cin=cin, cout=cout):
            return e.collective_compute("AllGather", ALU.bypass, replica_groups=groups,
                                        ins=[cin.ap().opt()], outs=[cout.ap().opt()])
        self.ops.append(("pool", fn, tuple(reads), tuple(writes), "cc"))

    def _sem(self, sk):
        if sk not in self.sems:
            self.sems[sk] = self.nc.semaphore("s_%s_%s" % (sk[0], sk[1])).__enter__()
        return self.sems[sk]

    def flush(self, barrier=False):
        ops = self.ops
        n = len(ops)
        for k in range(self.done, n):
            eng, fn, reads, writes, is_dma = ops[k]
            d = set()
            for b in reads:
                if b.w is not None:
                    d.add(b.w)
            for b in writes:
                if b.w is not None:
                    d.add(b.w)
                d.update(b.r)
            if is_dma == "cc":
                self.n_cc = getattr(self, "n_cc", 0) + 1
                evk = (("cc", 0), self.n_cc)
            elif is_dma:
                s = self.n_dma % self.n_dma_sems
                if s in self.dma_prev:
                    d.add(self.dma_prev[s])
                self.dma_prev[s] = k
                evk = (("dma", s), 16 * (self.n_dma // self.n_dma_sems + 1))
                self.n_dma += 1
            else:
                c = self.cnt[eng]
                evk = ((eng, c // self.EPOCH), c % self.EPOCH + 1)
                self.cnt[eng] = c + 1
            self.ev.append(evk)
            d.discard(k)
            for b in reads:
                b.r.append(k)
            for b in writes:
                b.w = k
                b.r = []
            e = self.eng[eng]
            need = {}
            for dk in d:
                if (not ops[dk][4]) and ops[dk][0] == eng and eng == "pe":
                    continue
                sk, v = self.ev[dk]
                if v > need.get(sk, 0):
                    need[sk] = v
            sn = self.seen[eng]
            for sk, v in need.items():
                if sn.get(sk, 0) < v:
                    e.wait_ge(self._sem(sk), v)
                    sn[sk] = v
            ins = fn(e)
            sk, v = evk
            ins.then_inc(self._sem(sk), 16 if is_dma is True else 1)
            if v > self.last.get(sk, 0):
                self.last[sk] = v
        self.done = n
        if barrier:
            for en, e in self.eng.items():
                sn = self.seen[en]
                for sk, v in self.last.items():
                    if sn.get(sk, 0) < v:
                        e.wait_ge(self._sem(sk), v)
                        sn[sk] = v
            for b in self.bufs:
                b.w = None
                b.r = []

    def emit(self):
        self.flush(False)
        for sk, v in self.last.items():
            self.nc.sync.wait_ge(self._sem(sk), v)
        self.n_ops = len(self.ops)


_UID = [0]


def _u(name):
    _UID[0] += 1
    return "%s_%d" % (name, _UID[0])


def _sb(nc, name, shape, dt):
    return nc.sbuf_tensor(_u(name), list(shape), dt).__enter__()


def _ps(nc, name, shape, dt=F32):
    return nc.psum_tensor(_u(name), list(shape), dt).__enter__()


def _din(nc, name, shape, dt):
    return nc.dram_tensor(name, list(shape), dt, kind="ExternalInput").ap()


def _dout(nc, name, shape, dt):
    return nc.dram_tensor(name, list(shape), dt, kind="ExternalOutput").ap()


def _dint(nc, name, shape, dt):
    return nc.dram_tensor(_u(name), list(shape), dt, kind="Internal").ap()


ADA_COLS = 3 * D // 8


def build_ada(depth=DEPTH):
    nc = bass.Bass("TRN2", target_bir_lowering=False)
    cT = _din(nc, "cT", [128, 32, 3], F32)
    adaw = _din(nc, "adaw", [depth, 32, 128, ADA_COLS], F32)
    adab = _din(nc, "adab", [depth, 3, ADA_COLS], F32)
    mod = _dout(nc, "mod", [depth, 3, ADA_COLS], F32)
    S = Sched(nc)
    ct = _sb(nc, "ct", [128, 32, 3], F32)
    st = _sb(nc, "st", [128, 32, 3], F32)
    NW = 3
    wt = [_sb(nc, "wt%d" % i, [128, ADA_COLS], F32) for i in range(NW)]
    bt = _sb(nc, "bt", [3, ADA_COLS], F32)
    rt = _sb(nc, "rt", [3, ADA_COLS], F32)
    pss = [_ps(nc, "ps%d" % i, [128, 512]) for i in range(3)]
    b_ct, b_st, b_bt, b_rt = Buf("ct"), Buf("st"), Buf("bt"), Buf("rt")
    b_wt = [Buf("wt%d" % i) for i in range(NW)]
    b_ps = [Buf("ps%d" % i) for i in range(3)]
    S.dma("sp", ct[:], cT[:, :, :], writes=[b_ct])
    S.op("act", lambda e: e.activation(out=st[:], in_=ct[:], func=AF.Silu), reads=[b_ct], writes=[b_st])
    it = 0
    for l in range(depth):
        S.dma("sp", bt[:], adab[l, :, :], writes=[b_bt])
        for kc in range(32):
            w = it % NW
            S.dma("sp", wt[w][:], adaw[l, kc, :, :], writes=[b_wt[w]])
            for j in range(3):
                S.op("pe", lambda e, w=w, j=j, kc=kc: e.matmul(
                    pss[j][0:3, :], lhsT=st[:, kc, :], rhs=wt[w][:, j * 512:(j + 1) * 512],
                    start=(kc == 0), stop=(kc == 31)), reads=[b_st, b_wt[w]], writes=[b_ps[j]])
            it += 1
        for j in range(3):
            S.op("dve", lambda e, j=j: e.tensor_tensor(
                out=rt[:, j * 512:(j + 1) * 512], in0=pss[j][0:3, :], in1=bt[:, j * 512:(j + 1) * 512], op=ALU.add),
                reads=[b_ps[j], b_bt], writes=[b_rt])
        S.dma("sp", mod[l, :, :], rt[:], reads=[b_rt])
    S.emit()
    return nc


PT = 256
NPT = NT // PT


def build_p(has_z, nt=NT):
    npt = nt // PT
    nc = bass.Bass("TRN2", target_bir_lowering=False)
    xT = _din(nc, "xT", [8, 128, nt], F32)
    gvec = _din(nc, "gvec", [128, 8], F32)
    scT = _din(nc, "scT", [128, 8, 2], F32)
    ones_in = _din(nc, "ones", [128, 128], F32)
    if has_z:
        zT = _din(nc, "zT", [32, 128, nt], BF16)
        wout = _din(nc, "wout", [32, 128, 1024], F32)
        gateT = _din(nc, "gateT", [128, 8, 2], F32)
        xoT = _dout(nc, "xoT", [8, 128, nt], F32)
    uT = _dout(nc, "uT", [8, 128, nt], BF16)
    ssq = _dout(nc, "ssq", [1, nt], F32)
    S = Sched(nc)
    gv = _sb(nc, "gv", [128, 8], F32)
    sc = _sb(nc, "sc", [128, 8, 2], F32)
    G = _sb(nc, "G", [128, 8, 2], F32)
    ones = _sb(nc, "ones_sb", [128, 128], F32)
    ssq_sb = _sb(nc, "ssq_sb", [1, nt], F32)
    xt = [_sb(nc, "xt%d" % i, [128, 8, PT], F32) for i in range(2)]
    ut = [_sb(nc, "ut%d" % i, [128, 8, PT], BF16) for i in range(2)]
    sq = [_sb(nc, "sq%d" % i, [128, PT], F32) for i in range(2)]
    b_gv, b_sc, b_G, b_ones, b_ssq = Buf("gv"), Buf("sc"), Buf("G"), Buf("ones"), Buf("ssq")
    b_xt = [Buf("xt0"), Buf("xt1")]
    b_ut = [Buf("ut0"), Buf("ut1")]
    b_sq = [Buf("sq0"), Buf("sq1")]
    pss = _ps(nc, "pss", [128, 512])
    b_pss = Buf("pss")
    if has_z:
        wbf = _sb(nc, "wbf", [128, 32, 1024], BF16)
        gt = _sb(nc, "gt", [128, 8, 2], F32)
        zt = [_sb(nc, "zt%d" % i, [128, 32, PT], BF16) for i in range(2)]
        xn = [_sb(nc, "xn%d" % i, [128, 8, PT], F32) for i in range(2)]
        b_wbf = [Buf("wbf%d" % k) for k in range(32)]
        b_gt = Buf("gt")
        b_zt = [Buf("zt0"), Buf("zt1")]
        b_xn = [Buf("xn0"), Buf("xn1")]
        psm = [_ps(nc, "psm%d" % i, [128, 512]) for i in range(2)]
        b_psm = [Buf("psm0"), Buf("psm1")]
    S.dma("sp", gv[:], gvec[:, :], writes=[b_gv])
    S.dma("sp", sc[:], scT[:, :, :], writes=[b_sc])
    S.dma("sp", ones[:], ones_in[:, :], writes=[b_ones])
    if has_z:
        S.dma("sp", gt[:], gateT[:, :, :], writes=[b_gt])
        for kc in range(32):
            S.dma("pool", wbf[:, kc, :], wout[kc, :, :], writes=[b_wbf[kc]])
    S.op("dve", lambda e: e.tensor_scalar(out=G[:], in0=sc[:], scalar1=1.0, scalar2=None, op0=ALU.add),
         reads=[b_sc], writes=[b_G])
    for s in range(2):
        S.op("dve", lambda e, s=s: e.tensor_tensor(out=G[:, :, s], in0=G[:, :, s], in1=gv[:], op=ALU.mult),
             reads=[b_G, b_gv], writes=[b_G])
    for tt in range(npt):
        c0 = tt * PT
        seg = 0 if tt == 0 else 1
        i = tt % 2
        S.dma("sp", xt[i][:], xT[:, :, c0:c0 + PT].rearrange("f p t -> p f t"), writes=[b_xt[i]])
        if has_z:
            S.dma("sp", zt[i][:], zT[:, :, c0:c0 + PT].rearrange("k p t -> p k t"), writes=[b_zt[i]])
        for fc in range(8):
            if has_z:
                pm = psm[fc % 2]
                for kc in range(32):
                    S.op("pe", lambda e, pm=pm, kc=kc, fc=fc, i=i: e.matmul(
                        pm[:, 0:PT], lhsT=wbf[:, kc, fc * 128:(fc + 1) * 128], rhs=zt[i][:, kc, :],
                        start=(kc == 0), stop=(kc == 31)),
                        reads=[b_wbf[kc], b_zt[i]], writes=[b_psm[fc % 2]])
                S.op("dve", lambda e, pm=pm, fc=fc, i=i, seg=seg: e.scalar_tensor_tensor(
                    out=xn[i][:, fc, :], in0=pm[:, 0:PT], scalar=gt[:, fc, seg:seg + 1], in1=xt[i][:, fc, :],
                    op0=ALU.mult, op1=ALU.add),
                    reads=[b_psm[fc % 2], b_gt, b_xt[i]], writes=[b_xn[i]])
                src, b_src = xn[i], b_xn[i]
            else:
                src, b_src = xt[i], b_xt[i]
            q = (tt * 8 + fc) % 2
            S.op("act", lambda e, src=src, fc=fc, q=q: e.activation(out=sq[q][:], in_=src[:, fc, :], func=AF.Square),
                 reads=[b_src], writes=[b_sq[q]])
            S.op("pe", lambda e, q=q, fc=fc: e.matmul(pss[0:1, 0:PT], lhsT=ones[:, 0:1], rhs=sq[q][:],
                                                      start=(fc == 0), stop=(fc == 7)),
                 reads=[b_ones, b_sq[q]], writes=[b_pss])
            S.op("dve", lambda e, src=src, fc=fc, i=i, seg=seg: e.tensor_scalar(
                out=ut[i][:, fc, :], in0=src[:, fc, :], scalar1=G[:, fc, seg:seg + 1], scalar2=None, op0=ALU.mult),
                reads=[b_src, b_G], writes=[b_ut[i]])
        S.op("dve", lambda e, c0=c0: e.tensor_copy(out=ssq_sb[:, c0:c0 + PT], in_=pss[0:1, 0:PT]),
             reads=[b_pss], writes=[b_ssq])
        if has_z:
            S.dma("sp", xoT[:, :, c0:c0 + PT].rearrange("f p t -> p f t"), xn[i][:], reads=[b_xn[i]])
        S.dma("sp", uT[:, :, c0:c0 + PT].rearrange("f p t -> p f t"), ut[i][:], reads=[b_ut[i]])
    S.dma("sp", ssq[:, :], ssq_sb[:], reads=[b_ssq])
    S.emit()
    return nc


def build_f(nl=SEQ):
    nc = bass.Bass("TRN2", target_bir_lowering=False)
    xT = _din(nc, "xT", [8, 128, nl], F32)
    ssq4 = _din(nc, "ssq4", [4, nl], F32)
    fg = _din(nc, "fg", [128, 8], F32)
    ones_in = _din(nc, "ones", [128, 128], F32)
    oT = _dout(nc, "oT", [8, 128, nl], F32)
    S = Sched(nc)
    TT = 512
    ones = _sb(nc, "ones_sb", [128, 128], F32)
    g = _sb(nc, "g", [128, 8], F32)
    s4 = _sb(nc, "s4", [4, nl], F32)
    rstd = _sb(nc, "rstd", [128, nl], F32)
    xt = [_sb(nc, "xt%d" % i, [128, 8, TT], F32) for i in range(2)]
    ot = [_sb(nc, "ot%d" % i, [128, 8, TT], F32) for i in range(2)]
    ps = [_ps(nc, "ps%d" % i, [128, 512]) for i in range(2)]
    b_ones, b_g, b_s4, b_rstd = Buf("ones"), Buf("g"), Buf("s4"), Buf("rstd")
    b_xt, b_ot, b_ps = [Buf("x0"), Buf("x1")], [Buf("o0"), Buf("o1")], [Buf("p0"), Buf("p1")]
    S.dma("sp", ones[:], ones_in[:, :], writes=[b_ones])
    S.dma("sp", g[:], fg[:, :], writes=[b_g])
    S.dma("sp", s4[:], ssq4[:, :], writes=[b_s4])
    epsb = _sb(nc, "epsb", [128, 1], F32)
    b_eps = Buf("eps")
    S.op("dve", lambda e: e.memset(epsb[:], EPS), writes=[b_eps])
    for tt in range(nl // TT):
        c0 = tt * TT
        i = tt % 2
        S.op("pe", lambda e, i=i, c0=c0: e.matmul(ps[i][:, :], lhsT=ones[0:4, :], rhs=s4[:, c0:c0 + TT],
                                                 start=True, stop=True), reads=[b_ones, b_s4], writes=[b_ps[i]])
        S.op("act", lambda e, i=i, c0=c0: e.activation(out=rstd[:, c0:c0 + TT], in_=ps[i][:, :], func=AF.Sqrt,
                                                     scale=1.0 / D, bias=epsb[:, 0:1]),
             reads=[b_ps[i], b_eps], writes=[b_rstd])
        S.op("dve", lambda e, c0=c0: e.reciprocal(out=rstd[:, c0:c0 + TT], in_=rstd[:, c0:c0 + TT]),
             reads=[b_rstd], writes=[b_rstd])
        S.dma("sp", xt[i][:], xT[:, :, c0:c0 + TT].rearrange("f p t -> p f t"), writes=[b_xt[i]])
        for fc in range(8):
            S.op("dve", lambda e, i=i, fc=fc, c0=c0: e.scalar_tensor_tensor(
                out=ot[i][:, fc, :], in0=xt[i][:, fc, :], scalar=g[:, fc:fc + 1], in1=rstd[:, c0:c0 + TT],
                op0=ALU.mult, op1=ALU.mult), reads=[b_xt[i], b_g, b_rstd], writes=[b_ot[i]])
        S.dma("sp", oT[:, :, c0:c0 + TT].rearrange("f p t -> p f t"), ot[i][:], reads=[b_ot[i]])
    S.emit()
    return nc


NA_QS = 128.0 ** -0.5
GLA_QS = 256.0 ** -0.5


def na_pairs(rows):
    kh = min(8, rows)
    out = []
    for m in range(rows // 2):
        prs = set()
        for r in (2 * m, 2 * m + 1):
            rs = min(max(r - kh // 2, 0), rows - kh)
            for y in range(rs, rs + kh):
                prs.add(y // 2)
        out.append(sorted(prs))
    return out


def a_cols(g):
    r = np.arange
    naq, nak, nav, nag = 0, 2048, 4096, 6144
    gq, gk, gv, gg, gf, gb = 8192, 9216, 10240, 12288, 14336, 14352
    p0 = np.concatenate([naq + 512 * g + r(512), nak + 512 * g + r(512)])
    p1 = np.concatenate([nag + 512 * g + r(512), gq + 256 * g + r(256), gk + 256 * g + r(256)])
    p2 = np.concatenate([gg + 512 * g + r(512), gf + r(16), gb + r(16)])
    p3 = np.concatenate([nav + 512 * g + r(512), gv + 512 * g + r(512)])
    return [p0, p1, p2, p3]


def build_a(rows=64, nctx=CTX, dbg=False, stop=9, env=None):
    if env is not None:
        return _build_a(rows, nctx, stop, env)
    global _dint
    _dint_saved = _dint
    if dbg:
        _dint = _dout
    try:
        return _build_a(rows, nctx, stop)
    finally:
        _dint = _dint_saved


def _build_a(rows, nctx, stop=9, env=None):
    nl = rows * GRID_W
    nt = nctx + nl
    nch = nt // 64
    nck = nctx // 128
    ntile128 = nt // 128
    pairs = na_pairs(rows)
    npair = rows // 2
    nbias = 25
    if env is None:
        nc = bass.Bass("TRN2", target_bir_lowering=False)
        uT = _din(nc, "uT", [32, 128, nt], BF16)
        ssq4 = _din(nc, "ssq4", [4, nt], F32)
        shT = _din(nc, "shT", [128, 32, 2], F32)
        wp = [_din(nc, "wp0", [32, 128, 1024], F32), _din(nc, "wp1", [32, 128, 1024], F32),
              _din(nc, "wp2", [32, 128, 544], F32), _din(nc, "wp3", [32, 128, 1024], F32)]
        biasT = _din(nc, "biasT", [128, 4 * nbias, 128], F32)
        wd = _din(nc, "wd", [17, 2, 256], F32)
        gng = _din(nc, "gng", [128, 4], F32)
        ropeR = _din(nc, "ropeR", [128, 2, rows], F32)
        ropeC = _din(nc, "ropeC", [128, 2, 64], F32)
        cf = _din(nc, "cf", [128, 5, 128], F32)
        cb = _din(nc, "cb", [128, 4, 128], BF16)
        zT = _dout(nc, "zT", [8, 128, nt], BF16)
        naS = _dint(nc, "naS", [12, 128, nt], BF16)
        glS = _dint(nc, "glS", [8, 128, nt], F32)
        lrS = _dint(nc, "lrS", [32, nt], F32)
        vS = _dint(nc, "vS", [nt, 1024], BF16)
        gpS = _dint(nc, "gpS", [2, 3, 128, 2, nt], BF16)
        oS = _dint(nc, "oS", [2, 128, 4, nt], F32)
        S = Sched(nc)
        B = S.buf
        banks = [_ps(nc, "bank%d" % i, [128, 512]) for i in range(8)]
        b_bank = [B("bank%d" % i) for i in range(8)]
    else:
        nc, S = env["nc"], env["S"]
        B = S.buf
        banks, b_bank = env["banks"], env["b_bank"]
        uT, ssq4, shT = env["uT"], env["ssq4"], None
        wp, biasT, wd, gng = env["wp"], env["biasT"], env["wd"], env["gng"]
        ropeR, ropeC, cf, cb = env["ropeR"], env["ropeC"], env["cf"], env["cb"]
        zT = env["zT"]
        naS, glS, lrS, vS, gpS, oS = (env[k_] for k_ in ("naS", "glS", "lrS", "vS", "gpS", "oS"))

    b_zrow = env["b_zrow"] if env is not None else [B("zrow%d" % i) for i in range(8)]
    ttiles = [(0, nctx, 0)] + [(nctx + 512 * i, 512, 1) for i in range(nl // 512)]

    if env is not None and "consts" in env:
        cft, cbt, epsb, onec, b_cf, b_cb, b_eps = env["consts"]
    else:
        cft = _sb(nc, "cft", [128, 5, 128], F32)
        cbt = _sb(nc, "cbt", [128, 4, 128], BF16)
        epsb = _sb(nc, "epsb", [128, 1], F32)
        onec = _sb(nc, "onec", [128, 1], F32)
        b_cf, b_cb, b_eps = B("cf"), B("cb"), B("eps")
        S.dma("sp", cft[:], cf[:, :, :], writes=[b_cf])
        S.dma("sp", cbt[:], cb[:, :, :], writes=[b_cb])
        S.op("dve", lambda e: e.memset(epsb[:], EPS), writes=[b_eps])
        S.op("dve", lambda e: e.memset(onec[:], 1.0), writes=[b_eps])
        if env is not None:
            env["consts"] = (cft, cbt, epsb, onec, b_cf, b_cb, b_eps)
    b_rstd, b_rcol = B("rstd"), B("rcol")
    ones_f = cft[:, 0, :]
    perm_f = cft[:, 1, :]
    tri_f = [cft[:, 2, :], cft[:, 3, :]]
    ident_f = cft[:, 4, :]
    ident_b = cbt[:, 0, :]
    ones_b = cbt[:, 1, :]
    mask_b = [cbt[0:64, 2, 0:64], cbt[0:64, 3, 0:64]]

    ph = []

    def alloc(name, shape, dt):
        t = nc.sbuf_tensor(_u(name), list(shape), dt)
        ph.append(t)
        return t.__enter__()

    def free_all():
        S.flush(barrier=True)
        while ph:
            ph.pop().__exit__(None, None, None)

    rstd = alloc("rstd", [128, nt], F32)
    rcol = alloc("rcol", [128, ntile128], F32)
    s4 = alloc("s4", [4, nt], F32)
    sh = alloc("sh", [128, 32, 2], F32)
    shb = alloc("shb", [128, 32, 2], BF16)
    shrep = alloc("shrep", [128, 32, 2, 128], BF16)
    wbf = alloc("wbf", [128, 32, 1024], BF16)
    ut = [alloc("ut%d" % i, [128, 32, 512], BF16) for i in range(2)]
    tmp = [alloc("tmp%d" % i, [128, 512], F32) for i in range(2)]
    ob = [alloc("ob%d" % i, [128, 512], BF16) for i in range(3)]
    of = [alloc("of%d" % i, [128, 512], F32) for i in range(3)]
    sbias = alloc("sbias", [128, 9, 2], F32)
    sbias2 = alloc("sbias2", [128, 9, 2], F32)
    srow = alloc("srow", [128, 2, 2, 512], F32)
    b_s4, b_sh, b_shb, b_shrep = B("s4"), B("sh"), B("shb"), B("shrep")
    b_wbf = [B("wbf%d" % k) for k in range(32)]
    b_ut = [B("ut0"), B("ut1")]
    b_tmp = [B("tmp0"), B("tmp1")]
    b_ob = [B("ob%d" % i) for i in range(3)]
    b_of = [B("of%d" % i) for i in range(3)]
    b_sbias, b_srow = B("sbias"), B("srow")

    S.dma("sp", s4[:], ssq4[:, :], writes=[b_s4])
    if env is None:
        S.dma("sp", sh[:], shT[:, :, :], writes=[b_sh])
        S.op("dve", lambda e: e.tensor_copy(out=shb[:], in_=sh[:]), reads=[b_sh], writes=[b_shb])
    else:
        env["load_shb"](shb, b_shb)
    S.op("dve", lambda e: e.tensor_copy(out=shrep[:], in_=shb[:].unsqueeze(3).broadcast_to([128, 32, 2, 128])),
         reads=[b_shb], writes=[b_shrep])
    for ti, (c0, n, seg) in enumerate(ttiles):
        bk = ti % 2
        S.op("pe", lambda e, bk=bk, c0=c0, n=n: e.matmul(banks[bk][:, 0:n], lhsT=ones_f[0:4, :], rhs=s4[:, c0:c0 + n],
                                                        start=True, stop=True),
             reads=[b_cf, b_s4], writes=[b_bank[bk]])
        S.op("act", lambda e, bk=bk, c0=c0, n=n: e.activation(out=rstd[:, c0:c0 + n], in_=banks[bk][:, 0:n], func=AF.Sqrt,
                                                            scale=1.0 / D, bias=epsb[:, 0:1]),
             reads=[b_bank[bk], b_eps], writes=[b_rstd])
        S.op("dve", lambda e, c0=c0, n=n: e.reciprocal(out=rstd[:, c0:c0 + n], in_=rstd[:, c0:c0 + n]),
             reads=[b_rstd], writes=[b_rstd])
    for t in range(ntile128):
        S.op("pe", lambda e, t=t: e.matmul(banks[2][:, t:t + 1], lhsT=s4[:, t * 128:(t + 1) * 128], rhs=ones_f[0:4, 0:1],
                                         start=True, stop=True), reads=[b_cf, b_s4], writes=[b_bank[2]])
    S.op("act", lambda e: e.activation(out=rcol[:], in_=banks[2][:, 0:ntile128], func=AF.Sqrt, scale=1.0 / D,
                                       bias=epsb[:, 0:1]), reads=[b_bank[2], b_eps], writes=[b_rcol])
    S.op("dve", lambda e: e.reciprocal(out=rcol[:], in_=rcol[:]), reads=[b_rcol], writes=[b_rcol])

    fm_pass = [
        [("id", NA_QS, naS, h, BF16) for h in range(4)] + [("id", 1.0, naS, 4 + h, BF16) for h in range(4)],
        [("silu", 1.0, naS, 8 + h, BF16) for h in range(4)] + [("id", GLA_QS, glS, 0, F32), ("id", GLA_QS, glS, 1, F32),
                                                               ("id", 1.0, glS, 2, F32), ("id", 1.0, glS, 3, F32)],
        [("silu", 1.0, glS, 4 + e_, F32) for e_ in range(4)] + [("lr", 1.0, lrS, 0, F32)],
    ]
    evn = 0
    for p in range(4):
        ncols = 544 if p == 2 else 1024
        for kc in range(32):
            S.dma("pool", wbf[:, kc, 0:ncols], wp[p][kc, :, :], reads=[], writes=[b_wbf[kc]])
        if p < 3:
            tiles = fm_pass[p]
            for ci, (kind, scl, dst, di, odt) in enumerate(tiles):
                m_ = 32 if kind == "lr" else 128
                for kc in range(32):
                    S.op("pe", lambda e, ci=ci, kc=kc, m_=m_: e.matmul(
                        banks[7][0:m_, ci * 2:ci * 2 + 2], lhsT=wbf[:, kc, ci * 128:ci * 128 + m_], rhs=shb[:, kc, :],
                        start=(kc == 0), stop=(kc == 31)), reads=[b_wbf[kc], b_shb], writes=[b_bank[7]])
            nci = len(tiles)
            S.op("dve", lambda e, nci=nci: e.tensor_copy(
                out=sbias[:, 0:nci, :], in_=banks[7][:, 0:2 * nci].rearrange("p (c s) -> p c s", s=2)),
                reads=[b_bank[7]], writes=[b_sbias])
            for ci, (kind, scl, dst, di, odt) in enumerate(tiles):
                S.op("dve", lambda e, ci=ci, scl=scl: e.tensor_scalar(
                    out=sbias2[:, ci, :], in0=sbias[:, ci, :], scalar1=float(scl), scalar2=None, op0=ALU.mult),
                    reads=[b_sbias], writes=[b_sbias])
            for ti, (c0, n, seg) in enumerate(ttiles):
                ui = ti % 2
                if env is None:
                    S.dma("sp", ut[ui][:, :, 0:n], uT[:, :, c0:c0 + n].rearrange("k p t -> p k t"), writes=[b_ut[ui]])
                else:
                    env["load_u"](ut[ui], c0, n, b_ut[ui])
                for ci, (kind, scl, dst, di, odt) in enumerate(tiles):
                    m_ = 32 if kind == "lr" else 128
                    bk = evn % 4
                    for kc in range(32):
                        S.op("pe", lambda e, bk=bk, ci=ci, kc=kc, m_=m_, ui=ui, n=n: e.matmul(
                            banks[bk][0:m_, 0:n], lhsT=wbf[:, kc, ci * 128:ci * 128 + m_], rhs=ut[ui][:, kc, 0:n],
                            start=(kc == 0), stop=(kc == 31)), reads=[b_wbf[kc], b_ut[ui]], writes=[b_bank[bk]])
                    ti2 = evn % 2
                    S.op("dve", lambda e, bk=bk, ti2=ti2, m_=m_, n=n, c0=c0: e.tensor_tensor(
                        out=tmp[ti2][0:m_, 0:n], in0=banks[bk][0:m_, 0:n], in1=rstd[0:m_, c0:c0 + n], op=ALU.mult),
                        reads=[b_bank[bk], b_rstd], writes=[b_tmp[ti2]])
                    oi = evn % 3
                    otile, b_ot = (ob[oi], b_ob[oi]) if odt == BF16 else (of[oi], b_of[oi])
                    func = AF.Silu if kind == "silu" else AF.Identity
                    S.op("act", lambda e, otile=otile, ti2=ti2, m_=m_, n=n, func=func, scl=scl, ci=ci, seg=seg: e.activation(
                        out=otile[0:m_, 0:n], in_=tmp[ti2][0:m_, 0:n], func=func, scale=float(scl),
                        bias=sbias2[0:m_, ci, seg:seg + 1]), reads=[b_tmp[ti2], b_sbias], writes=[b_ot])
                    if kind == "lr":
                        S.dma("sp", dst[:, c0:c0 + n], otile[0:32, 0:n], reads=[b_ot])
                    else:
                        S.dma("sp", dst[di, :, c0:c0 + n], otile[:, 0:n], reads=[b_ot])
                    evn += 1
        else:
            for seg in range(2):
                for hf in range(2):
                    bk = 4 + (seg * 2 + hf) % 2
                    for kc in range(32):
                        S.op("pe", lambda e, bk=bk, kc=kc, seg=seg, hf=hf: e.matmul(
                            banks[bk][:, :], lhsT=shrep[:, kc, seg, :], rhs=wbf[:, kc, hf * 512:(hf + 1) * 512],
                            start=(kc == 0), stop=(kc == 31)), reads=[b_wbf[kc], b_shrep], writes=[b_bank[bk]])
                    S.op("dve", lambda e, bk=bk, seg=seg, hf=hf: e.tensor_copy(out=srow[:, seg, hf, :], in_=banks[bk][:, :]),
                         reads=[b_bank[bk]], writes=[b_srow])
            for ti, (c0, n, seg) in enumerate(ttiles):
                ui = ti % 2
                if env is None:
                    S.dma("sp", ut[ui][:, :, 0:n], uT[:, :, c0:c0 + n].rearrange("k p t -> p k t"), writes=[b_ut[ui]])
                else:
                    env["load_u"](ut[ui], c0, n, b_ut[ui])
                for st in range(n // 128):
                    tg = (c0 + st * 128) // 128
                    for hf in range(2):
                        bk = evn % 4
                        for kc in range(32):
                            S.op("pe", lambda e, bk=bk, kc=kc, ui=ui, st=st, hf=hf: e.matmul(
                                banks[bk][:, :], lhsT=ut[ui][:, kc, st * 128:(st + 1) * 128],
                                rhs=wbf[:, kc, hf * 512:(hf + 1) * 512], start=(kc == 0), stop=(kc == 31)),
                                reads=[b_wbf[kc], b_ut[ui]], writes=[b_bank[bk]])
                        oi = evn % 3
                        S.op("dve", lambda e, bk=bk, oi=oi, tg=tg, seg=seg, hf=hf: e.scalar_tensor_tensor(
                            out=ob[oi][:, :], in0=banks[bk][:, :], scalar=rcol[:, tg:tg + 1], in1=srow[:, seg, hf, :],
                            op0=ALU.mult, op1=ALU.add), reads=[b_bank[bk], b_rcol, b_srow], writes=[b_ob[oi]])
                        S.dma("sp", vS[tg * 128:(tg + 1) * 128, hf * 512:(hf + 1) * 512], ob[oi][:, :], reads=[b_ob[oi]])
                        evn += 1
    free_all()
    if stop <= 1:
        S.emit()
        return nc, S, locals()

    def phase_na():
        NB = 25

        def cls_of(m):
            if m < 2:
                return m
            if m >= npair - 2:
                return 3 + (m - (npair - 2))
            return 2

        biasb = alloc("biasb", [128, 4 * NB, 128], BF16)
        b_biasb = B("biasb")
        for i in range(0, 4 * NB, 10):
            S.dma("pool", biasb[:, i:i + 10, :], biasT[:, i:i + 10, :], writes=[b_biasb])
        hk = [alloc("hk%d" % i, [128, nt], BF16) for i in range(2)]
        hq = [alloc("hq%d" % i, [128, nt], BF16) for i in range(2)]
        hg = [alloc("hg%d" % i, [128, nt], BF16) for i in range(2)]
        hv = [alloc("hv%d" % i, [128, ntile128, 128], BF16) for i in range(2)]
        pT = [alloc("pT%d" % i, [128, 1024], BF16) for i in range(2)]
        rec = [alloc("rec%d" % i, [128, 256], F32) for i in range(2)]
        t1 = [alloc("t1_%d" % i, [128, 256], F32) for i in range(2)]
        zst = [alloc("zst%d" % i, [128, 256], BF16) for i in range(2)]
        b_hk, b_hq, b_hg, b_hv = ([B("h%s%d" % (c_, i)) for i in range(2)] for c_ in "kqgv")
        b_pT, b_rec, b_t1, b_zst = ([B("%s%d" % (c_, i)) for i in range(2)] for c_ in ("pT", "rec", "t1", "zst"))
        un = 0
        for h in range(4):
            hi = h % 2
            S.dma("sp", hq[hi][:], naS[h, :, :], writes=[b_hq[hi]])
            S.dma("sp", hk[hi][:], naS[4 + h, :, :], writes=[b_hk[hi]])
            S.dma("sp", hg[hi][:], naS[8 + h, :, :], writes=[b_hg[hi]])
            S.dma("sp", hv[hi][:], vS[:, h * 128:(h + 1) * 128].rearrange("(c p) d -> p c d", p=128), writes=[b_hv[hi]])
            units = [(0, nctx, [(128 * c, None) for c in range(nck)])]
            for m in range(npair):
                ch = [(nctx + 128 * p_, h * NB + cls_of(m) * 5 + j) for j, p_ in enumerate(pairs[m])]
                ch += [(128 * c, None) for c in range(nck)]
                units.append((nctx + 128 * m, 128, ch))
            for (q0, N, ch) in units:
                u_ = un % 2
                per = 512 // N
                sb_ = [2 * u_, 2 * u_ + 1]
                bo, bs = 4 + 2 * u_, 5 + 2 * u_
                for i, (k0, bi) in enumerate(ch):
                    bk = sb_[i // per]
                    o0 = (i % per) * N
                    S.op("pe", lambda e, bk=bk, o0=o0, N=N, k0=k0, q0=q0, hi=hi, bi=bi: e.matmul(
                        banks[bk][:, o0:o0 + N], lhsT=hk[hi][:, k0:k0 + 128], rhs=hq[hi][:, q0:q0 + N],
                        start=True, stop=(bi is None)), reads=[b_hk[hi], b_hq[hi]], writes=[b_bank[bk]])
                    if bi is not None:
                        S.op("pe", lambda e, bk=bk, o0=o0, N=N, bi=bi: e.matmul(
                            banks[bk][:, o0:o0 + N], lhsT=ident_b, rhs=biasb[:, bi, :], start=False, stop=True),
                            reads=[b_cb, b_biasb], writes=[b_bank[bk]])
                ng = (len(ch) + per - 1) // per
                for gi in range(ng):
                    cnt = min(per, len(ch) - gi * per)
                    S.op("act", lambda e, gi=gi, cnt=cnt, N=N, u_=u_, bk=sb_[gi], per=per: e.activation(
                        out=pT[u_][:, gi * per * N:gi * per * N + cnt * N], in_=banks[bk][:, 0:cnt * N], func=AF.Exp),
                        reads=[b_bank[sb_[gi]]], writes=[b_pT[u_]])
                for i, (k0, bi) in enumerate(ch):
                    S.op("pe", lambda e, i=i, k0=k0, N=N, bo=bo, hi=hi, u_=u_, last=(i == len(ch) - 1): e.matmul(
                        banks[bo][:, 0:N], lhsT=hv[hi][:, k0 // 128, :], rhs=pT[u_][:, i * N:(i + 1) * N],
                        start=(i == 0), stop=last), reads=[b_hv[hi], b_pT[u_]], writes=[b_bank[bo]])
                for i, (k0, bi) in enumerate(ch):
                    S.op("pe", lambda e, i=i, N=N, bs=bs, u_=u_, last=(i == len(ch) - 1): e.matmul(
                        banks[bs][:, 0:N], lhsT=ones_b, rhs=pT[u_][:, i * N:(i + 1) * N],
                        start=(i == 0), stop=last), reads=[b_cb, b_pT[u_]], writes=[b_bank[bs]])
                S.op("dve", lambda e, N=N, bs=bs, u_=u_: e.reciprocal(out=rec[u_][:, 0:N], in_=banks[bs][:, 0:N]),
                     reads=[b_bank[bs]], writes=[b_rec[u_]])
                S.op("dve", lambda e, N=N, bo=bo, u_=u_: e.tensor_tensor(out=t1[u_][:, 0:N], in0=banks[bo][:, 0:N],
                                                                        in1=rec[u_][:, 0:N], op=ALU.mult),
                     reads=[b_bank[bo], b_rec[u_]], writes=[b_t1[u_]])
                S.op("dve", lambda e, N=N, u_=u_, hi=hi, q0=q0: e.tensor_tensor(out=zst[u_][:, 0:N], in0=t1[u_][:, 0:N],
                                                                            in1=hg[hi][:, q0:q0 + N], op=ALU.mult),
                     reads=[b_t1[u_], b_hg[hi]], writes=[b_zst[u_]])
                S.dma("sp", zT[h, :, q0:q0 + N], zst[u_][:, 0:N], reads=[b_zst[u_]], writes=[b_zrow[h]])
                un += 1
            if env is not None:
                env["xchg_rows"]([h])
        if env is not None:
            env["xchg_flush"]()
        free_all()


    def phase_gla():
        wdt = alloc("wdt", [17, 2, 256], F32)
        gngt = alloc("gngt", [128, 4], F32)
        rR = alloc("rR", [128, 2, rows], F32)
        rC = alloc("rC", [128, 2, 64], F32)
        dec = alloc("dec", [128, 2, 2, nch], F32)
        b_wdt, b_gng, b_rope, b_dec = B("wdt"), B("gng"), B("rope"), B("dec")
        S.dma("sp", wdt[:], wd[:, :, :], writes=[b_wdt])
        S.dma("sp", gngt[:], gng[:, :], writes=[b_gng])
        S.dma("sp", rR[:], ropeR[:, :, :], writes=[b_rope])
        S.dma("sp", rC[:], ropeC[:, :, :], writes=[b_rope])
        sub = []

        def salloc(name, shape, dt):
            t = nc.sbuf_tensor(_u(name), list(shape), dt)
            sub.append(t)
            return t.__enter__()

        def sfree():
            S.flush(barrier=True)
            while sub:
                sub.pop().__exit__(None, None, None)

        qk = [salloc("qk%d" % i, [128, 4, 512], F32) for i in range(2)]
        qr = [salloc("qr%d" % i, [128, 4, 512], F32) for i in range(2)]
        rt = salloc("rt", [128, 512], F32)
        lrt = [salloc("lrt%d" % i, [17, 512], F32) for i in range(2)]
        e1 = [salloc("e1_%d" % i, [128, 256], F32) for i in range(2)]
        Lt = [salloc("Lt%d" % i, [128, 256], F32) for i in range(2)]
        bl = [salloc("bl%d" % i, [128, 2, 8], F32) for i in range(2)]
        E = [salloc("E%d" % i, [128, 3, 2, 512], F32) for i in range(2)]
        gp = [salloc("gp%d" % i, [128, 3, 2, 512], BF16) for i in range(2)]
        b_qk, b_qr, b_lrt, b_e1, b_Lt, b_bl, b_E, b_gp = ([B("%s%d" % (c_, i)) for i in range(2)]
                                                          for c_ in ("qk", "qr", "lrt", "e1", "Lt", "bl", "E", "gp"))
        b_rt = B("rt")
        for dr in range(2):
            S.op("dve", lambda e, dr=dr: e.memset(lrt[dr][:], 1.0), writes=[b_lrt[dr]])
        it = 0
        for ti, (c0, n, seg) in enumerate(ttiles):
            i2 = ti % 2
            S.dma("sp", qk[i2][:, :, 0:n], glS[0:4, :, c0:c0 + n].rearrange("j p t -> p j t"), writes=[b_qk[i2]])
            if seg == 1:
                r0 = (c0 - nctx) // 64
                nr = n // 64
                for j in range(4):
                    hf = j % 2
                    if hf == 0:
                        cosap = rR[:, 0, r0:r0 + nr].unsqueeze(2).broadcast_to([128, nr, 64])
                        sinap = rR[:, 1, r0:r0 + nr].unsqueeze(2).broadcast_to([128, nr, 64])
                    else:
                        cosap = rC[:, 0, :].unsqueeze(1).broadcast_to([128, nr, 64])
                        sinap = rC[:, 1, :].unsqueeze(1).broadcast_to([128, nr, 64])
                    bk = j % 2
                    S.op("pe", lambda e, bk=bk, i2=i2, j=j, n=n: e.matmul(banks[bk][:, 0:n], lhsT=perm_f, rhs=qk[i2][:, j, 0:n],
                                                                      start=True, stop=True),
                         reads=[b_cf, b_qk[i2]], writes=[b_bank[bk]])
                    S.op("dve", lambda e, i2=i2, j=j, n=n, cosap=cosap: e.tensor_tensor(
                        out=qr[i2][:, j, 0:n].rearrange("p (r c) -> p r c", c=64),
                        in0=qk[i2][:, j, 0:n].rearrange("p (r c) -> p r c", c=64), in1=cosap, op=ALU.mult),
                        reads=[b_qk[i2], b_rope], writes=[b_qr[i2]])
                    S.op("dve", lambda e, bk=bk, n=n, sinap=sinap: e.tensor_tensor(
                        out=rt[:, 0:n].rearrange("p (r c) -> p r c", c=64),
                        in0=banks[bk][:, 0:n].rearrange("p (r c) -> p r c", c=64), in1=sinap, op=ALU.mult),
                        reads=[b_bank[bk], b_rope], writes=[b_rt])
                    S.op("dve", lambda e, i2=i2, j=j, n=n: e.tensor_tensor(out=qr[i2][:, j, 0:n], in0=qr[i2][:, j, 0:n],
                                                                       in1=rt[:, 0:n], op=ALU.add),
                         reads=[b_qr[i2], b_rt], writes=[b_qr[i2]])
                src, b_src = qr[i2], b_qr[i2]
            else:
                src, b_src = qk[i2], b_qk[i2]
            for dr in range(2):
                S.dma("sp", lrt[dr][0:16, 0:n], lrS[16 * dr:16 * dr + 16, c0:c0 + n], writes=[b_lrt[dr]])
                for st in range(n // 128):
                    bg = 2 + it % 2
                    ei = it % 2
                    S.op("pe", lambda e, bg=bg, dr=dr, st=st: e.matmul(
                        banks[bg][:, 0:256], lhsT=lrt[dr][0:17, st * 128:(st + 1) * 128], rhs=wdt[0:17, dr, :],
                        start=True, stop=True), reads=[b_lrt[dr], b_wdt], writes=[b_bank[bg]])
                    S.op("act", lambda e, bg=bg, ei=ei: e.activation(out=e1[ei][:], in_=banks[bg][:, 0:256], func=AF.Exp, scale=-1.0),
                         reads=[b_bank[bg]], writes=[b_e1[ei]])
                    S.op("act", lambda e, ei=ei: e.activation(out=Lt[ei][:], in_=e1[ei][:], func=AF.Ln, bias=onec[:, 0:1]),
                         reads=[b_e1[ei], b_eps], writes=[b_Lt[ei]])
                    for dh in range(2):
                        S.op("pe", lambda e, dh=dh, ei=ei, st=st, dr=dr: e.matmul(
                            banks[4 + dh][:, st * 128:(st + 1) * 128], lhsT=Lt[ei][:, dh * 128:(dh + 1) * 128], rhs=tri_f[dr],
                            start=True, stop=True), reads=[b_Lt[ei], b_cf], writes=[b_bank[4 + dh]])
                    it += 1
                di = (ti * 2 + dr) % 2
                ncb = n // 64
                cb0 = c0 // 64
                off = 63 if dr == 0 else 0
                for dh in range(2):
                    S.op("dve", lambda e, di=di, dh=dh, ncb=ncb, off=off, n=n: e.tensor_copy(
                        out=bl[di][:, dh, 0:ncb], in_=banks[4 + dh][:, 0:n].rearrange("p (c t) -> p c t", t=64)[:, :, off]),
                        reads=[b_bank[4 + dh]], writes=[b_bl[di]])
                S.op("act", lambda e, di=di, dr=dr, ncb=ncb, cb0=cb0: e.activation(
                    out=dec[:, dr, :, cb0:cb0 + ncb], in_=bl[di][:, :, 0:ncb], func=AF.Exp),
                    reads=[b_bl[di]], writes=[b_dec])
                for dh in range(2):
                    S.op("act", lambda e, di=di, dh=dh, n=n: e.activation(out=E[di][:, 0, dh, 0:n], in_=banks[4 + dh][:, 0:n],
                                                                       func=AF.Exp), reads=[b_bank[4 + dh]], writes=[b_E[di]])
                    S.op("act", lambda e, di=di, dh=dh, n=n: e.activation(out=E[di][:, 1, dh, 0:n], in_=banks[4 + dh][:, 0:n],
                                                                       func=AF.Exp, scale=-1.0),
                         reads=[b_bank[4 + dh]], writes=[b_E[di]])
                    for c in range(ncb):
                        S.op("act", lambda e, di=di, dh=dh, c=c: e.activation(
                            out=E[di][:, 2, dh, c * 64:(c + 1) * 64], in_=banks[4 + dh][:, c * 64:(c + 1) * 64], func=AF.Exp,
                            scale=-1.0, bias=bl[di][:, dh, c:c + 1]), reads=[b_bank[4 + dh], b_bl[di]], writes=[b_E[di]])
                for dh in range(2):
                    for j, sj in ((0, dh), (1, 2 + dh), (2, 2 + dh)):
                        S.op("dve", lambda e, di=di, dh=dh, j=j, sj=sj, n=n, src=src: e.tensor_tensor(
                            out=gp[di][:, j, dh, 0:n], in0=src[:, sj, 0:n], in1=E[di][:, j, dh, 0:n], op=ALU.mult),
                            reads=[b_src, b_E[di]], writes=[b_gp[di]])
                for j in range(3):
                    S.dma("sp", gpS[dr, j, :, :, c0:c0 + n], gp[di][:, j, :, 0:n], reads=[b_gp[di]])
        sfree()

        Sf = [salloc("Sf%d" % i, [128, 2, 512], F32) for i in range(2)]
        Sb = [[salloc("Sb%d_%d" % (d_, i), [128, 2, 512], BF16) for i in range(2)] for d_ in range(2)]
        gpb = [[salloc("gpb%d_%d" % (d_, i), [128, 3, 2, 512], BF16) for i in range(2)] for d_ in range(2)]
        vb = [[salloc("vb%d_%d" % (d_, i), [64, 8, 512], BF16) for i in range(2)] for d_ in range(2)]
        am = [[salloc("am%d_%d" % (d_, i), [64, 64], BF16) for i in range(2)] for d_ in range(2)]
        kdt = [[salloc("kdt%d_%d" % (d_, i), [64, 256], BF16) for i in range(2)] for d_ in range(2)]
        och = [[salloc("och%d_%d" % (d_, i), [128, 4, 64], F32) for i in range(2)] for d_ in range(2)]
        b_Sf = [B("Sf0"), B("Sf1")]
        b_Sb, b_gpb, b_vb, b_am, b_kdt, b_och = ([[B("%s%d_%d" % (c_, d_, i)) for i in range(2)] for d_ in range(2)]
                                                 for c_ in ("Sb", "gpb", "vb", "am", "kdt", "och"))
        b_A = [B("psA0"), B("psA1")]
        b_T = [B("psT0"), B("psT1")]
        nck64 = nctx // 64
        orders = [list(range(nch)), list(range(nck64 - 1, -1, -1)) + list(range(nch - 1, nck64 - 1, -1))]
        cur_blk, slot = [-1, -1], [1, 1]
        for dr in range(2):
            S.op("dve", lambda e, dr=dr: e.memset(Sf[dr][:], 0.0), reads=[], writes=[b_Sf[dr]])
            S.op("dve", lambda e, dr=dr: e.memset(Sb[dr][0][:], 0.0), reads=[], writes=[b_Sb[dr][0]])
        for step in range(nch):
            for dr in range(2):
                c = orders[dr][step]
                blk = c // 8
                if blk != cur_blk[dr]:
                    cur_blk[dr] = blk
                    slot[dr] ^= 1
                    sl = slot[dr]
                    t0b = blk * 512
                    nb = min(512, nt - t0b)
                    for j in range(3):
                        S.dma("sp", gpb[dr][sl][:, j, :, 0:nb], gpS[dr, j, :, :, t0b:t0b + nb], writes=[b_gpb[dr][sl]])
                    S.dma("sp", vb[dr][sl][:, 0:nb // 64, :],
                          vS[t0b:t0b + nb, 512:1024].rearrange("(c p) e -> p c e", p=64), writes=[b_vb[dr][sl]])
                sl = slot[dr]
                g_, v_, bg_, bv_ = gpb[dr][sl], vb[dr][sl], b_gpb[dr][sl], b_vb[dr][sl]
                ci = c % 8
                cs = ci * 64
                pi = step % 2
                bAT = banks[4 * dr]
                bO = banks[4 * dr + 1]
                bOb = b_bank[4 * dr + 1]
                am_, kdt_, och_ = am[dr][pi], kdt[dr][pi], och[dr][pi]
                bam_, bkdt_, boch_ = b_am[dr][pi], b_kdt[dr][pi], b_och[dr][pi]
                Sb_r, bSb_r = Sb[dr][pi], b_Sb[dr][pi]
                Sb_w, bSb_w = Sb[dr][1 - pi], b_Sb[dr][1 - pi]
                for dh in range(2):
                    S.op("pe", lambda e, dh=dh, g_=g_, cs=cs, bAT=bAT: e.matmul(
                        bAT[0:64, 0:64], lhsT=g_[:, 1, dh, cs:cs + 64], rhs=g_[:, 0, dh, cs:cs + 64],
                        start=(dh == 0), stop=(dh == 1)), reads=[bg_], writes=[b_A[dr]])
                S.op("dve", lambda e, am_=am_, bAT=bAT, dr=dr: e.tensor_tensor(out=am_[:], in0=bAT[0:64, 0:64], in1=mask_b[dr],
                                                                          op=ALU.mult), reads=[b_A[dr], b_cb], writes=[bam_])
                for dh in range(2):
                    S.op("pe", lambda e, dh=dh, g_=g_, cs=cs, bAT=bAT: e.matmul(
                        bAT[0:64, 128 + dh * 128:128 + (dh + 1) * 128], lhsT=g_[:, 2, dh, cs:cs + 64], rhs=ident_b,
                        start=True, stop=True), reads=[bg_, b_cb], writes=[b_T[dr]])
                S.op("act", lambda e, kdt_=kdt_, bAT=bAT: e.activation(out=kdt_[:], in_=bAT[0:64, 128:384], func=AF.Identity),
                     reads=[b_T[dr]], writes=[bkdt_])
                for ec in range(4):
                    S.op("pe", lambda e, ec=ec, bO=bO, v_=v_, ci=ci, am_=am_: e.matmul(
                        bO[:, ec * 64:(ec + 1) * 64], lhsT=v_[0:64, ci, ec * 128:(ec + 1) * 128], rhs=am_[:],
                        start=True, stop=False), reads=[bv_, bam_], writes=[bOb])
                    for dh in range(2):
                        S.op("pe", lambda e, ec=ec, bO=bO, dh=dh, Sb_r=Sb_r, g_=g_, cs=cs: e.matmul(
                            bO[:, ec * 64:(ec + 1) * 64], lhsT=Sb_r[:, dh, ec * 128:(ec + 1) * 128],
                            rhs=g_[:, 0, dh, cs:cs + 64], start=False, stop=(dh == 1)),
                            reads=[bSb_r, bg_], writes=[bOb])
                S.op("act", lambda e, och_=och_, bO=bO: e.activation(out=och_[:].rearrange("p a t -> p (a t)"),
                                                                  in_=bO[:, 0:256], func=AF.Identity),
                     reads=[bOb], writes=[boch_])
                S.dma("sp", oS[dr, :, :, c * 64:(c + 1) * 64], och_[:], reads=[boch_])
                for dh in range(2):
                    bu = 4 * dr + 2 + dh
                    S.op("pe", lambda e, bu=bu, dh=dh, kdt_=kdt_, v_=v_, ci=ci: e.matmul(
                        banks[bu][:, :], lhsT=kdt_[0:64, dh * 128:(dh + 1) * 128], rhs=v_[0:64, ci, :],
                        start=True, stop=True), reads=[bkdt_, bv_], writes=[b_bank[bu]])
                    S.op("dve", lambda e, bu=bu, dh=dh, dr=dr, c=c: e.scalar_tensor_tensor(
                        out=Sf[dr][:, dh, :], in0=Sf[dr][:, dh, :], scalar=dec[:, dr, dh, c:c + 1], in1=banks[bu][:, :],
                        op0=ALU.mult, op1=ALU.add), reads=[b_Sf[dr], b_dec, b_bank[bu]], writes=[b_Sf[dr]])
                S.op("act", lambda e, Sb_w=Sb_w, dr=dr: e.activation(out=Sb_w[:], in_=Sf[dr][:], func=AF.Identity),
                     reads=[b_Sf[dr]], writes=[bSb_w])
        sfree()

        ofb = [salloc("ofb%d" % i, [128, 4, 512], F32) for i in range(2)]
        obb = [salloc("obb%d" % i, [128, 4, 512], F32) for i in range(2)]
        sg = [salloc("sg%d" % i, [128, 4, 512], F32) for i in range(2)]
        sq3 = [salloc("sq3_%d" % i, [128, 4, 512], F32) for i in range(2)]
        rr = [salloc("rr%d" % i, [128, 512], F32) for i in range(2)]
        zt3 = [salloc("zt3_%d" % i, [128, 4, 512], BF16) for i in range(2)]
        b_ofb, b_obb, b_sg, b_sq3, b_rr, b_zt3 = ([B("%s%d" % (c_, i)) for i in range(2)]
                                                  for c_ in ("ofb", "obb", "sg", "sq3", "rr", "zt3"))
        for ti, (c0, n, seg) in enumerate(ttiles):
            i2 = ti % 2
            S.dma("sp", ofb[i2][:, :, 0:n], oS[0, :, :, c0:c0 + n], writes=[b_ofb[i2]])
            S.dma("sp", obb[i2][:, :, 0:n], oS[1, :, :, c0:c0 + n], writes=[b_obb[i2]])
            S.dma("sp", sg[i2][:, :, 0:n], glS[4:8, :, c0:c0 + n].rearrange("j p t -> p j t"), writes=[b_sg[i2]])
            S.op("dve", lambda e, i2=i2, n=n: e.tensor_tensor(out=ofb[i2][:, :, 0:n], in0=ofb[i2][:, :, 0:n],
                                                            in1=obb[i2][:, :, 0:n], op=ALU.add),
                 reads=[b_ofb[i2], b_obb[i2]], writes=[b_ofb[i2]])
            S.op("act", lambda e, i2=i2, n=n: e.activation(out=sq3[i2][:, :, 0:n], in_=ofb[i2][:, :, 0:n], func=AF.Square),
                 reads=[b_ofb[i2]], writes=[b_sq3[i2]])
            bk = i2
            for ec in range(4):
                S.op("pe", lambda e, bk=bk, ec=ec, i2=i2, n=n: e.matmul(banks[bk][:, 0:n], lhsT=ones_f, rhs=sq3[i2][:, ec, 0:n],
                                                                     start=(ec == 0), stop=(ec == 3)),
                     reads=[b_cf, b_sq3[i2]], writes=[b_bank[bk]])
            S.op("act", lambda e, bk=bk, i2=i2, n=n: e.activation(out=rr[i2][:, 0:n], in_=banks[bk][:, 0:n], func=AF.Sqrt,
                                                                scale=1.0 / 512.0, bias=epsb[:, 0:1]),
                 reads=[b_bank[bk], b_eps], writes=[b_rr[i2]])
            S.op("dve", lambda e, i2=i2, n=n: e.reciprocal(out=rr[i2][:, 0:n], in_=rr[i2][:, 0:n]),
                 reads=[b_rr[i2]], writes=[b_rr[i2]])
            for ec in range(4):
                S.op("dve", lambda e, i2=i2, ec=ec, n=n: e.scalar_tensor_tensor(
                    out=sq3[i2][:, ec, 0:n], in0=ofb[i2][:, ec, 0:n], scalar=gngt[:, ec:ec + 1], in1=rr[i2][:, 0:n],
                    op0=ALU.mult, op1=ALU.mult), reads=[b_ofb[i2], b_gng, b_rr[i2], b_bank[bk]], writes=[b_sq3[i2]])
                S.op("dve", lambda e, i2=i2, ec=ec, n=n: e.tensor_tensor(out=zt3[i2][:, ec, 0:n], in0=sq3[i2][:, ec, 0:n],
                                                                      in1=sg[i2][:, ec, 0:n], op=ALU.mult),
                     reads=[b_sq3[i2], b_sg[i2]], writes=[b_zt3[i2]])
            S.dma("sp", zT[4:8, :, c0:c0 + n].rearrange("j p t -> p j t"), zt3[i2][:, :, 0:n], reads=[b_zt3[i2]],
                  writes=b_zrow[4:8])
        if env is not None:
            env["xchg_rows"]([4, 5, 6, 7])
        sfree()
        free_all()

    phase_gla()
    phase_na()
    if env is None:
        S.emit()
    return nc, S, locals()


def const_f32():
    c = np.zeros((128, 5, 128), np.float32)
    c[:, 0, :] = 1.0
    i = np.arange(128)
    c[i, 1, (i + 64) % 128] = 1.0
    c[:, 1, :] = c[:, 1, :].T
    s_, t_ = np.meshgrid(i, i, indexing="ij")
    same = (s_ // 64) == (t_ // 64)
    c[:, 2, :] = np.where(same & (s_ <= t_), -1.0 / 16.0, 0.0)
    c[:, 3, :] = np.where(same & (s_ >= t_), -1.0 / 16.0, 0.0)
    c[i, 4, i] = 1.0
    return c


def const_bf16():
    c = np.zeros((128, 4, 128), np.float32)
    i = np.arange(128)
    c[i, 0, i] = 1.0
    c[:, 1, :] = 1.0
    s_, t_ = np.meshgrid(np.arange(64), np.arange(64), indexing="ij")
    c[0:64, 2, 0:64] = (s_ <= t_)
    c[0:64, 3, 0:64] = (s_ >= t_)
    return c.astype(NPBF)


def rope_tables(rows):
    i = np.arange(64, dtype=np.float32)
    inv = (np.float32(10000.0) ** (-i / np.float32(64.0))).astype(np.float32)

    def tab(npos):
        ang = (np.arange(npos, dtype=np.float32)[None, :] * inv[:, None]).astype(np.float32)
        cos = np.concatenate([np.cos(ang), np.cos(ang)], 0)
        sin = np.concatenate([-np.sin(ang), np.sin(ang)], 0)
        return np.ascontiguousarray(np.stack([cos, sin], 1).astype(np.float32))

    return tab(rows), tab(64)


def na_bias_table(rpb_l, g, rows):
    kh = min(8, rows)
    pairs = na_pairs(rows)
    npair = rows // 2
    reps = [0, 1, 2, npair - 2, npair - 1]
    out = np.full((128, 4, 25, 128), NEG, np.float32)
    loc = np.arange(128)
    yl, xl = loc // 64, loc % 64
    for ci, m in enumerate(reps):
        for j, p in enumerate(pairs[m]):
            ky = (2 * p + yl)[:, None]
            kx = xl[:, None]
            qy = (2 * m + yl)[None, :]
            qx = xl[None, :]
            rs = np.clip(qy - kh // 2, 0, rows - kh)
            okr = (ky >= rs) & (ky < rs + kh)
            cst = np.clip(qx - 8, 0, 64 - 16)
            okc = (kx >= cst) & (kx < cst + 16)
            dy = np.clip(ky - qy + 7, 0, 14)
            dx = np.clip(kx - qx + 15, 0, 30)
            ok = okr & okc
            for h in range(4):
                vals = rpb_l[4 * g + h][dy, dx]
                out[:, h, ci * 5 + j, :] = np.where(ok, vals, np.float32(NEG))
    return np.ascontiguousarray(out.reshape(128, 100, 128))


def pmaj(a):
    n = a.shape[0] // 128
    return np.ascontiguousarray(a.reshape(n, 128, *a.shape[1:]).swapaxes(0, 1))


def a_weight_inputs(w_in_l, rpb_l, wdec_l, bdec_l, gng_l, g, rows):
    cols = a_cols(g)
    im = {}
    for p in range(4):
        im["wp%d" % p] = np.ascontiguousarray(w_in_l[:, cols[p]]).reshape(32, 128, -1)
    im["biasT"] = na_bias_table(rpb_l, g, rows)
    wd = np.empty((17, 2, 256), np.float32)
    wd[0:16] = wdec_l[:, :, 256 * g:256 * g + 256].transpose(1, 0, 2)
    wd[16] = bdec_l[:, 256 * g:256 * g + 256]
    im["wd"] = wd
    im["gng"] = np.ascontiguousarray(gng_l.reshape(4, 128).T)
    rR, rC = rope_tables(rows)
    im["ropeR"], im["ropeC"] = rR, rC
    im["cf"], im["cb"] = const_f32(), const_bf16()
    return im


_PROG = {}
_DBG = None


def _prog(key, fn):
    if key not in _PROG:
        _PROG[key] = fn()
    return _PROG[key]


def _run(nc, in_maps):
    res = run_bass_kernel_spmd(nc, in_maps, core_ids=list(range(len(in_maps))))
    return res.results


def kernel_impl(x, c, ctx, c_ctx, ada_w, ada_b, norm_g, w_in, na_rpb, gla_w_decay, gla_b_decay, gla_norm_g, w_out,
                final_norm_g):
    f32 = np.float32
    x, c, ctx, c_ctx = (np.asarray(a, f32) for a in (x, c, ctx, c_ctx))
    ada_w, ada_b, norm_g, w_in, na_rpb = (np.asarray(a, f32) for a in (ada_w, ada_b, norm_g, w_in, na_rpb))
    gla_w_decay, gla_b_decay, gla_norm_g, w_out, final_norm_g = (
        np.asarray(a, f32) for a in (gla_w_decay, gla_b_decay, gla_norm_g, w_out, final_norm_g))
    depth = ada_w.shape[0]
    nb, nl, _ = x.shape
    nctx = ctx.shape[1]
    rows = nl // GRID_W
    nt = nctx + nl
    assert nb == 2
    cores = [(b, g) for b in range(2) for g in range(4)]
    ones = np.ones((128, 128), f32)

    nc_ada = _prog(("ada", depth), lambda: build_ada(depth))
    cvec = np.stack([c[0], c[1], c_ctx], 0)
    cT = np.ascontiguousarray(cvec.T.reshape(32, 128, 3).transpose(1, 0, 2))
    ims = []
    for j in range(8):
        cs = slice(ADA_COLS * j, ADA_COLS * (j + 1))
        ims.append({"cT": cT, "adaw": np.ascontiguousarray(ada_w[:, :, cs]).reshape(depth, 32, 128, ADA_COLS),
                    "adab": np.ascontiguousarray(np.broadcast_to(ada_b[:, None, cs], (depth, 3, ADA_COLS)))})
    r = _run(nc_ada, ims)
    mod = np.concatenate([r[j]["mod"] for j in range(8)], axis=-1)
    if _DBG is not None:
        _DBG['mod'] = mod
    shift, scale, gate = mod[:, :, 0:D], mod[:, :, D:2 * D], mod[:, :, 2 * D:3 * D]

    def seg2(v, l, b, sl):
        return pmaj(np.stack([v[l, 2, sl], v[l, b, sl]], -1))

    xs = []
    for (b, g) in cores:
        fs = slice(1024 * g, 1024 * (g + 1))
        xb = np.concatenate([ctx[b][:, fs], x[b][:, fs]], 0)
        xs.append(np.ascontiguousarray(xb.T).reshape(8, 128, nt))

    def gather_u(res):
        uT = [np.concatenate([res[4 * b + g]["uT"] for g in range(4)], 0) for b in range(2)]
        ssq4 = [np.concatenate([res[4 * b + g]["ssq"] for g in range(4)], 0) for b in range(2)]
        return uT, ssq4

    nc_p0 = _prog(("p0", nt), lambda: build_p(False, nt))
    ims = []
    for ci, (b, g) in enumerate(cores):
        fs = slice(1024 * g, 1024 * (g + 1))
        ims.append({"xT": xs[ci], "gvec": pmaj(norm_g[0, fs]), "scT": seg2(scale, 0, b, fs), "ones": ones})
    uT, ssq4 = gather_u(_run(nc_p0, ims))
    if _DBG is not None:
        _DBG['u0'] = uT
        _DBG['ssq0'] = ssq4

    nc_a = _prog(("a", rows, nctx), lambda: build_a(rows, nctx)[0])
    nc_p = _prog(("p", nt), lambda: build_p(True, nt))
    for l in range(depth):
        ims = []
        wcache = {}
        for ci, (b, g) in enumerate(cores):
            if g not in wcache:
                wcache[g] = a_weight_inputs(w_in[l], na_rpb[l], gla_w_decay[l], gla_b_decay[l], gla_norm_g[l], g, rows)
            im = {"uT": uT[b], "ssq4": ssq4[b], "shT": seg2(shift, l, b, slice(0, D))}
            im.update(wcache[g])
            ims.append(im)
        res = _run(nc_a, ims)
        del wcache
        zT = []
        for b in range(2):
            zf = np.empty((32, 128, nt), NPBF)
            for g in range(4):
                zf[4 * g:4 * g + 4] = res[4 * b + g]["zT"][0:4]
                zf[16 + 4 * g:16 + 4 * g + 4] = res[4 * b + g]["zT"][4:8]
            zT.append(zf)
        if _DBG is not None:
            _DBG['z%d' % l] = zT
        ln = min(l + 1, depth - 1)
        ims = []
        for ci, (b, g) in enumerate(cores):
            fs = slice(1024 * g, 1024 * (g + 1))
            ims.append({"xT": xs[ci], "zT": zT[b], "wout": np.ascontiguousarray(w_out[l][:, fs]).reshape(32, 128, 1024),
                        "gateT": seg2(gate, l, b, fs), "gvec": pmaj(norm_g[ln, fs]), "scT": seg2(scale, ln, b, fs),
                        "ones": ones})
        res = _run(nc_p, ims)
        xs = [res[ci]["xoT"] for ci in range(8)]
        uT, ssq4 = gather_u(res)
        if _DBG is not None:
            _DBG['x%d' % (l + 1)] = xs
            _DBG['u%d' % (l + 1)] = uT
            _DBG['ssq%d' % (l + 1)] = ssq4

    nc_f = _prog(("f", nl), lambda: build_f(nl))
    ims = []
    for ci, (b, g) in enumerate(cores):
        fs = slice(1024 * g, 1024 * (g + 1))
        ims.append({"xT": np.ascontiguousarray(xs[ci][:, :, nctx:]), "ssq4": np.ascontiguousarray(ssq4[b][:, nctx:]),
                    "fg": pmaj(final_norm_g[fs]), "ones": ones})
    res = _run(nc_f, ims)
    out = np.empty((2, nl, D), f32)
    for ci, (b, g) in enumerate(cores):
        out[b][:, 1024 * g:1024 * (g + 1)] = res[ci]["oT"].reshape(1024, nl).T
    return out


def kernel(**inputs):
    return kernel_fused(**inputs)


XPAD = 16
GROUPS4 = [[0, 1, 2, 3], [4, 5, 6, 7]]


def build_fused(depth=DEPTH, rows=64, nctx=CTX):
    nl = rows * GRID_W
    nt = nctx + nl
    ntx = nt + XPAD
    npt = nt // PT
    nc = bass.Bass("TRN2", target_bir_lowering=False)
    S = Sched(nc)
    B = S.buf
    banks = [_ps(nc, "bank%d" % i, [128, 512]) for i in range(8)]
    b_bank = [B("bank%d" % i) for i in range(8)]
    xT = _din(nc, "xT", [8, 128, nt], F32)
    cT2 = _din(nc, "cT2", [128, 32, 2], F32)
    adaw = _din(nc, "adaw", [depth, 32, 128, 3072], F32)
    adab = _din(nc, "adab", [depth, 2, 3072], F32)
    gvec = _din(nc, "gvec", [128, depth, 8], F32)
    fg = _din(nc, "fg", [128, 8], F32)
    wout = _din(nc, "wout", [depth, 32, 128, 1024], F32)
    wp = [[_din(nc, "wp%d_%d" % (p, l), [32, 128, 544 if p == 2 else 1024], F32) for p in range(4)] for l in range(depth)]
    biasT = [_din(nc, "biasT_%d" % l, [128, 100, 128], F32) for l in range(depth)]
    wd = [_din(nc, "wd_%d" % l, [17, 2, 256], F32) for l in range(depth)]
    gng = [_din(nc, "gng_%d" % l, [128, 4], F32) for l in range(depth)]
    ropeR = _din(nc, "ropeR", [128, 2, rows], F32)
    ropeC = _din(nc, "ropeC", [128, 2, 64], F32)
    cf = _din(nc, "cf", [128, 5, 128], F32)
    cb = _din(nc, "cb", [128, 4, 128], BF16)
    oT = _dout(nc, "oT", [8, 128, nl], F32)
    xS = _dint(nc, "xS", [8, 128, nt], F32)
    uloc = _dint(nc, "uloc", [8, 128, ntx], BF16)
    ufull = _dint(nc, "ufull", [32, 128, ntx], BF16)
    sloc = _dint(nc, "sloc", [1, nt], F32)
    half = 64 * ntx
    cin = [nc.dram_tensor(_u("cin"), [32, half // 32], BF16) for _ in range(16)]
    cout = [nc.dram_tensor(_u("cout"), [128, half // 32], BF16) for _ in range(16)]
    sin_ = nc.dram_tensor(_u("sin"), [32, nt // 32], F32)
    sout = nc.dram_tensor(_u("sout"), [128, nt // 32], F32)
    b_cin = [B("cin%d" % j) for j in range(16)]
    b_cout = [B("cout%d" % j) for j in range(16)]
    b_sin, b_sout = B("sin"), B("sout")
    ssq4 = sout.ap().rearrange("(g a) b -> g (a b)", g=4)

    def barrier():
        S.flush(barrier=True)

    ncu = (npt + 1) // 2
    cinU = [nc.dram_tensor(_u("cinU"), [32, 16384], BF16) for _ in range(ncu)]
    coutU = [nc.dram_tensor(_u("coutU"), [128, 16384], BF16) for _ in range(ncu)]
    cinSh = nc.dram_tensor(_u("cinSh"), [32, 64], BF16)
    coutSh = nc.dram_tensor(_u("coutSh"), [128, 64], BF16)
    b_cinU = [B("cinU%d" % j) for j in range(ncu)]
    b_coutU = [B("coutU%d" % j) for j in range(ncu)]
    b_cinSh, b_coutSh = B("cinSh"), B("coutSh")
    b_zrow = [B("zrow%d" % i) for i in range(8)]
    b_ufull = B("ufull")

    def load_u(dst, c0, n, b_dst):
        for off in range(0, n, PT):
            tix = (c0 + off) // PT
            j, hf = tix // 2, tix % 2
            src = coutU[j].ap().rearrange("a (b t) -> (a b) t", t=2 * PT).rearrange("(k p) t -> p k t", p=128)
            S.dma("sp", dst[:, :, off:off + PT], src[:, :, hf * PT:(hf + 1) * PT], reads=[b_coutU[j]], writes=[b_dst])

    def load_shb(dst, b_dst):
        src = coutSh.ap().rearrange("a (b s) -> (a b) s", s=2).rearrange("(k p) s -> p k s", p=128)
        S.dma("sp", dst[:], src, reads=[b_coutSh], writes=[b_dst])

    def xchg_rows(rws):
        ufg = ufull.rearrange("(g i) p t -> g i p t", g=4)
        js = [(i, hf) for i in rws for hf in range(2)]
        for (i, hf) in js:
            j = 2 * i + hf
            S.dma("sp", cin[j].ap().rearrange("a (b t) -> (a b) t", b=2), uloc[i, 64 * hf:64 * hf + 64, :],
                  reads=[b_zrow[i]], writes=[b_cin[j]])
        for (i, hf) in js:
            j = 2 * i + hf
            S.cc(GROUPS4, cin[j], cout[j], reads=[b_cin[j]], writes=[b_cout[j]])
            pending.append((i, hf))

    pending = []

    def xchg_flush():
        ufg = ufull.rearrange("(g i) p t -> g i p t", g=4)
        for (i, hf) in pending:
            j = 2 * i + hf
            S.dma("sp", ufg[:, i, 64 * hf:64 * hf + 64, :].rearrange("g p t -> p g t"),
                  cout[j].ap().rearrange("(g a) (b t) -> (a b) g t", g=4, b=2), reads=[b_cout[j]], writes=[b_ufull])
        del pending[:]

    modT = _sb(nc, "modT", [128, depth, 24, 2], F32)
    gv = _sb(nc, "gv", [128, depth, 8], F32)
    b_mod, b_gv = B("modT"), B("gv")
    env = {"nc": nc, "S": S, "banks": banks, "b_bank": b_bank, "ropeR": ropeR, "ropeC": ropeC, "cf": cf, "cb": cb,
           "load_u": load_u, "load_shb": load_shb, "xchg_rows": xchg_rows, "xchg_flush": xchg_flush, "b_zrow": b_zrow}
    for k_, shp, dt_ in (("naS", [12, 128, nt], BF16), ("glS", [8, 128, nt], F32), ("lrS", [32, nt], F32),
                         ("vS", [nt, 1024], BF16), ("gpS", [2, 3, 128, 2, nt], BF16), ("oS", [2, 128, 4, nt], F32)):
        env[k_] = _dint(nc, k_, shp, dt_)
    S.dma("sp", gv[:], gvec[:, :, :], writes=[b_gv])

    ph = []

    def alloc(name, shape, dt):
        t = nc.sbuf_tensor(_u(name), list(shape), dt)
        ph.append(t)
        return t.__enter__()

    def free_all():
        S.flush(barrier=True)
        while ph:
            ph.pop().__exit__(None, None, None)

    ct = alloc("ct", [128, 32, 2], F32)
    st = alloc("st", [128, 32, 2], F32)
    NW = 3
    wt = [alloc("wt%d" % i, [128, 3072], F32) for i in range(NW)]
    bt = alloc("bt", [2, 3072], F32)
    rt = alloc("rt", [2, 3072], F32)
    idf = alloc("idf", [128, 128], F32)
    b_ct, b_st, b_bt, b_rt, b_idf = B("ct"), B("st"), B("bt"), B("rt"), B("idf")
    b_wt = [B("wt%d" % i) for i in range(NW)]
    S.dma("sp", ct[:], cT2[:, :, :], writes=[b_ct])
    S.dma("sp", idf[:], cf[:, 4, :], writes=[b_idf])
    S.op("act", lambda e: e.activation(out=st[:], in_=ct[:], func=AF.Silu), reads=[b_ct], writes=[b_st])
    it = 0
    for l in range(depth):
        S.dma("sp", bt[:], adab[l, :, :], writes=[b_bt])
        for kc in range(32):
            w = it % NW
            S.dma("sp", wt[w][:], adaw[l, kc, :, :], writes=[b_wt[w]])
            for j in range(6):
                S.op("pe", lambda e, w=w, j=j, kc=kc: e.matmul(
                    banks[j][0:2, :], lhsT=st[:, kc, :], rhs=wt[w][:, j * 512:(j + 1) * 512],
                    start=(kc == 0), stop=(kc == 31)), reads=[b_st, b_wt[w]], writes=[b_bank[j]])
            it += 1
        for j in range(6):
            S.op("dve", lambda e, j=j: e.tensor_tensor(
                out=rt[:, j * 512:(j + 1) * 512], in0=banks[j][0:2, :], in1=bt[:, j * 512:(j + 1) * 512], op=ALU.add),
                reads=[b_bank[j], b_bt], writes=[b_rt])
        for j in range(24):
            S.op("pe", lambda e, j=j: e.matmul(banks[6][:, 2 * j:2 * j + 2], lhsT=rt[0:2, j * 128:(j + 1) * 128],
                                             rhs=idf[0:2, 0:2], start=True, stop=True),
                 reads=[b_rt, b_idf], writes=[b_bank[6]])
        S.op("dve", lambda e, l=l: e.tensor_copy(out=modT[:, l, :, :], in_=banks[6][:, 0:48].rearrange("p (j s) -> p j s", s=2)),
             reads=[b_bank[6]], writes=[b_mod])
    free_all()

    def emit_p(l_next, l_prev, first):
        has_z = not first
        xin = xT if first else xS
        G = alloc("G", [128, 8, 2], F32)
        shb2 = alloc("shb2", [128, 8, 2], BF16)
        ones = alloc("ones_sb", [128, 128], F32)
        ssq_sb = alloc("ssq_sb", [1, nt], F32)
        xt = [alloc("xt%d" % i, [128, 8, PT], F32) for i in range(2)]
        ut = [alloc("ut%d" % i, [128, 8, PT], BF16) for i in range(2)]
        sq = [alloc("sq%d" % i, [128, PT], F32) for i in range(2)]
        b_G, b_ones, b_ssq, b_shb2 = B("G"), B("ones"), B("ssq"), B("shb2")
        b_xt, b_ut, b_sq = [B("xt0"), B("xt1")], [B("ut0"), B("ut1")], [B("sq0"), B("sq1")]
        pss, b_pss = banks[7], b_bank[7]
        if has_z:
            wbf = alloc("wbf", [128, 32, 1024], BF16)
            zt = [alloc("zt%d" % i, [128, 32, PT], BF16) for i in range(2)]
            xn = [alloc("xn%d" % i, [128, 8, PT], F32) for i in range(2)]
            b_wbf = [B("wbf%d" % k) for k in range(32)]
            b_zt, b_xn = [B("zt0"), B("zt1")], [B("xn0"), B("xn1")]
            for kc in range(32):
                S.dma("pool", wbf[:, kc, :], wout[l_prev, kc, :, :], writes=[b_wbf[kc]])
        S.dma("sp", ones[:], cf[:, 0, :], writes=[b_ones])
        S.op("dve", lambda e: e.tensor_scalar(out=G[:], in0=modT[:, l_next, 8:16, :], scalar1=1.0, scalar2=None, op0=ALU.add),
             reads=[b_mod], writes=[b_G])
        for s_ in range(2):
            S.op("dve", lambda e, s_=s_: e.tensor_tensor(out=G[:, :, s_], in0=G[:, :, s_], in1=gv[:, l_next, :], op=ALU.mult),
                 reads=[b_G, b_gv], writes=[b_G])
        S.op("dve", lambda e: e.tensor_copy(out=shb2[:], in_=modT[:, l_next, 0:8, :]), reads=[b_mod], writes=[b_shb2])
        S.dma("sp", cinSh.ap().rearrange("a (b s) -> (a b) s", s=2).rearrange("(f p) s -> p f s", p=128), shb2[:],
              reads=[b_shb2], writes=[b_cinSh])
        S.cc(GROUPS4, cinSh, coutSh, reads=[b_cinSh], writes=[b_coutSh])
        for tt in range(npt):
            c0 = tt * PT
            seg = 0 if tt == 0 else 1
            i = tt % 2
            S.dma("sp", xt[i][:], xin[:, :, c0:c0 + PT].rearrange("f p t -> p f t"), writes=[b_xt[i]])
            if has_z:
                S.dma("sp", zt[i][:], ufull[:, :, c0:c0 + PT].rearrange("k p t -> p k t"), reads=[b_ufull],
                      writes=[b_zt[i]])
            for fc in range(8):
                if has_z:
                    pb = fc % 2
                    for kc in range(32):
                        S.op("pe", lambda e, pb=pb, kc=kc, fc=fc, i=i: e.matmul(
                            banks[pb][:, 0:PT], lhsT=wbf[:, kc, fc * 128:(fc + 1) * 128], rhs=zt[i][:, kc, :],
                            start=(kc == 0), stop=(kc == 31)), reads=[b_wbf[kc], b_zt[i]], writes=[b_bank[pb]])
                    S.op("dve", lambda e, pb=pb, fc=fc, i=i, seg=seg: e.scalar_tensor_tensor(
                        out=xn[i][:, fc, :], in0=banks[pb][:, 0:PT], scalar=modT[:, l_prev, 16 + fc, seg:seg + 1],
                        in1=xt[i][:, fc, :], op0=ALU.mult, op1=ALU.add),
                        reads=[b_bank[pb], b_mod, b_xt[i]], writes=[b_xn[i]])
                    src, b_src = xn[i], b_xn[i]
                else:
                    src, b_src = xt[i], b_xt[i]
                q = (tt * 8 + fc) % 2
                S.op("act", lambda e, src=src, fc=fc, q=q: e.activation(out=sq[q][:], in_=src[:, fc, :], func=AF.Square),
                     reads=[b_src], writes=[b_sq[q]])
                S.op("pe", lambda e, q=q, fc=fc: e.matmul(pss[0:1, 0:PT], lhsT=ones[:, 0:1], rhs=sq[q][:],
                                                          start=(fc == 0), stop=(fc == 7)),
                     reads=[b_ones, b_sq[q]], writes=[b_pss])
                S.op("dve", lambda e, src=src, fc=fc, i=i, seg=seg: e.tensor_scalar(
                    out=ut[i][:, fc, :], in0=src[:, fc, :], scalar1=G[:, fc, seg:seg + 1], scalar2=None, op0=ALU.mult),
                    reads=[b_src, b_G], writes=[b_ut[i]])
            S.op("dve", lambda e, c0=c0: e.tensor_copy(out=ssq_sb[:, c0:c0 + PT], in_=pss[0:1, 0:PT]),
                 reads=[b_pss], writes=[b_ssq])
            S.dma("sp", xS[:, :, c0:c0 + PT].rearrange("f p t -> p f t"), src[:], reads=[b_src])
            ju, hu = tt // 2, tt % 2
            S.dma("sp", cinU[ju].ap().rearrange("a (b t) -> (a b) t", t=2 * PT).rearrange("(f p) t -> p f t", p=128)[
                :, :, hu * PT:(hu + 1) * PT], ut[i][:], reads=[b_ut[i]], writes=[b_cinU[ju]])
            if hu == 1 or tt == npt - 1:
                S.cc(GROUPS4, cinU[ju], coutU[ju], reads=[b_cinU[ju]], writes=[b_coutU[ju]])
        S.dma("sp", sin_.ap().rearrange("a b -> (a b)").rearrange("(o n) -> o n", o=1), ssq_sb[:], reads=[b_ssq],
              writes=[b_sin])
        S.cc(GROUPS4, sin_, sout, reads=[b_sin], writes=[b_sout])
        free_all()

    emit_p(0, None, True)
    for l in range(depth):
        env.update({"uT": None, "ssq4": ssq4, "wp": wp[l], "biasT": biasT[l], "wd": wd[l], "gng": gng[l],
                    "zT": uloc})
        _build_a(rows, nctx, 9, env)
        emit_p(min(l + 1, depth - 1), l, False)

    TT = 512
    ones = alloc("ones_f", [128, 128], F32)
    g = alloc("g_f", [128, 8], F32)
    s4 = alloc("s4_f", [4, nt], F32)
    rstd = alloc("rstd_f", [128, nl], F32)
    epsb = alloc("epsb_f", [128, 1], F32)
    xt = [alloc("xtf%d" % i, [128, 8, TT], F32) for i in range(2)]
    ot = [alloc("otf%d" % i, [128, 8, TT], F32) for i in range(2)]
    b_ones, b_g, b_s4, b_rstd, b_eps = B("ones"), B("g"), B("s4"), B("rstd"), B("epsf")
    b_xt, b_ot = [B("x0"), B("x1")], [B("o0"), B("o1")]
    S.dma("sp", ones[:], cf[:, 0, :], writes=[b_ones])
    S.dma("sp", g[:], fg[:, :], writes=[b_g])
    S.dma("sp", s4[:], ssq4, writes=[b_s4])
    S.op("dve", lambda e: e.memset(epsb[:], EPS), writes=[b_eps])
    for tt in range(nl // TT):
        c0 = tt * TT
        i = tt % 2
        S.op("pe", lambda e, i=i, c0=c0: e.matmul(banks[i][:, :], lhsT=ones[0:4, :], rhs=s4[:, nctx + c0:nctx + c0 + TT],
                                                 start=True, stop=True), reads=[b_ones, b_s4], writes=[b_bank[i]])
        S.op("act", lambda e, i=i, c0=c0: e.activation(out=rstd[:, c0:c0 + TT], in_=banks[i][:, :], func=AF.Sqrt,
                                                     scale=1.0 / D, bias=epsb[:, 0:1]),
             reads=[b_bank[i], b_eps], writes=[b_rstd])
        S.op("dve", lambda e, c0=c0: e.reciprocal(out=rstd[:, c0:c0 + TT], in_=rstd[:, c0:c0 + TT]),
             reads=[b_rstd], writes=[b_rstd])
        S.dma("sp", xt[i][:], xS[:, :, nctx + c0:nctx + c0 + TT].rearrange("f p t -> p f t"), writes=[b_xt[i]])
        for fc in range(8):
            S.op("dve", lambda e, i=i, fc=fc, c0=c0: e.scalar_tensor_tensor(
                out=ot[i][:, fc, :], in0=xt[i][:, fc, :], scalar=g[:, fc:fc + 1], in1=rstd[:, c0:c0 + TT],
                op0=ALU.mult, op1=ALU.mult), reads=[b_xt[i], b_g, b_rstd], writes=[b_ot[i]])
        S.dma("sp", oT[:, :, c0:c0 + TT].rearrange("f p t -> p f t"), ot[i][:], reads=[b_ot[i]])
    free_all()
    S.emit()
    return nc, S


def z_perm():
    perm = np.empty(32, np.int64)
    for g in range(4):
        for i in range(8):
            perm[8 * g + i] = (4 * g + i) if i < 4 else (16 + 4 * g + (i - 4))
    return perm


def kernel_fused(x, c, ctx, c_ctx, ada_w, ada_b, norm_g, w_in, na_rpb, gla_w_decay, gla_b_decay, gla_norm_g, w_out,
                 final_norm_g):
    f32 = np.float32
    x, c, ctx, c_ctx = (np.asarray(a, f32) for a in (x, c, ctx, c_ctx))
    ada_w, ada_b, norm_g, w_in, na_rpb = (np.asarray(a, f32) for a in (ada_w, ada_b, norm_g, w_in, na_rpb))
    gla_w_decay, gla_b_decay, gla_norm_g, w_out, final_norm_g = (
        np.asarray(a, f32) for a in (gla_w_decay, gla_b_decay, gla_norm_g, w_out, final_norm_g))
    depth = ada_w.shape[0]
    nb, nl, _ = x.shape
    nctx = ctx.shape[1]
    rows = nl // GRID_W
    nt = nctx + nl
    nc = _prog(("fused", depth, rows, nctx), lambda: build_fused(depth, rows, nctx)[0])
    perm = z_perm()
    rR, rC = rope_tables(rows)
    cfa, cba = const_f32(), const_bf16()
    shared = {}
    ims = []
    for b in range(2):
        for g in range(4):
            fs = np.arange(1024 * g, 1024 * (g + 1))
            im = {}
            xb = np.concatenate([ctx[b][:, fs], x[b][:, fs]], 0)
            im["xT"] = np.ascontiguousarray(xb.T).reshape(8, 128, nt)
            cv = np.stack([c_ctx, c[b]], -1)
            im["cT2"] = pmaj(cv)
            if g not in shared:
                sh = {}
                acols = np.concatenate([fs, D + fs, 2 * D + fs])
                sh["adaw"] = np.ascontiguousarray(ada_w[:, :, acols]).reshape(depth, 32, 128, 3072)
                sh["adab"] = np.ascontiguousarray(np.broadcast_to(ada_b[:, None, acols], (depth, 2, 3072)))
                sh["gvec"] = np.ascontiguousarray(norm_g[:, fs].reshape(depth, 8, 128).transpose(2, 0, 1))
                sh["fg"] = pmaj(final_norm_g[fs])
                wo = w_out[:, :, fs].reshape(depth, 32, 128, 1024)
                sh["wout"] = np.ascontiguousarray(wo[:, perm])
                for l in range(depth):
                    aw = a_weight_inputs(w_in[l], na_rpb[l], gla_w_decay[l], gla_b_decay[l], gla_norm_g[l], g, rows)
                    for p in range(4):
                        sh["wp%d_%d" % (p, l)] = aw["wp%d" % p]
                    sh["biasT_%d" % l] = aw["biasT"]
                    sh["wd_%d" % l] = aw["wd"]
                    sh["gng_%d" % l] = aw["gng"]
                shared[g] = sh
            im.update(shared[g])
            im.update({"ropeR": rR, "ropeC": rC, "cf": cfa, "cb": cba})
            ims.append(im)
    res = _run(nc, ims)
    out = np.empty((2, nl, D), f32)
    for ci in range(8):
        b, g = ci // 4, ci % 4
        out[b][:, 1024 * g:1024 * (g + 1)] = res[ci]["oT"].reshape(1024, nl).T
    return out
```

```python
import numpy as np
import ml_dtypes
import concourse.bass as bass
import concourse.mybir as mybir
from concourse.bass_utils import run_bass_kernel_spmd

F32 = mybir.dt.float32
BF16 = mybir.dt.bfloat16
AF = mybir.ActivationFunctionType
ALU = mybir.AluOpType
NPBF = ml_dtypes.bfloat16

D = 4096
DEPTH = 4
SEQ = 4096
CTX = 256
NT = SEQ + CTX
GRID_W = 64
EPS = 1e-6
NEG = -1e30


class Buf:
    __slots__ = ("name", "w", "r")

    def __init__(self, name):
        self.name = name
        self.w = None
        self.r = []


class Sched:
    EPOCH = 20000

    def __init__(self, nc, n_dma_sems=40):
        self.nc = nc
        self.eng = {"pe": nc.tensor, "act": nc.scalar, "dve": nc.vector, "pool": nc.gpsimd, "sp": nc.sync}
        self.ops = []
        self.n_dma_sems = n_dma_sems
        self.done = 0
        self.ev = []
        self.deps = []
        self.cnt = {e: 0 for e in self.eng}
        self.n_dma = 0
        self.dma_prev = {}
        self.sems = {}
        self.seen = {e: {} for e in self.eng}
        self.last = {}
        self.bufs = []

    def buf(self, name):
        b = Buf(name)
        self.bufs.append(b)
        return b

    def op(self, eng, fn, reads=(), writes=()):
        self.ops.append((eng, fn, tuple(reads), tuple(writes), False))

    def dma(self, q, out, in_, reads=(), writes=()):
        self.ops.append((q, lambda e, o=out, i=in_: e.dma_start(out=o, in_=i), tuple(reads), tuple(writes), True))

    def cc(self, groups, cin, cout, reads=(), writes=()):
        def fn(e, cin=cin, cout=cout):
            return e.collective_compute("AllGather", ALU.bypass, replica_groups=groups,
                                        ins=[cin.ap().opt()], outs=[cout.ap().opt()])
        self.ops.append(("pool", fn, tuple(reads), tuple(writes), "cc"))

    def _sem(self, sk):
        if sk not in self.sems:
            self.sems[sk] = self.nc.semaphore("s_%s_%s" % (sk[0], sk[1])).__enter__()
        return self.sems[sk]

    def flush(self, barrier=False):
        ops = self.ops
        n = len(ops)
        for k in range(self.done, n):
            eng, fn, reads, writes, is_dma = ops[k]
            d = set()
            for b in reads:
                if b.w is not None:
                    d.add(b.w)
            for b in writes:
                if b.w is not None:
                    d.add(b.w)
                d.update(b.r)
            if is_dma == "cc":
                self.n_cc = getattr(self, "n_cc", 0) + 1
                evk = (("cc", 0), self.n_cc)
            elif is_dma:
                s = self.n_dma % self.n_dma_sems
                if s in self.dma_prev:
                    d.add(self.dma_prev[s])
                self.dma_prev[s] = k
                evk = (("dma", s), 16 * (self.n_dma // self.n_dma_sems + 1))
                self.n_dma += 1
            else:
                c = self.cnt[eng]
                evk = ((eng, c // self.EPOCH), c % self.EPOCH + 1)
                self.cnt[eng] = c + 1
            self.ev.append(evk)
            d.discard(k)
            for b in reads:
                b.r.append(k)
            for b in writes:
                b.w = k
                b.r = []
            e = self.eng[eng]
            need = {}
            for dk in d:
                if (not ops[dk][4]) and ops[dk][0] == eng and eng == "pe":
                    continue
                sk, v = self.ev[dk]
                if v > need.get(sk, 0):
                    need[sk] = v
            sn = self.seen[eng]
            for sk, v in need.items():
                if sn.get(sk, 0) < v:
                    e.wait_ge(self._sem(sk), v)
                    sn[sk] = v
            ins = fn(e)
            sk, v = evk
            ins.then_inc(self._sem(sk), 16 if is_dma is True else 1)
            if v > self.last.get(sk, 0):
                self.last[sk] = v
        self.done = n
        if barrier:
            for en, e in self.eng.items():
                sn = self.seen[en]
                for sk, v in self.last.items():
                    if sn.get(sk, 0) < v:
                        e.wait_ge(self._sem(sk), v)
                        sn[sk] = v
            for b in self.bufs:
                b.w = None
                b.r = []

    def emit(self):
        self.flush(False)
        for sk, v in self.last.items():
            self.nc.sync.wait_ge(self._sem(sk), v)
        self.n_ops = len(self.ops)


_UID = [0]


def _u(name):
    _UID[0] += 1
    return "%s_%d" % (name, _UID[0])


def _sb(nc, name, shape, dt):
    return nc.sbuf_tensor(_u(name), list(shape), dt).__enter__()


def _ps(nc, name, shape, dt=F32):
    return nc.psum_tensor(_u(name), list(shape), dt).__enter__()


def _din(nc, name, shape, dt):
    return nc.dram_tensor(name, list(shape), dt, kind="ExternalInput").ap()


def _dout(nc, name, shape, dt):
    return nc.dram_tensor(name, list(shape), dt, kind="ExternalOutput").ap()


def _dint(nc, name, shape, dt):
    return nc.dram_tensor(_u(name), list(shape), dt, kind="Internal").ap()


ADA_COLS = 3 * D // 8


def build_ada(depth=DEPTH):
    nc = bass.Bass("TRN2", target_bir_lowering=False)
    cT = _din(nc, "cT", [128, 32, 3], F32)
    adaw = _din(nc, "adaw", [depth, 32, 128, ADA_COLS], F32)
    adab = _din(nc, "adab", [depth, 3, ADA_COLS], F32)
    mod = _dout(nc, "mod", [depth, 3, ADA_COLS], F32)
    S = Sched(nc)
    ct = _sb(nc, "ct", [128, 32, 3], F32)
    st = _sb(nc, "st", [128, 32, 3], F32)
    NW = 3
    wt = [_sb(nc, "wt%d" % i, [128, ADA_COLS], F32) for i in range(NW)]
    bt = _sb(nc, "bt", [3, ADA_COLS], F32)
    rt = _sb(nc, "rt", [3, ADA_COLS], F32)
    pss = [_ps(nc, "ps%d" % i, [128, 512]) for i in range(3)]
    b_ct, b_st, b_bt, b_rt = Buf("ct"), Buf("st"), Buf("bt"), Buf("rt")
    b_wt = [Buf("wt%d" % i) for i in range(NW)]
    b_ps = [Buf("ps%d" % i) for i in range(3)]
    S.dma("sp", ct[:], cT[:, :, :], writes=[b_ct])
    S.op("act", lambda e: e.activation(out=st[:], in_=ct[:], func=AF.Silu), reads=[b_ct], writes=[b_st])
    it = 0
    for l in range(depth):
        S.dma("sp", bt[:], adab[l, :, :], writes=[b_bt])
        for kc in range(32):
            w = it % NW
            S.dma("sp", wt[w][:], adaw[l, kc, :, :], writes=[b_wt[w]])
            for j in range(3):
                S.op("pe", lambda e, w=w, j=j, kc=kc: e.matmul(
                    pss[j][0:3, :], lhsT=st[:, kc, :], rhs=wt[w][:, j * 512:(j + 1) * 512],
                    start=(kc == 0), stop=(kc == 31)), reads=[b_st, b_wt[w]], writes=[b_ps[j]])
            it += 1
        for j in range(3):
            S.op("dve", lambda e, j=j: e.tensor_tensor(
                out=rt[:, j * 512:(j + 1) * 512], in0=pss[j][0:3, :], in1=bt[:, j * 512:(j + 1) * 512], op=ALU.add),
                reads=[b_ps[j], b_bt], writes=[b_rt])
        S.dma("sp", mod[l, :, :], rt[:], reads=[b_rt])
    S.emit()
    return nc


PT = 256
NPT = NT // PT


def build_p(has_z, nt=NT):
    npt = nt // PT
    nc = bass.Bass("TRN2", target_bir_lowering=False)
    xT = _din(nc, "xT", [8, 128, nt], F32)
    gvec = _din(nc, "gvec", [128, 8], F32)
    scT = _din(nc, "scT", [128, 8, 2], F32)
    ones_in = _din(nc, "ones", [128, 128], F32)
    if has_z:
        zT = _din(nc, "zT", [32, 128, nt], BF16)
        wout = _din(nc, "wout", [32, 128, 1024], F32)
        gateT = _din(nc, "gateT", [128, 8, 2], F32)
        xoT = _dout(nc, "xoT", [8, 128, nt], F32)
    uT = _dout(nc, "uT", [8, 128, nt], BF16)
    ssq = _dout(nc, "ssq", [1, nt], F32)
    S = Sched(nc)
    gv = _sb(nc, "gv", [128, 8], F32)
    sc = _sb(nc, "sc", [128, 8, 2], F32)
    G = _sb(nc, "G", [128, 8, 2], F32)
    ones = _sb(nc, "ones_sb", [128, 128], F32)
    ssq_sb = _sb(nc, "ssq_sb", [1, nt], F32)
    xt = [_sb(nc, "xt%d" % i, [128, 8, PT], F32) for i in range(2)]
    ut = [_sb(nc, "ut%d" % i, [128, 8, PT], BF16) for i in range(2)]
    sq = [_sb(nc, "sq%d" % i, [128, PT], F32) for i in range(2)]
    b_gv, b_sc, b_G, b_ones, b_ssq = Buf("gv"), Buf("sc"), Buf("G"), Buf("ones"), Buf("ssq")
    b_xt = [Buf("xt0"), Buf("xt1")]
    b_ut = [Buf("ut0"), Buf("ut1")]
    b_sq = [Buf("sq0"), Buf("sq1")]
    pss = _ps(nc, "pss", [128, 512])
    b_pss = Buf("pss")
    if has_z:
        wbf = _sb(nc, "wbf", [128, 32, 1024], BF16)
        gt = _sb(nc, "gt", [128, 8, 2], F32)
        zt = [_sb(nc, "zt%d" % i, [128, 32, PT], BF16) for i in range(2)]
        xn = [_sb(nc, "xn%d" % i, [128, 8, PT], F32) for i in range(2)]
        b_wbf = [Buf("wbf%d" % k) for k in range(32)]
        b_gt = Buf("gt")
        b_zt = [Buf("zt0"), Buf("zt1")]
        b_xn = [Buf("xn0"), Buf("xn1")]
        psm = [_ps(nc, "psm%d" % i, [128, 512]) for i in range(2)]
        b_psm = [Buf("psm0"), Buf("psm1")]
    S.dma("sp", gv[:], gvec[:, :], writes=[b_gv])
    S.dma("sp", sc[:], scT[:, :, :], writes=[b_sc])
    S.dma("sp", ones[:], ones_in[:, :], writes=[b_ones])
    if has_z:
        S.dma("sp", gt[:], gateT[:, :, :], writes=[b_gt])
        for kc in range(32):
            S.dma("pool", wbf[:, kc, :], wout[kc, :, :], writes=[b_wbf[kc]])
    S.op("dve", lambda e: e.tensor_scalar(out=G[:], in0=sc[:], scalar1=1.0, scalar2=None, op0=ALU.add),
         reads=[b_sc], writes=[b_G])
    for s in range(2):
        S.op("dve", lambda e, s=s: e.tensor_tensor(out=G[:, :, s], in0=G[:, :, s], in1=gv[:], op=ALU.mult),
             reads=[b_G, b_gv], writes=[b_G])
    for tt in range(npt):
        c0 = tt * PT
        seg = 0 if tt == 0 else 1
        i = tt % 2
        S.dma("sp", xt[i][:], xT[:, :, c0:c0 + PT].rearrange("f p t -> p f t"), writes=[b_xt[i]])
        if has_z:
            S.dma("sp", zt[i][:], zT[:, :, c0:c0 + PT].rearrange("k p t -> p k t"), writes=[b_zt[i]])
        for fc in range(8):
            if has_z:
                pm = psm[fc % 2]
                for kc in range(32):
                    S.op("pe", lambda e, pm=pm, kc=kc, fc=fc, i=i: e.matmul(
                        pm[:, 0:PT], lhsT=wbf[:, kc, fc * 128:(fc + 1) * 128], rhs=zt[i][:, kc, :],
                        start=(kc == 0), stop=(kc == 31)),
                        reads=[b_wbf[kc], b_zt[i]], writes=[b_psm[fc % 2]])
                S.op("dve", lambda e, pm=pm, fc=fc, i=i, seg=seg: e.scalar_tensor_tensor(
                    out=xn[i][:, fc, :], in0=pm[:, 0:PT], scalar=gt[:, fc, seg:seg + 1], in1=xt[i][:, fc, :],
                    op0=ALU.mult, op1=ALU.add),
                    reads=[b_psm[fc % 2], b_gt, b_xt[i]], writes=[b_xn[i]])
                src, b_src = xn[i], b_xn[i]
            else:
                src, b_src = xt[i], b_xt[i]
            q = (tt * 8 + fc) % 2
            S.op("act", lambda e, src=src, fc=fc, q=q: e.activation(out=sq[q][:], in_=src[:, fc, :], func=AF.Square),
                 reads=[b_src], writes=[b_sq[q]])
            S.op("pe", lambda e, q=q, fc=fc: e.matmul(pss[0:1, 0:PT], lhsT=ones[:, 0:1], rhs=sq[q][:],
                                                      start=(fc == 0), stop=(fc == 7)),
                 reads=[b_ones, b_sq[q]], writes=[b_pss])
            S.op("dve", lambda e, src=src, fc=fc, i=i, seg=seg: e.tensor_scalar(
                out=ut[i][:, fc, :], in0=src[:, fc, :], scalar1=G[:, fc, seg:seg + 1], scalar2=None, op0=ALU.mult),
                reads=[b_src, b_G], writes=[b_ut[i]])
        S.op("dve", lambda e, c0=c0: e.tensor_copy(out=ssq_sb[:, c0:c0 + PT], in_=pss[0:1, 0:PT]),
             reads=[b_pss], writes=[b_ssq])
        if has_z:
            S.dma("sp", xoT[:, :, c0:c0 + PT].rearrange("f p t -> p f t"), xn[i][:], reads=[b_xn[i]])
        S.dma("sp", uT[:, :, c0:c0 + PT].rearrange("f p t -> p f t"), ut[i][:], reads=[b_ut[i]])
    S.dma("sp", ssq[:, :], ssq_sb[:], reads=[b_ssq])
    S.emit()
    return nc


def build_f(nl=SEQ):
    nc = bass.Bass("TRN2", target_bir_lowering=False)
    xT = _din(nc, "xT", [8, 128, nl], F32)
    ssq4 = _din(nc, "ssq4", [4, nl], F32)
    fg = _din(nc, "fg", [128, 8], F32)
    ones_in = _din(nc, "ones", [128, 128], F32)
    oT = _dout(nc, "oT", [8, 128, nl], F32)
    S = Sched(nc)
    TT = 512
    ones = _sb(nc, "ones_sb", [128, 128], F32)
    g = _sb(nc, "g", [128, 8], F32)
    s4 = _sb(nc, "s4", [4, nl], F32)
    rstd = _sb(nc, "rstd", [128, nl], F32)
    xt = [_sb(nc, "xt%d" % i, [128, 8, TT], F32) for i in range(2)]
    ot = [_sb(nc, "ot%d" % i, [128, 8, TT], F32) for i in range(2)]
    ps = [_ps(nc, "ps%d" % i, [128, 512]) for i in range(2)]
    b_ones, b_g, b_s4, b_rstd = Buf("ones"), Buf("g"), Buf("s4"), Buf("rstd")
    b_xt, b_ot, b_ps = [Buf("x0"), Buf("x1")], [Buf("o0"), Buf("o1")], [Buf("p0"), Buf("p1")]
    S.dma("sp", ones[:], ones_in[:, :], writes=[b_ones])
    S.dma("sp", g[:], fg[:, :], writes=[b_g])
    S.dma("sp", s4[:], ssq4[:, :], writes=[b_s4])
    epsb = _sb(nc, "epsb", [128, 1], F32)
    b_eps = Buf("eps")
    S.op("dve", lambda e: e.memset(epsb[:], EPS), writes=[b_eps])
    for tt in range(nl // TT):
        c0 = tt * TT
        i = tt % 2
        S.op("pe", lambda e, i=i, c0=c0: e.matmul(ps[i][:, :], lhsT=ones[0:4, :], rhs=s4[:, c0:c0 + TT],
                                                 start=True, stop=True), reads=[b_ones, b_s4], writes=[b_ps[i]])
        S.op("act", lambda e, i=i, c0=c0: e.activation(out=rstd[:, c0:c0 + TT], in_=ps[i][:, :], func=AF.Sqrt,
                                                     scale=1.0 / D, bias=epsb[:, 0:1]),
             reads=[b_ps[i], b_eps], writes=[b_rstd])
        S.op("dve", lambda e, c0=c0: e.reciprocal(out=rstd[:, c0:c0 + TT], in_=rstd[:, c0:c0 + TT]),
             reads=[b_rstd], writes=[b_rstd])
        S.dma("sp", xt[i][:], xT[:, :, c0:c0 + TT].rearrange("f p t -> p f t"), writes=[b_xt[i]])
        for fc in range(8):
            S.op("dve", lambda e, i=i, fc=fc, c0=c0: e.scalar_tensor_tensor(
                out=ot[i][:, fc, :], in0=xt[i][:, fc, :], scalar=g[:, fc:fc + 1], in1=rstd[:, c0:c0 + TT],
                op0=ALU.mult, op1=ALU.mult), reads=[b_xt[i], b_g, b_rstd], writes=[b_ot[i]])
        S.dma("sp", oT[:, :, c0:c0 + TT].rearrange("f p t -> p f t"), ot[i][:], reads=[b_ot[i]])
    S.emit()
    return nc


NA_QS = 128.0 ** -0.5
GLA_QS = 256.0 ** -0.5


def na_pairs(rows):
    kh = min(8, rows)
    out = []
    for m in range(rows // 2):
        prs = set()
        for r in (2 * m, 2 * m + 1):
            rs = min(max(r - kh // 2, 0), rows - kh)
            for y in range(rs, rs + kh):
                prs.add(y // 2)
        out.append(sorted(prs))
    return out


def a_cols(g):
    r = np.arange
    naq, nak, nav, nag = 0, 2048, 4096, 6144
    gq, gk, gv, gg, gf, gb = 8192, 9216, 10240, 12288, 14336, 14352
    p0 = np.concatenate([naq + 512 * g + r(512), nak + 512 * g + r(512)])
    p1 = np.concatenate([nag + 512 * g + r(512), gq + 256 * g + r(256), gk + 256 * g + r(256)])
    p2 = np.concatenate([gg + 512 * g + r(512), gf + r(16), gb + r(16)])
    p3 = np.concatenate([nav + 512 * g + r(512), gv + 512 * g + r(512)])
    return [p0, p1, p2, p3]


def build_a(rows=64, nctx=CTX, dbg=False, stop=9, env=None):
    if env is not None:
        return _build_a(rows, nctx, stop, env)
    global _dint
    _dint_saved = _dint
    if dbg:
        _dint = _dout
    try:
        return _build_a(rows, nctx, stop)
    finally:
        _dint = _dint_saved


def _build_a(rows, nctx, stop=9, env=None):
    nl = rows * GRID_W
    nt = nctx + nl
    nch = nt // 64
    nck = nctx // 128
    ntile128 = nt // 128
    pairs = na_pairs(rows)
    npair = rows // 2
    nbias = 25
    if env is None:
        nc = bass.Bass("TRN2", target_bir_lowering=False)
        uT = _din(nc, "uT", [32, 128, nt], BF16)
        ssq4 = _din(nc, "ssq4", [4, nt], F32)
        shT = _din(nc, "shT", [128, 32, 2], F32)
        wp = [_din(nc, "wp0", [32, 128, 1024], F32), _din(nc, "wp1", [32, 128, 1024], F32),
              _din(nc, "wp2", [32, 128, 544], F32), _din(nc, "wp3", [32, 128, 1024], F32)]
        biasT = _din(nc, "biasT", [128, 4 * nbias, 128], F32)
        wd = _din(nc, "wd", [17, 2, 256], F32)
        gng = _din(nc, "gng", [128, 4], F32)
        ropeR = _din(nc, "ropeR", [128, 2, rows], F32)
        ropeC = _din(nc, "ropeC", [128, 2, 64], F32)
        cf = _din(nc, "cf", [128, 5, 128], F32)
        cb = _din(nc, "cb", [128, 4, 128], BF16)
        zT = _dout(nc, "zT", [8, 128, nt], BF16)
        naS = _dint(nc, "naS", [12, 128, nt], BF16)
        glS = _dint(nc, "glS", [8, 128, nt], F32)
        lrS = _dint(nc, "lrS", [32, nt], F32)
        vS = _dint(nc, "vS", [nt, 1024], BF16)
        gpS = _dint(nc, "gpS", [2, 3, 128, 2, nt], BF16)
        oS = _dint(nc, "oS", [2, 128, 4, nt], F32)
        S = Sched(nc)
        B = S.buf
        banks = [_ps(nc, "bank%d" % i, [128, 512]) for i in range(8)]
        b_bank = [B("bank%d" % i) for i in range(8)]
    else:
        nc, S = env["nc"], env["S"]
        B = S.buf
        banks, b_bank = env["banks"], env["b_bank"]
        uT, ssq4, shT = env["uT"], env["ssq4"], None
        wp, biasT, wd, gng = env["wp"], env["biasT"], env["wd"], env["gng"]
        ropeR, ropeC, cf, cb = env["ropeR"], env["ropeC"], env["cf"], env["cb"]
        zT = env["zT"]
        naS, glS, lrS, vS, gpS, oS = (env[k_] for k_ in ("naS", "glS", "lrS", "vS", "gpS", "oS"))

    ttiles = [(0, nctx, 0)] + [(nctx + 512 * i, 512, 1) for i in range(nl // 512)]

    if env is not None and "consts" in env:
        cft, cbt, epsb, onec, b_cf, b_cb, b_eps = env["consts"]
    else:
        cft = _sb(nc, "cft", [128, 5, 128], F32)
        cbt = _sb(nc, "cbt", [128, 4, 128], BF16)
        epsb = _sb(nc, "epsb", [128, 1], F32)
        onec = _sb(nc, "onec", [128, 1], F32)
        b_cf, b_cb, b_eps = B("cf"), B("cb"), B("eps")
        S.dma("sp", cft[:], cf[:, :, :], writes=[b_cf])
        S.dma("sp", cbt[:], cb[:, :, :], writes=[b_cb])
        S.op("dve", lambda e: e.memset(epsb[:], EPS), writes=[b_eps])
        S.op("dve", lambda e: e.memset(onec[:], 1.0), writes=[b_eps])
        if env is not None:
            env["consts"] = (cft, cbt, epsb, onec, b_cf, b_cb, b_eps)
    b_rstd, b_rcol = B("rstd"), B("rcol")
    ones_f = cft[:, 0, :]
    perm_f = cft[:, 1, :]
    tri_f = [cft[:, 2, :], cft[:, 3, :]]
    ident_f = cft[:, 4, :]
    ident_b = cbt[:, 0, :]
    ones_b = cbt[:, 1, :]
    mask_b = [cbt[0:64, 2, 0:64], cbt[0:64, 3, 0:64]]

    ph = []

    def alloc(name, shape, dt):
        t = nc.sbuf_tensor(_u(name), list(shape), dt)
        ph.append(t)
        return t.__enter__()

    def free_all():
        S.flush(barrier=True)
        while ph:
            ph.pop().__exit__(None, None, None)

    rstd = alloc("rstd", [128, nt], F32)
    rcol = alloc("rcol", [128, ntile128], F32)
    s4 = alloc("s4", [4, nt], F32)
    sh = alloc("sh", [128, 32, 2], F32)
    shb = alloc("shb", [128, 32, 2], BF16)
    shrep = alloc("shrep", [128, 32, 2, 128], BF16)
    wbf = alloc("wbf", [128, 32, 1024], BF16)
    ut = [alloc("ut%d" % i, [128, 32, 512], BF16) for i in range(2)]
    tmp = [alloc("tmp%d" % i, [128, 512], F32) for i in range(2)]
    ob = [alloc("ob%d" % i, [128, 512], BF16) for i in range(3)]
    of = [alloc("of%d" % i, [128, 512], F32) for i in range(3)]
    sbias = alloc("sbias", [128, 9, 2], F32)
    sbias2 = alloc("sbias2", [128, 9, 2], F32)
    srow = alloc("srow", [128, 2, 2, 512], F32)
    b_s4, b_sh, b_shb, b_shrep = B("s4"), B("sh"), B("shb"), B("shrep")
    b_wbf = [B("wbf%d" % k) for k in range(32)]
    b_ut = [B("ut0"), B("ut1")]
    b_tmp = [B("tmp0"), B("tmp1")]
    b_ob = [B("ob%d" % i) for i in range(3)]
    b_of = [B("of%d" % i) for i in range(3)]
    b_sbias, b_srow = B("sbias"), B("srow")

    S.dma("sp", s4[:], ssq4[:, :], writes=[b_s4])
    if env is None:
        S.dma("sp", sh[:], shT[:, :, :], writes=[b_sh])
        S.op("dve", lambda e: e.tensor_copy(out=shb[:], in_=sh[:]), reads=[b_sh], writes=[b_shb])
    else:
        S.dma("sp", shb[:], uT[:, :, nt:nt + 2].rearrange("k p t -> p k t"), writes=[b_shb])
    S.op("dve", lambda e: e.tensor_copy(out=shrep[:], in_=shb[:].unsqueeze(3).broadcast_to([128, 32, 2, 128])),
         reads=[b_shb], writes=[b_shrep])
    for ti, (c0, n, seg) in enumerate(ttiles):
        bk = ti % 2
        S.op("pe", lambda e, bk=bk, c0=c0, n=n: e.matmul(banks[bk][:, 0:n], lhsT=ones_f[0:4, :], rhs=s4[:, c0:c0 + n],
                                                        start=True, stop=True),
             reads=[b_cf, b_s4], writes=[b_bank[bk]])
        S.op("act", lambda e, bk=bk, c0=c0, n=n: e.activation(out=rstd[:, c0:c0 + n], in_=banks[bk][:, 0:n], func=AF.Sqrt,
                                                            scale=1.0 / D, bias=epsb[:, 0:1]),
             reads=[b_bank[bk], b_eps], writes=[b_rstd])
        S.op("dve", lambda e, c0=c0, n=n: e.reciprocal(out=rstd[:, c0:c0 + n], in_=rstd[:, c0:c0 + n]),
             reads=[b_rstd], writes=[b_rstd])
    for t in range(ntile128):
        S.op("pe", lambda e, t=t: e.matmul(banks[2][:, t:t + 1], lhsT=s4[:, t * 128:(t + 1) * 128], rhs=ones_f[0:4, 0:1],
                                         start=True, stop=True), reads=[b_cf, b_s4], writes=[b_bank[2]])
    S.op("act", lambda e: e.activation(out=rcol[:], in_=banks[2][:, 0:ntile128], func=AF.Sqrt, scale=1.0 / D,
                                       bias=epsb[:, 0:1]), reads=[b_bank[2], b_eps], writes=[b_rcol])
    S.op("dve", lambda e: e.reciprocal(out=rcol[:], in_=rcol[:]), reads=[b_rcol], writes=[b_rcol])

    fm_pass = [
        [("id", NA_QS, naS, h, BF16) for h in range(4)] + [("id", 1.0, naS, 4 + h, BF16) for h in range(4)],
        [("silu", 1.0, naS, 8 + h, BF16) for h in range(4)] + [("id", GLA_QS, glS, 0, F32), ("id", GLA_QS, glS, 1, F32),
                                                               ("id", 1.0, glS, 2, F32), ("id", 1.0, glS, 3, F32)],
        [("silu", 1.0, glS, 4 + e_, F32) for e_ in range(4)] + [("lr", 1.0, lrS, 0, F32)],
    ]
    def ld_u(ti_):
        c0_, n_, _sg = ttiles[ti_]
        S.dma("sp", ut[ti_ % 2][:, :, 0:n_], uT[:, :, c0_:c0_ + n_].rearrange("k p t -> p k t"), writes=[b_ut[ti_ % 2]])

    evn = 0
    for p in range(4):
        ncols = 544 if p == 2 else 1024
        for kc in range(32):
            S.dma("pool", wbf[:, kc, 0:ncols], wp[p][kc, :, :], reads=[], writes=[b_wbf[kc]])
        if p < 3:
            tiles = fm_pass[p]
            for ci, (kind, scl, dst, di, odt) in enumerate(tiles):
                m_ = 32 if kind == "lr" else 128
                for kc in range(32):
                    S.op("pe", lambda e, ci=ci, kc=kc, m_=m_: e.matmul(
                        banks[7][0:m_, ci * 2:ci * 2 + 2], lhsT=wbf[:, kc, ci * 128:ci * 128 + m_], rhs=shb[:, kc, :],
                        start=(kc == 0), stop=(kc == 31)), reads=[b_wbf[kc], b_shb], writes=[b_bank[7]])
            nci = len(tiles)
            S.op("dve", lambda e, nci=nci: e.tensor_copy(
                out=sbias[:, 0:nci, :], in_=banks[7][:, 0:2 * nci].rearrange("p (c s) -> p c s", s=2)),
                reads=[b_bank[7]], writes=[b_sbias])
            for ci, (kind, scl, dst, di, odt) in enumerate(tiles):
                S.op("dve", lambda e, ci=ci, scl=scl: e.tensor_scalar(
                    out=sbias2[:, ci, :], in0=sbias[:, ci, :], scalar1=float(scl), scalar2=None, op0=ALU.mult),
                    reads=[b_sbias], writes=[b_sbias])
            for ti, (c0, n, seg) in enumerate(ttiles):
                ui = ti % 2
                if ti == 0:
                    ld_u(0)
                if ti + 1 < len(ttiles):
                    ld_u(ti + 1)
                for ci, (kind, scl, dst, di, odt) in enumerate(tiles):
                    m_ = 32 if kind == "lr" else 128
                    bk = evn % 4
                    for kc in range(32):
                        S.op("pe", lambda e, bk=bk, ci=ci, kc=kc, m_=m_, ui=ui, n=n: e.matmul(
                            banks[bk][0:m_, 0:n], lhsT=wbf[:, kc, ci * 128:ci * 128 + m_], rhs=ut[ui][:, kc, 0:n],
                            start=(kc == 0), stop=(kc == 31)), reads=[b_wbf[kc], b_ut[ui]], writes=[b_bank[bk]])
                    ti2 = evn % 2
                    S.op("dve", lambda e, bk=bk, ti2=ti2, m_=m_, n=n, c0=c0: e.tensor_tensor(
                        out=tmp[ti2][0:m_, 0:n], in0=banks[bk][0:m_, 0:n], in1=rstd[0:m_, c0:c0 + n], op=ALU.mult),
                        reads=[b_bank[bk], b_rstd], writes=[b_tmp[ti2]])
                    oi = evn % 3
                    otile, b_ot = (ob[oi], b_ob[oi]) if odt == BF16 else (of[oi], b_of[oi])
                    func = AF.Silu if kind == "silu" else AF.Identity
                    S.op("act", lambda e, otile=otile, ti2=ti2, m_=m_, n=n, func=func, scl=scl, ci=ci, seg=seg: e.activation(
                        out=otile[0:m_, 0:n], in_=tmp[ti2][0:m_, 0:n], func=func, scale=float(scl),
                        bias=sbias2[0:m_, ci, seg:seg + 1]), reads=[b_tmp[ti2], b_sbias], writes=[b_ot])
                    if kind == "lr":
                        S.dma("sp", dst[:, c0:c0 + n], otile[0:32, 0:n], reads=[b_ot])
                    else:
                        S.dma("sp", dst[di, :, c0:c0 + n], otile[:, 0:n], reads=[b_ot])
                    evn += 1
        else:
            for seg in range(2):
                for hf in range(2):
                    bk = 4 + (seg * 2 + hf) % 2
                    for kc in range(32):
                        S.op("pe", lambda e, bk=bk, kc=kc, seg=seg, hf=hf: e.matmul(
                            banks[bk][:, :], lhsT=shrep[:, kc, seg, :], rhs=wbf[:, kc, hf * 512:(hf + 1) * 512],
                            start=(kc == 0), stop=(kc == 31)), reads=[b_wbf[kc], b_shrep], writes=[b_bank[bk]])
                    S.op("dve", lambda e, bk=bk, seg=seg, hf=hf: e.tensor_copy(out=srow[:, seg, hf, :], in_=banks[bk][:, :]),
                         reads=[b_bank[bk]], writes=[b_srow])
            for ti, (c0, n, seg) in enumerate(ttiles):
                ui = ti % 2
                if ti == 0:
                    ld_u(0)
                if ti + 1 < len(ttiles):
                    ld_u(ti + 1)
                for st in range(n // 128):
                    tg = (c0 + st * 128) // 128
                    for hf in range(2):
                        bk = evn % 4
                        for kc in range(32):
                            S.op("pe", lambda e, bk=bk, kc=kc, ui=ui, st=st, hf=hf: e.matmul(
                                banks[bk][:, :], lhsT=ut[ui][:, kc, st * 128:(st + 1) * 128],
                                rhs=wbf[:, kc, hf * 512:(hf + 1) * 512], start=(kc == 0), stop=(kc == 31)),
                                reads=[b_wbf[kc], b_ut[ui]], writes=[b_bank[bk]])
                        oi = evn % 3
                        S.op("dve", lambda e, bk=bk, oi=oi, tg=tg, seg=seg, hf=hf: e.scalar_tensor_tensor(
                            out=ob[oi][:, :], in0=banks[bk][:, :], scalar=rcol[:, tg:tg + 1], in1=srow[:, seg, hf, :],
                            op0=ALU.mult, op1=ALU.add), reads=[b_bank[bk], b_rcol, b_srow], writes=[b_ob[oi]])
                        S.dma("sp", vS[tg * 128:(tg + 1) * 128, hf * 512:(hf + 1) * 512], ob[oi][:, :], reads=[b_ob[oi]])
                        evn += 1
    free_all()
    if stop <= 1:
        S.emit()
        return nc, S, locals()

    NB = 25

    def cls_of(m):
        if m < 2:
            return m
        if m >= npair - 2:
            return 3 + (m - (npair - 2))
        return 2

    biasb = alloc("biasb", [128, 4 * NB, 128], BF16)
    b_biasb = B("biasb")
    for i in range(0, 4 * NB, 10):
        S.dma("pool", biasb[:, i:i + 10, :], biasT[:, i:i + 10, :], writes=[b_biasb])
    hk = [alloc("hk%d" % i, [128, nt], BF16) for i in range(2)]
    hq = [alloc("hq%d" % i, [128, nt], BF16) for i in range(2)]
    hg = [alloc("hg%d" % i, [128, nt], BF16) for i in range(2)]
    hv = [alloc("hv%d" % i, [128, ntile128, 128], BF16) for i in range(2)]
    pT = [alloc("pT%d" % i, [128, 1024], BF16) for i in range(2)]
    rec = [alloc("rec%d" % i, [128, 256], F32) for i in range(2)]
    t1 = [alloc("t1_%d" % i, [128, 256], F32) for i in range(2)]
    zst = [alloc("zst%d" % i, [128, 256], BF16) for i in range(2)]
    b_hk, b_hq, b_hg, b_hv = ([B("h%s%d" % (c_, i)) for i in range(2)] for c_ in "kqgv")
    b_pT, b_rec, b_t1, b_zst = ([B("%s%d" % (c_, i)) for i in range(2)] for c_ in ("pT", "rec", "t1", "zst"))
    un = 0
    for h in range(4):
        hi = h % 2
        S.dma("sp", hq[hi][:], naS[h, :, :], writes=[b_hq[hi]])
        S.dma("sp", hk[hi][:], naS[4 + h, :, :], writes=[b_hk[hi]])
        S.dma("sp", hg[hi][:], naS[8 + h, :, :], writes=[b_hg[hi]])
        S.dma("sp", hv[hi][:], vS[:, h * 128:(h + 1) * 128].rearrange("(c p) d -> p c d", p=128), writes=[b_hv[hi]])
        units = [(0, nctx, [(128 * c, None) for c in range(nck)])]
        for m in range(npair):
            ch = [(nctx + 128 * p_, h * NB + cls_of(m) * 5 + j) for j, p_ in enumerate(pairs[m])]
            ch += [(128 * c, None) for c in range(nck)]
            units.append((nctx + 128 * m, 128, ch))
        for (q0, N, ch) in units:
            u_ = un % 2
            per = 512 // N
            sb_ = [2 * u_, 2 * u_ + 1]
            bo, bs = 4 + 2 * u_, 5 + 2 * u_
            for i, (k0, bi) in enumerate(ch):
                bk = sb_[i // per]
                o0 = (i % per) * N
                S.op("pe", lambda e, bk=bk, o0=o0, N=N, k0=k0, q0=q0, hi=hi, bi=bi: e.matmul(
                    banks[bk][:, o0:o0 + N], lhsT=hk[hi][:, k0:k0 + 128], rhs=hq[hi][:, q0:q0 + N],
                    start=True, stop=(bi is None)), reads=[b_hk[hi], b_hq[hi]], writes=[b_bank[bk]])
                if bi is not None:
                    S.op("pe", lambda e, bk=bk, o0=o0, N=N, bi=bi: e.matmul(
                        banks[bk][:, o0:o0 + N], lhsT=ident_b, rhs=biasb[:, bi, :], start=False, stop=True),
                        reads=[b_cb, b_biasb], writes=[b_bank[bk]])
            ng = (len(ch) + per - 1) // per
            for gi in range(ng):
                cnt = min(per, len(ch) - gi * per)
                S.op("act", lambda e, gi=gi, cnt=cnt, N=N, u_=u_, bk=sb_[gi], per=per: e.activation(
                    out=pT[u_][:, gi * per * N:gi * per * N + cnt * N], in_=banks[bk][:, 0:cnt * N], func=AF.Exp),
                    reads=[b_bank[sb_[gi]]], writes=[b_pT[u_]])
            for i, (k0, bi) in enumerate(ch):
                S.op("pe", lambda e, i=i, k0=k0, N=N, bo=bo, hi=hi, u_=u_, last=(i == len(ch) - 1): e.matmul(
                    banks[bo][:, 0:N], lhsT=hv[hi][:, k0 // 128, :], rhs=pT[u_][:, i * N:(i + 1) * N],
                    start=(i == 0), stop=last), reads=[b_hv[hi], b_pT[u_]], writes=[b_bank[bo]])
            for i, (k0, bi) in enumerate(ch):
                S.op("pe", lambda e, i=i, N=N, bs=bs, u_=u_, last=(i == len(ch) - 1): e.matmul(
                    banks[bs][:, 0:N], lhsT=ones_b, rhs=pT[u_][:, i * N:(i + 1) * N],
                    start=(i == 0), stop=last), reads=[b_cb, b_pT[u_]], writes=[b_bank[bs]])
            S.op("dve", lambda e, N=N, bs=bs, u_=u_: e.reciprocal(out=rec[u_][:, 0:N], in_=banks[bs][:, 0:N]),
                 reads=[b_bank[bs]], writes=[b_rec[u_]])
            S.op("dve", lambda e, N=N, bo=bo, u_=u_: e.tensor_tensor(out=t1[u_][:, 0:N], in0=banks[bo][:, 0:N],
                                                                    in1=rec[u_][:, 0:N], op=ALU.mult),
                 reads=[b_bank[bo], b_rec[u_]], writes=[b_t1[u_]])
            S.op("dve", lambda e, N=N, u_=u_, hi=hi, q0=q0: e.tensor_tensor(out=zst[u_][:, 0:N], in0=t1[u_][:, 0:N],
                                                                        in1=hg[hi][:, q0:q0 + N], op=ALU.mult),
                 reads=[b_t1[u_], b_hg[hi]], writes=[b_zst[u_]])
            S.dma("sp", zT[h, :, q0:q0 + N], zst[u_][:, 0:N], reads=[b_zst[u_]])
            un += 1
    free_all()
    if stop <= 2:
        S.emit()
        return nc, S, locals()

    wdt = alloc("wdt", [17, 2, 256], F32)
    gngt = alloc("gngt", [128, 4], F32)
    rR = alloc("rR", [128, 2, rows], F32)
    rC = alloc("rC", [128, 2, 64], F32)
    dec = alloc("dec", [128, 2, 2, nch], F32)
    b_wdt, b_gng, b_rope, b_dec = B("wdt"), B("gng"), B("rope"), B("dec")
    S.dma("sp", wdt[:], wd[:, :, :], writes=[b_wdt])
    S.dma("sp", gngt[:], gng[:, :], writes=[b_gng])
    S.dma("sp", rR[:], ropeR[:, :, :], writes=[b_rope])
    S.dma("sp", rC[:], ropeC[:, :, :], writes=[b_rope])
    sub = []

    def salloc(name, shape, dt):
        t = nc.sbuf_tensor(_u(name), list(shape), dt)
        sub.append(t)
        return t.__enter__()

    def sfree():
        S.flush(barrier=True)
        while sub:
            sub.pop().__exit__(None, None, None)

    qk = [salloc("qk%d" % i, [128, 4, 512], F32) for i in range(2)]
    qr = [salloc("qr%d" % i, [128, 4, 512], F32) for i in range(2)]
    rt = salloc("rt", [128, 512], F32)
    lrt = [salloc("lrt%d" % i, [17, 512], F32) for i in range(2)]
    e1 = [salloc("e1_%d" % i, [128, 256], F32) for i in range(2)]
    Lt = [salloc("Lt%d" % i, [128, 256], F32) for i in range(2)]
    bl = [salloc("bl%d" % i, [128, 2, 8], F32) for i in range(2)]
    E = [salloc("E%d" % i, [128, 3, 2, 512], F32) for i in range(2)]
    gp = [salloc("gp%d" % i, [128, 3, 2, 512], BF16) for i in range(2)]
    b_qk, b_qr, b_lrt, b_e1, b_Lt, b_bl, b_E, b_gp = ([B("%s%d" % (c_, i)) for i in range(2)]
                                                      for c_ in ("qk", "qr", "lrt", "e1", "Lt", "bl", "E", "gp"))
    b_rt = B("rt")
    for dr in range(2):
        S.op("dve", lambda e, dr=dr: e.memset(lrt[dr][:], 1.0), writes=[b_lrt[dr]])
    it = 0
    for ti, (c0, n, seg) in enumerate(ttiles):
        i2 = ti % 2
        S.dma("sp", qk[i2][:, :, 0:n], glS[0:4, :, c0:c0 + n].rearrange("j p t -> p j t"), writes=[b_qk[i2]])
        if seg == 1:
            r0 = (c0 - nctx) // 64
            nr = n // 64
            for j in range(4):
                hf = j % 2
                if hf == 0:
                    cosap = rR[:, 0, r0:r0 + nr].unsqueeze(2).broadcast_to([128, nr, 64])
                    sinap = rR[:, 1, r0:r0 + nr].unsqueeze(2).broadcast_to([128, nr, 64])
                else:
                    cosap = rC[:, 0, :].unsqueeze(1).broadcast_to([128, nr, 64])
                    sinap = rC[:, 1, :].unsqueeze(1).broadcast_to([128, nr, 64])
                bk = j % 2
                S.op("pe", lambda e, bk=bk, i2=i2, j=j, n=n: e.matmul(banks[bk][:, 0:n], lhsT=perm_f, rhs=qk[i2][:, j, 0:n],
                                                                  start=True, stop=True),
                     reads=[b_cf, b_qk[i2]], writes=[b_bank[bk]])
                S.op("dve", lambda e, i2=i2, j=j, n=n, cosap=cosap: e.tensor_tensor(
                    out=qr[i2][:, j, 0:n].rearrange("p (r c) -> p r c", c=64),
                    in0=qk[i2][:, j, 0:n].rearrange("p (r c) -> p r c", c=64), in1=cosap, op=ALU.mult),
                    reads=[b_qk[i2], b_rope], writes=[b_qr[i2]])
                S.op("dve", lambda e, bk=bk, n=n, sinap=sinap: e.tensor_tensor(
                    out=rt[:, 0:n].rearrange("p (r c) -> p r c", c=64),
                    in0=banks[bk][:, 0:n].rearrange("p (r c) -> p r c", c=64), in1=sinap, op=ALU.mult),
                    reads=[b_bank[bk], b_rope], writes=[b_rt])
                S.op("dve", lambda e, i2=i2, j=j, n=n: e.tensor_tensor(out=qr[i2][:, j, 0:n], in0=qr[i2][:, j, 0:n],
                                                                   in1=rt[:, 0:n], op=ALU.add),
                     reads=[b_qr[i2], b_rt], writes=[b_qr[i2]])
            src, b_src = qr[i2], b_qr[i2]
        else:
            src, b_src = qk[i2], b_qk[i2]
        for dr in range(2):
            S.dma("sp", lrt[dr][0:16, 0:n], lrS[16 * dr:16 * dr + 16, c0:c0 + n], writes=[b_lrt[dr]])
            for st in range(n // 128):
                bg = 2 + it % 2
                ei = it % 2
                S.op("pe", lambda e, bg=bg, dr=dr, st=st: e.matmul(
                    banks[bg][:, 0:256], lhsT=lrt[dr][0:17, st * 128:(st + 1) * 128], rhs=wdt[0:17, dr, :],
                    start=True, stop=True), reads=[b_lrt[dr], b_wdt], writes=[b_bank[bg]])
                S.op("act", lambda e, bg=bg, ei=ei: e.activation(out=e1[ei][:], in_=banks[bg][:, 0:256], func=AF.Exp, scale=-1.0),
                     reads=[b_bank[bg]], writes=[b_e1[ei]])
                S.op("act", lambda e, ei=ei: e.activation(out=Lt[ei][:], in_=e1[ei][:], func=AF.Ln, bias=onec[:, 0:1]),
                     reads=[b_e1[ei], b_eps], writes=[b_Lt[ei]])
                for dh in range(2):
                    S.op("pe", lambda e, dh=dh, ei=ei, st=st, dr=dr: e.matmul(
                        banks[4 + dh][:, st * 128:(st + 1) * 128], lhsT=Lt[ei][:, dh * 128:(dh + 1) * 128], rhs=tri_f[dr],
                        start=True, stop=True), reads=[b_Lt[ei], b_cf], writes=[b_bank[4 + dh]])
                it += 1
            di = (ti * 2 + dr) % 2
            ncb = n // 64
            cb0 = c0 // 64
            off = 63 if dr == 0 else 0
            for dh in range(2):
                S.op("dve", lambda e, di=di, dh=dh, ncb=ncb, off=off, n=n: e.tensor_copy(
                    out=bl[di][:, dh, 0:ncb], in_=banks[4 + dh][:, 0:n].rearrange("p (c t) -> p c t", t=64)[:, :, off]),
                    reads=[b_bank[4 + dh]], writes=[b_bl[di]])
            S.op("act", lambda e, di=di, dr=dr, ncb=ncb, cb0=cb0: e.activation(
                out=dec[:, dr, :, cb0:cb0 + ncb], in_=bl[di][:, :, 0:ncb], func=AF.Exp),
                reads=[b_bl[di]], writes=[b_dec])
            for dh in range(2):
                S.op("act", lambda e, di=di, dh=dh, n=n: e.activation(out=E[di][:, 0, dh, 0:n], in_=banks[4 + dh][:, 0:n],
                                                                   func=AF.Exp), reads=[b_bank[4 + dh]], writes=[b_E[di]])
                S.op("act", lambda e, di=di, dh=dh, n=n: e.activation(out=E[di][:, 1, dh, 0:n], in_=banks[4 + dh][:, 0:n],
                                                                   func=AF.Exp, scale=-1.0),
                     reads=[b_bank[4 + dh]], writes=[b_E[di]])
                for c in range(ncb):
                    S.op("act", lambda e, di=di, dh=dh, c=c: e.activation(
                        out=E[di][:, 2, dh, c * 64:(c + 1) * 64], in_=banks[4 + dh][:, c * 64:(c + 1) * 64], func=AF.Exp,
                        scale=-1.0, bias=bl[di][:, dh, c:c + 1]), reads=[b_bank[4 + dh], b_bl[di]], writes=[b_E[di]])
            for dh in range(2):
                for j, sj in ((0, dh), (1, 2 + dh), (2, 2 + dh)):
                    S.op("dve", lambda e, di=di, dh=dh, j=j, sj=sj, n=n, src=src: e.tensor_tensor(
                        out=gp[di][:, j, dh, 0:n], in0=src[:, sj, 0:n], in1=E[di][:, j, dh, 0:n], op=ALU.mult),
                        reads=[b_src, b_E[di]], writes=[b_gp[di]])
            for j in range(3):
                S.dma("sp", gpS[dr, j, :, :, c0:c0 + n], gp[di][:, j, :, 0:n], reads=[b_gp[di]])
    sfree()
    if stop <= 3:
        free_all()
        S.emit()
        return nc, S, locals()

    Sf = [salloc("Sf%d" % i, [128, 2, 512], F32) for i in range(2)]
    Sb = [[salloc("Sb%d_%d" % (d_, i), [128, 2, 512], BF16) for i in range(2)] for d_ in range(2)]
    gpb = [[salloc("gpb%d_%d" % (d_, i), [128, 3, 2, 512], BF16) for i in range(2)] for d_ in range(2)]
    vb = [[salloc("vb%d_%d" % (d_, i), [64, 8, 512], BF16) for i in range(2)] for d_ in range(2)]
    am = [[salloc("am%d_%d" % (d_, i), [64, 64], BF16) for i in range(2)] for d_ in range(2)]
    kdt = [[salloc("kdt%d_%d" % (d_, i), [64, 256], BF16) for i in range(2)] for d_ in range(2)]
    och = [[salloc("och%d_%d" % (d_, i), [128, 4, 64], F32) for i in range(2)] for d_ in range(2)]
    b_Sf = [B("Sf0"), B("Sf1")]
    b_Sb, b_gpb, b_vb, b_am, b_kdt, b_och = ([[B("%s%d_%d" % (c_, d_, i)) for i in range(2)] for d_ in range(2)]
                                             for c_ in ("Sb", "gpb", "vb", "am", "kdt", "och"))
    b_A = [B("psA0"), B("psA1")]
    b_T = [B("psT0"), B("psT1")]
    nck64 = nctx // 64
    orders = [list(range(nch)), list(range(nck64 - 1, -1, -1)) + list(range(nch - 1, nck64 - 1, -1))]
    cur_blk, slot = [-1, -1], [1, 1]
    for dr in range(2):
        S.op("dve", lambda e, dr=dr: e.memset(Sf[dr][:], 0.0), reads=[], writes=[b_Sf[dr]])
        S.op("dve", lambda e, dr=dr: e.memset(Sb[dr][0][:], 0.0), reads=[], writes=[b_Sb[dr][0]])
    for step in range(nch):
        for dr in range(2):
            c = orders[dr][step]
            blk = c // 8
            if blk != cur_blk[dr]:
                cur_blk[dr] = blk
                slot[dr] ^= 1
                sl = slot[dr]
                t0b = blk * 512
                nb = min(512, nt - t0b)
                for j in range(3):
                    S.dma("sp", gpb[dr][sl][:, j, :, 0:nb], gpS[dr, j, :, :, t0b:t0b + nb], writes=[b_gpb[dr][sl]])
                S.dma("sp", vb[dr][sl][:, 0:nb // 64, :],
                      vS[t0b:t0b + nb, 512:1024].rearrange("(c p) e -> p c e", p=64), writes=[b_vb[dr][sl]])
            sl = slot[dr]
            g_, v_, bg_, bv_ = gpb[dr][sl], vb[dr][sl], b_gpb[dr][sl], b_vb[dr][sl]
            ci = c % 8
            cs = ci * 64
            pi = step % 2
            bAT = banks[4 * dr]
            bO = banks[4 * dr + 1]
            bOb = b_bank[4 * dr + 1]
            am_, kdt_, och_ = am[dr][pi], kdt[dr][pi], och[dr][pi]
            bam_, bkdt_, boch_ = b_am[dr][pi], b_kdt[dr][pi], b_och[dr][pi]
            Sb_r, bSb_r = Sb[dr][pi], b_Sb[dr][pi]
            Sb_w, bSb_w = Sb[dr][1 - pi], b_Sb[dr][1 - pi]
            for dh in range(2):
                S.op("pe", lambda e, dh=dh, g_=g_, cs=cs, bAT=bAT: e.matmul(
                    bAT[0:64, 0:64], lhsT=g_[:, 1, dh, cs:cs + 64], rhs=g_[:, 0, dh, cs:cs + 64],
                    start=(dh == 0), stop=(dh == 1)), reads=[bg_], writes=[b_A[dr]])
            S.op("dve", lambda e, am_=am_, bAT=bAT, dr=dr: e.tensor_tensor(out=am_[:], in0=bAT[0:64, 0:64], in1=mask_b[dr],
                                                                      op=ALU.mult), reads=[b_A[dr], b_cb], writes=[bam_])
            for dh in range(2):
                S.op("pe", lambda e, dh=dh, g_=g_, cs=cs, bAT=bAT: e.matmul(
                    bAT[0:64, 128 + dh * 128:128 + (dh + 1) * 128], lhsT=g_[:, 2, dh, cs:cs + 64], rhs=ident_b,
                    start=True, stop=True), reads=[bg_, b_cb], writes=[b_T[dr]])
            S.op("act", lambda e, kdt_=kdt_, bAT=bAT: e.activation(out=kdt_[:], in_=bAT[0:64, 128:384], func=AF.Identity),
                 reads=[b_T[dr]], writes=[bkdt_])
            for ec in range(4):
                S.op("pe", lambda e, ec=ec, bO=bO, v_=v_, ci=ci, am_=am_: e.matmul(
                    bO[:, ec * 64:(ec + 1) * 64], lhsT=v_[0:64, ci, ec * 128:(ec + 1) * 128], rhs=am_[:],
                    start=True, stop=False), reads=[bv_, bam_], writes=[bOb])
                for dh in range(2):
                    S.op("pe", lambda e, ec=ec, bO=bO, dh=dh, Sb_r=Sb_r, g_=g_, cs=cs: e.matmul(
                        bO[:, ec * 64:(ec + 1) * 64], lhsT=Sb_r[:, dh, ec * 128:(ec + 1) * 128],
                        rhs=g_[:, 0, dh, cs:cs + 64], start=False, stop=(dh == 1)),
                        reads=[bSb_r, bg_], writes=[bOb])
            S.op("act", lambda e, och_=och_, bO=bO: e.activation(out=och_[:].rearrange("p a t -> p (a t)"),
                                                              in_=bO[:, 0:256], func=AF.Identity),
                 reads=[bOb], writes=[boch_])
            S.dma("sp", oS[dr, :, :, c * 64:(c + 1) * 64], och_[:], reads=[boch_])
            for dh in range(2):
                bu = 4 * dr + 2 + dh
                S.op("pe", lambda e, bu=bu, dh=dh, kdt_=kdt_, v_=v_, ci=ci: e.matmul(
                    banks[bu][:, :], lhsT=kdt_[0:64, dh * 128:(dh + 1) * 128], rhs=v_[0:64, ci, :],
                    start=True, stop=True), reads=[bkdt_, bv_], writes=[b_bank[bu]])
                S.op("dve", lambda e, bu=bu, dh=dh, dr=dr, c=c: e.scalar_tensor_tensor(
                    out=Sf[dr][:, dh, :], in0=Sf[dr][:, dh, :], scalar=dec[:, dr, dh, c:c + 1], in1=banks[bu][:, :],
                    op0=ALU.mult, op1=ALU.add), reads=[b_Sf[dr], b_dec, b_bank[bu]], writes=[b_Sf[dr]])
            S.op("act", lambda e, Sb_w=Sb_w, dr=dr: e.activation(out=Sb_w[:], in_=Sf[dr][:], func=AF.Identity),
                 reads=[b_Sf[dr]], writes=[bSb_w])
    sfree()
    if stop <= 4:
        free_all()
        if env is None:
            S.emit()
        return nc, S, locals()

    ofb = [salloc("ofb%d" % i, [128, 4, 512], F32) for i in range(2)]
    obb = [salloc("obb%d" % i, [128, 4, 512], F32) for i in range(2)]
    sg = [salloc("sg%d" % i, [128, 4, 512], F32) for i in range(2)]
    sq3 = [salloc("sq3_%d" % i, [128, 4, 512], F32) for i in range(2)]
    rr = [salloc("rr%d" % i, [128, 512], F32) for i in range(2)]
    zt3 = [salloc("zt3_%d" % i, [128, 4, 512], BF16) for i in range(2)]
    b_ofb, b_obb, b_sg, b_sq3, b_rr, b_zt3 = ([B("%s%d" % (c_, i)) for i in range(2)]
                                              for c_ in ("ofb", "obb", "sg", "sq3", "rr", "zt3"))
    for ti, (c0, n, seg) in enumerate(ttiles):
        i2 = ti % 2
        S.dma("sp", ofb[i2][:, :, 0:n], oS[0, :, :, c0:c0 + n], writes=[b_ofb[i2]])
        S.dma("sp", obb[i2][:, :, 0:n], oS[1, :, :, c0:c0 + n], writes=[b_obb[i2]])
        S.dma("sp", sg[i2][:, :, 0:n], glS[4:8, :, c0:c0 + n].rearrange("j p t -> p j t"), writes=[b_sg[i2]])
        S.op("dve", lambda e, i2=i2, n=n: e.tensor_tensor(out=ofb[i2][:, :, 0:n], in0=ofb[i2][:, :, 0:n],
                                                        in1=obb[i2][:, :, 0:n], op=ALU.add),
             reads=[b_ofb[i2], b_obb[i2]], writes=[b_ofb[i2]])
        S.op("act", lambda e, i2=i2, n=n: e.activation(out=sq3[i2][:, :, 0:n], in_=ofb[i2][:, :, 0:n], func=AF.Square),
             reads=[b_ofb[i2]], writes=[b_sq3[i2]])
        bk = i2
        for ec in range(4):
            S.op("pe", lambda e, bk=bk, ec=ec, i2=i2, n=n: e.matmul(banks[bk][:, 0:n], lhsT=ones_f, rhs=sq3[i2][:, ec, 0:n],
                                                                 start=(ec == 0), stop=(ec == 3)),
                 reads=[b_cf, b_sq3[i2]], writes=[b_bank[bk]])
        S.op("act", lambda e, bk=bk, i2=i2, n=n: e.activation(out=rr[i2][:, 0:n], in_=banks[bk][:, 0:n], func=AF.Sqrt,
                                                            scale=1.0 / 512.0, bias=epsb[:, 0:1]),
             reads=[b_bank[bk], b_eps], writes=[b_rr[i2]])
        S.op("dve", lambda e, i2=i2, n=n: e.reciprocal(out=rr[i2][:, 0:n], in_=rr[i2][:, 0:n]),
             reads=[b_rr[i2]], writes=[b_rr[i2]])
        for ec in range(4):
            S.op("dve", lambda e, i2=i2, ec=ec, n=n: e.scalar_tensor_tensor(
                out=sq3[i2][:, ec, 0:n], in0=ofb[i2][:, ec, 0:n], scalar=gngt[:, ec:ec + 1], in1=rr[i2][:, 0:n],
                op0=ALU.mult, op1=ALU.mult), reads=[b_ofb[i2], b_gng, b_rr[i2], b_bank[bk]], writes=[b_sq3[i2]])
            S.op("dve", lambda e, i2=i2, ec=ec, n=n: e.tensor_tensor(out=zt3[i2][:, ec, 0:n], in0=sq3[i2][:, ec, 0:n],
                                                                  in1=sg[i2][:, ec, 0:n], op=ALU.mult),
                 reads=[b_sq3[i2], b_sg[i2]], writes=[b_zt3[i2]])
        S.dma("sp", zT[4:8, :, c0:c0 + n].rearrange("j p t -> p j t"), zt3[i2][:, :, 0:n], reads=[b_zt3[i2]])
    sfree()
    free_all()
    if env is None:
        S.emit()
    return nc, S, locals()


def const_f32():
    c = np.zeros((128, 5, 128), np.float32)
    c[:, 0, :] = 1.0
    i = np.arange(128)
    c[i, 1, (i + 64) % 128] = 1.0
    c[:, 1, :] = c[:, 1, :].T
    s_, t_ = np.meshgrid(i, i, indexing="ij")
    same = (s_ // 64) == (t_ // 64)
    c[:, 2, :] = np.where(same & (s_ <= t_), -1.0 / 16.0, 0.0)
    c[:, 3, :] = np.where(same & (s_ >= t_), -1.0 / 16.0, 0.0)
    c[i, 4, i] = 1.0
    return c


def const_bf16():
    c = np.zeros((128, 4, 128), np.float32)
    i = np.arange(128)
    c[i, 0, i] = 1.0
    c[:, 1, :] = 1.0
    s_, t_ = np.meshgrid(np.arange(64), np.arange(64), indexing="ij")
    c[0:64, 2, 0:64] = (s_ <= t_)
    c[0:64, 3, 0:64] = (s_ >= t_)
    return c.astype(NPBF)


def rope_tables(rows):
    i = np.arange(64, dtype=np.float32)
    inv = (np.float32(10000.0) ** (-i / np.float32(64.0))).astype(np.float32)

    def tab(npos):
        ang = (np.arange(npos, dtype=np.float32)[None, :] * inv[:, None]).astype(np.float32)
        cos = np.concatenate([np.cos(ang), np.cos(ang)], 0)
        sin = np.concatenate([-np.sin(ang), np.sin(ang)], 0)
        return np.ascontiguousarray(np.stack([cos, sin], 1).astype(np.float32))

    return tab(rows), tab(64)


def na_bias_table(rpb_l, g, rows):
    kh = min(8, rows)
    pairs = na_pairs(rows)
    npair = rows // 2
    reps = [0, 1, 2, npair - 2, npair - 1]
    out = np.full((128, 4, 25, 128), NEG, np.float32)
    loc = np.arange(128)
    yl, xl = loc // 64, loc % 64
    for ci, m in enumerate(reps):
        for j, p in enumerate(pairs[m]):
            ky = (2 * p + yl)[:, None]
            kx = xl[:, None]
            qy = (2 * m + yl)[None, :]
            qx = xl[None, :]
            rs = np.clip(qy - kh // 2, 0, rows - kh)
            okr = (ky >= rs) & (ky < rs + kh)
            cst = np.clip(qx - 8, 0, 64 - 16)
            okc = (kx >= cst) & (kx < cst + 16)
            dy = np.clip(ky - qy + 7, 0, 14)
            dx = np.clip(kx - qx + 15, 0, 30)
            ok = okr & okc
            for h in range(4):
                vals = rpb_l[4 * g + h][dy, dx]
                out[:, h, ci * 5 + j, :] = np.where(ok, vals, np.float32(NEG))
    return np.ascontiguousarray(out.reshape(128, 100, 128))


def pmaj(a):
    n = a.shape[0] // 128
    return np.ascontiguousarray(a.reshape(n, 128, *a.shape[1:]).swapaxes(0, 1))


def a_weight_inputs(w_in_l, rpb_l, wdec_l, bdec_l, gng_l, g, rows):
    cols = a_cols(g)
    im = {}
    for p in range(4):
        im["wp%d" % p] = np.ascontiguousarray(w_in_l[:, cols[p]]).reshape(32, 128, -1)
    im["biasT"] = na_bias_table(rpb_l, g, rows)
    wd = np.empty((17, 2, 256), np.float32)
    wd[0:16] = wdec_l[:, :, 256 * g:256 * g + 256].transpose(1, 0, 2)
    wd[16] = bdec_l[:, 256 * g:256 * g + 256]
    im["wd"] = wd
    im["gng"] = np.ascontiguousarray(gng_l.reshape(4, 128).T)
    rR, rC = rope_tables(rows)
    im["ropeR"], im["ropeC"] = rR, rC
    im["cf"], im["cb"] = const_f32(), const_bf16()
    return im


_PROG = {}
_DBG = None


def _prog(key, fn):
    if key not in _PROG:
        _PROG[key] = fn()
    return _PROG[key]


def _run(nc, in_maps):
    res = run_bass_kernel_spmd(nc, in_maps, core_ids=list(range(len(in_maps))))
    return res.results


def kernel_impl(x, c, ctx, c_ctx, ada_w, ada_b, norm_g, w_in, na_rpb, gla_w_decay, gla_b_decay, gla_norm_g, w_out,
                final_norm_g):
    f32 = np.float32
    x, c, ctx, c_ctx = (np.asarray(a, f32) for a in (x, c, ctx, c_ctx))
    ada_w, ada_b, norm_g, w_in, na_rpb = (np.asarray(a, f32) for a in (ada_w, ada_b, norm_g, w_in, na_rpb))
    gla_w_decay, gla_b_decay, gla_norm_g, w_out, final_norm_g = (
        np.asarray(a, f32) for a in (gla_w_decay, gla_b_decay, gla_norm_g, w_out, final_norm_g))
    depth = ada_w.shape[0]
    nb, nl, _ = x.shape
    nctx = ctx.shape[1]
    rows = nl // GRID_W
    nt = nctx + nl
    assert nb == 2
    cores = [(b, g) for b in range(2) for g in range(4)]
    ones = np.ones((128, 128), f32)

    nc_ada = _prog(("ada", depth), lambda: build_ada(depth))
    cvec = np.stack([c[0], c[1], c_ctx], 0)
    cT = np.ascontiguousarray(cvec.T.reshape(32, 128, 3).transpose(1, 0, 2))
    ims = []
    for j in range(8):
        cs = slice(ADA_COLS * j, ADA_COLS * (j + 1))
        ims.append({"cT": cT, "adaw": np.ascontiguousarray(ada_w[:, :, cs]).reshape(depth, 32, 128, ADA_COLS),
                    "adab": np.ascontiguousarray(np.broadcast_to(ada_b[:, None, cs], (depth, 3, ADA_COLS)))})
    r = _run(nc_ada, ims)
    mod = np.concatenate([r[j]["mod"] for j in range(8)], axis=-1)
    if _DBG is not None:
        _DBG['mod'] = mod
    shift, scale, gate = mod[:, :, 0:D], mod[:, :, D:2 * D], mod[:, :, 2 * D:3 * D]

    def seg2(v, l, b, sl):
        return pmaj(np.stack([v[l, 2, sl], v[l, b, sl]], -1))

    xs = []
    for (b, g) in cores:
        fs = slice(1024 * g, 1024 * (g + 1))
        xb = np.concatenate([ctx[b][:, fs], x[b][:, fs]], 0)
        xs.append(np.ascontiguousarray(xb.T).reshape(8, 128, nt))

    def gather_u(res):
        uT = [np.concatenate([res[4 * b + g]["uT"] for g in range(4)], 0) for b in range(2)]
        ssq4 = [np.concatenate([res[4 * b + g]["ssq"] for g in range(4)], 0) for b in range(2)]
        return uT, ssq4

    nc_p0 = _prog(("p0", nt), lambda: build_p(False, nt))
    ims = []
    for ci, (b, g) in enumerate(cores):
        fs = slice(1024 * g, 1024 * (g + 1))
        ims.append({"xT": xs[ci], "gvec": pmaj(norm_g[0, fs]), "scT": seg2(scale, 0, b, fs), "ones": ones})
    uT, ssq4 = gather_u(_run(nc_p0, ims))
    if _DBG is not None:
        _DBG['u0'] = uT
        _DBG['ssq0'] = ssq4

    nc_a = _prog(("a", rows, nctx), lambda: build_a(rows, nctx)[0])
    nc_p = _prog(("p", nt), lambda: build_p(True, nt))
    for l in range(depth):
        ims = []
        wcache = {}
        for ci, (b, g) in enumerate(cores):
            if g not in wcache:
                wcache[g] = a_weight_inputs(w_in[l], na_rpb[l], gla_w_decay[l], gla_b_decay[l], gla_norm_g[l], g, rows)
            im = {"uT": uT[b], "ssq4": ssq4[b], "shT": seg2(shift, l, b, slice(0, D))}
            im.update(wcache[g])
            ims.append(im)
        res = _run(nc_a, ims)
        del wcache
        zT = []
        for b in range(2):
            zf = np.empty((32, 128, nt), NPBF)
            for g in range(4):
                zf[4 * g:4 * g + 4] = res[4 * b + g]["zT"][0:4]
                zf[16 + 4 * g:16 + 4 * g + 4] = res[4 * b + g]["zT"][4:8]
            zT.append(zf)
        if _DBG is not None:
            _DBG['z%d' % l] = zT
        ln = min(l + 1, depth - 1)
        ims = []
        for ci, (b, g) in enumerate(cores):
            fs = slice(1024 * g, 1024 * (g + 1))
            ims.append({"xT": xs[ci], "zT": zT[b], "wout": np.ascontiguousarray(w_out[l][:, fs]).reshape(32, 128, 1024),
                        "gateT": seg2(gate, l, b, fs), "gvec": pmaj(norm_g[ln, fs]), "scT": seg2(scale, ln, b, fs),
                        "ones": ones})
        res = _run(nc_p, ims)
        xs = [res[ci]["xoT"] for ci in range(8)]
        uT, ssq4 = gather_u(res)
        if _DBG is not None:
            _DBG['x%d' % (l + 1)] = xs
            _DBG['u%d' % (l + 1)] = uT
            _DBG['ssq%d' % (l + 1)] = ssq4

    nc_f = _prog(("f", nl), lambda: build_f(nl))
    ims = []
    for ci, (b, g) in enumerate(cores):
        fs = slice(1024 * g, 1024 * (g + 1))
        ims.append({"xT": np.ascontiguousarray(xs[ci][:, :, nctx:]), "ssq4": np.ascontiguousarray(ssq4[b][:, nctx:]),
                    "fg": pmaj(final_norm_g[fs]), "ones": ones})
    res = _run(nc_f, ims)
    out = np.empty((2, nl, D), f32)
    for ci, (b, g) in enumerate(cores):
        out[b][:, 1024 * g:1024 * (g + 1)] = res[ci]["oT"].reshape(1024, nl).T
    return out


def kernel(**inputs):
    return kernel_fused(**inputs)


XPAD = 16
GROUPS4 = [[0, 1, 2, 3], [4, 5, 6, 7]]


def build_fused(depth=DEPTH, rows=64, nctx=CTX):
    nl = rows * GRID_W
    nt = nctx + nl
    ntx = nt + XPAD
    npt = nt // PT
    nc = bass.Bass("TRN2", target_bir_lowering=False)
    S = Sched(nc)
    B = S.buf
    banks = [_ps(nc, "bank%d" % i, [128, 512]) for i in range(8)]
    b_bank = [B("bank%d" % i) for i in range(8)]
    xT = _din(nc, "xT", [8, 128, nt], F32)
    cT2 = _din(nc, "cT2", [128, 32, 2], F32)
    adaw = _din(nc, "adaw", [depth, 32, 128, 3072], F32)
    adab = _din(nc, "adab", [depth, 2, 3072], F32)
    gvec = _din(nc, "gvec", [128, depth, 8], F32)
    fg = _din(nc, "fg", [128, 8], F32)
    wout = _din(nc, "wout", [depth, 32, 128, 1024], F32)
    wp = [[_din(nc, "wp%d_%d" % (p, l), [32, 128, 544 if p == 2 else 1024], F32) for p in range(4)] for l in range(depth)]
    biasT = [_din(nc, "biasT_%d" % l, [128, 100, 128], F32) for l in range(depth)]
    wd = [_din(nc, "wd_%d" % l, [17, 2, 256], F32) for l in range(depth)]
    gng = [_din(nc, "gng_%d" % l, [128, 4], F32) for l in range(depth)]
    ropeR = _din(nc, "ropeR", [128, 2, rows], F32)
    ropeC = _din(nc, "ropeC", [128, 2, 64], F32)
    cf = _din(nc, "cf", [128, 5, 128], F32)
    cb = _din(nc, "cb", [128, 4, 128], BF16)
    oT = _dout(nc, "oT", [8, 128, nl], F32)
    xS = _dint(nc, "xS", [8, 128, nt], F32)
    uloc = _dint(nc, "uloc", [8, 128, ntx], BF16)
    ufull = _dint(nc, "ufull", [32, 128, ntx], BF16)
    sloc = _dint(nc, "sloc", [1, nt], F32)
    half = 64 * ntx
    cin = [nc.dram_tensor(_u("cin"), [32, half // 32], BF16) for _ in range(16)]
    cout = [nc.dram_tensor(_u("cout"), [128, half // 32], BF16) for _ in range(16)]
    sin_ = nc.dram_tensor(_u("sin"), [32, nt // 32], F32)
    sout = nc.dram_tensor(_u("sout"), [128, nt // 32], F32)
    b_cin = [B("cin%d" % j) for j in range(16)]
    b_cout = [B("cout%d" % j) for j in range(16)]
    b_sin, b_sout = B("sin"), B("sout")
    ssq4 = sout.ap().rearrange("(g a) b -> g (a b)", g=4)

    def barrier():
        S.flush(barrier=True)

    def exchange(with_ssq):
        barrier()
        ufg = ufull.rearrange("(g i) p t -> g i p t", g=4)
        for i in range(8):
            for hf in range(2):
                j = 2 * i + hf
                S.dma("sp", cin[j].ap().rearrange("a (b t) -> (a b) t", b=2), uloc[i, 64 * hf:64 * hf + 64, :],
                      writes=[b_cin[j]])
        for j in range(16):
            S.cc(GROUPS4, cin[j], cout[j], reads=[b_cin[j]], writes=[b_cout[j]])
        for i in range(8):
            for hf in range(2):
                j = 2 * i + hf
                S.dma("sp", ufg[:, i, 64 * hf:64 * hf + 64, :].rearrange("g p t -> p g t"),
                      cout[j].ap().rearrange("(g a) (b t) -> (a b) g t", g=4, b=2), reads=[b_cout[j]])
        if with_ssq:
            S.dma("sp", sin_.ap().rearrange("a b -> (a b)").rearrange("(o n) -> o n", o=1), sloc[:, :], writes=[b_sin])
            S.cc(GROUPS4, sin_, sout, reads=[b_sin], writes=[b_sout])
        barrier()

    modT = _sb(nc, "modT", [128, depth, 24, 2], F32)
    gv = _sb(nc, "gv", [128, depth, 8], F32)
    b_mod, b_gv = B("modT"), B("gv")
    env = {"nc": nc, "S": S, "banks": banks, "b_bank": b_bank, "ropeR": ropeR, "ropeC": ropeC, "cf": cf, "cb": cb}
    for k_, shp, dt_ in (("naS", [12, 128, nt], BF16), ("glS", [8, 128, nt], F32), ("lrS", [32, nt], F32),
                         ("vS", [nt, 1024], BF16), ("gpS", [2, 3, 128, 2, nt], BF16), ("oS", [2, 128, 4, nt], F32)):
        env[k_] = _dint(nc, k_, shp, dt_)
    S.dma("sp", gv[:], gvec[:, :, :], writes=[b_gv])

    ph = []

    def alloc(name, shape, dt):
        t = nc.sbuf_tensor(_u(name), list(shape), dt)
        ph.append(t)
        return t.__enter__()

    def free_all():
        S.flush(barrier=True)
        while ph:
            ph.pop().__exit__(None, None, None)

    ct = alloc("ct", [128, 32, 2], F32)
    st = alloc("st", [128, 32, 2], F32)
    NW = 3
    wt = [alloc("wt%d" % i, [128, 3072], F32) for i in range(NW)]
    bt = alloc("bt", [2, 3072], F32)
    rt = alloc("rt", [2, 3072], F32)
    idf = alloc("idf", [128, 128], F32)
    b_ct, b_st, b_bt, b_rt, b_idf = B("ct"), B("st"), B("bt"), B("rt"), B("idf")
    b_wt = [B("wt%d" % i) for i in range(NW)]
    S.dma("sp", ct[:], cT2[:, :, :], writes=[b_ct])
    S.dma("sp", idf[:], cf[:, 4, :], writes=[b_idf])
    S.op("act", lambda e: e.activation(out=st[:], in_=ct[:], func=AF.Silu), reads=[b_ct], writes=[b_st])
    it = 0
    for l in range(depth):
        S.dma("sp", bt[:], adab[l, :, :], writes=[b_bt])
        for kc in range(32):
            w = it % NW
            S.dma("sp", wt[w][:], adaw[l, kc, :, :], writes=[b_wt[w]])
            for j in range(6):
                S.op("pe", lambda e, w=w, j=j, kc=kc: e.matmul(
                    banks[j][0:2, :], lhsT=st[:, kc, :], rhs=wt[w][:, j * 512:(j + 1) * 512],
                    start=(kc == 0), stop=(kc == 31)), reads=[b_st, b_wt[w]], writes=[b_bank[j]])
            it += 1
        for j in range(6):
            S.op("dve", lambda e, j=j: e.tensor_tensor(
                out=rt[:, j * 512:(j + 1) * 512], in0=banks[j][0:2, :], in1=bt[:, j * 512:(j + 1) * 512], op=ALU.add),
                reads=[b_bank[j], b_bt], writes=[b_rt])
        for j in range(24):
            S.op("pe", lambda e, j=j: e.matmul(banks[6][:, 2 * j:2 * j + 2], lhsT=rt[0:2, j * 128:(j + 1) * 128],
                                             rhs=idf[0:2, 0:2], start=True, stop=True),
                 reads=[b_rt, b_idf], writes=[b_bank[6]])
        S.op("dve", lambda e, l=l: e.tensor_copy(out=modT[:, l, :, :], in_=banks[6][:, 0:48].rearrange("p (j s) -> p j s", s=2)),
             reads=[b_bank[6]], writes=[b_mod])
    free_all()

    def emit_p(l_next, l_prev, first):
        has_z = not first
        xin = xT if first else xS
        G = alloc("G", [128, 8, 2], F32)
        shb2 = alloc("shb2", [128, 8, 2], BF16)
        ones = alloc("ones_sb", [128, 128], F32)
        ssq_sb = alloc("ssq_sb", [1, nt], F32)
        xt = [alloc("xt%d" % i, [128, 8, PT], F32) for i in range(2)]
        ut = [alloc("ut%d" % i, [128, 8, PT], BF16) for i in range(2)]
        sq = [alloc("sq%d" % i, [128, PT], F32) for i in range(2)]
        b_G, b_ones, b_ssq, b_shb2 = B("G"), B("ones"), B("ssq"), B("shb2")
        b_xt, b_ut, b_sq = [B("xt0"), B("xt1")], [B("ut0"), B("ut1")], [B("sq0"), B("sq1")]
        pss, b_pss = banks[7], b_bank[7]
        if has_z:
            wbf = alloc("wbf", [128, 32, 1024], BF16)
            zt = [alloc("zt%d" % i, [128, 32, PT], BF16) for i in range(2)]
            xn = [alloc("xn%d" % i, [128, 8, PT], F32) for i in range(2)]
            b_wbf = [B("wbf%d" % k) for k in range(32)]
            b_zt, b_xn = [B("zt0"), B("zt1")], [B("xn0"), B("xn1")]
            for kc in range(32):
                S.dma("pool", wbf[:, kc, :], wout[l_prev, kc, :, :], writes=[b_wbf[kc]])
        S.dma("sp", ones[:], cf[:, 0, :], writes=[b_ones])
        S.op("dve", lambda e: e.tensor_scalar(out=G[:], in0=modT[:, l_next, 8:16, :], scalar1=1.0, scalar2=None, op0=ALU.add),
             reads=[b_mod], writes=[b_G])
        for s_ in range(2):
            S.op("dve", lambda e, s_=s_: e.tensor_tensor(out=G[:, :, s_], in0=G[:, :, s_], in1=gv[:, l_next, :], op=ALU.mult),
                 reads=[b_G, b_gv], writes=[b_G])
        S.op("dve", lambda e: e.tensor_copy(out=shb2[:], in_=modT[:, l_next, 0:8, :]), reads=[b_mod], writes=[b_shb2])
        S.dma("sp", uloc[:, :, nt:nt + 2].rearrange("f p t -> p f t"), shb2[:], reads=[b_shb2])
        def ld_p(tt_):
            c0_ = tt_ * PT
            S.dma("sp", xt[tt_ % 2][:], xin[:, :, c0_:c0_ + PT].rearrange("f p t -> p f t"), writes=[b_xt[tt_ % 2]])
            if has_z:
                S.dma("sp", zt[tt_ % 2][:], ufull[:, :, c0_:c0_ + PT].rearrange("k p t -> p k t"), writes=[b_zt[tt_ % 2]])

        for tt in range(npt):
            c0 = tt * PT
            seg = 0 if tt == 0 else 1
            i = tt % 2
            if tt == 0:
                ld_p(0)
            if tt + 1 < npt:
                ld_p(tt + 1)
            for fc in range(8):
                if has_z:
                    pb = fc % 2
                    for kc in range(32):
                        S.op("pe", lambda e, pb=pb, kc=kc, fc=fc, i=i: e.matmul(
                            banks[pb][:, 0:PT], lhsT=wbf[:, kc, fc * 128:(fc + 1) * 128], rhs=zt[i][:, kc, :],
                            start=(kc == 0), stop=(kc == 31)), reads=[b_wbf[kc], b_zt[i]], writes=[b_bank[pb]])
                    S.op("dve", lambda e, pb=pb, fc=fc, i=i, seg=seg: e.scalar_tensor_tensor(
                        out=xn[i][:, fc, :], in0=banks[pb][:, 0:PT], scalar=modT[:, l_prev, 16 + fc, seg:seg + 1],
                        in1=xt[i][:, fc, :], op0=ALU.mult, op1=ALU.add),
                        reads=[b_bank[pb], b_mod, b_xt[i]], writes=[b_xn[i]])
                    src, b_src = xn[i], b_xn[i]
                else:
                    src, b_src = xt[i], b_xt[i]
                q = (tt * 8 + fc) % 2
                S.op("act", lambda e, src=src, fc=fc, q=q: e.activation(out=sq[q][:], in_=src[:, fc, :], func=AF.Square),
                     reads=[b_src], writes=[b_sq[q]])
                S.op("pe", lambda e, q=q, fc=fc: e.matmul(pss[0:1, 0:PT], lhsT=ones[:, 0:1], rhs=sq[q][:],
                                                          start=(fc == 0), stop=(fc == 7)),
                     reads=[b_ones, b_sq[q]], writes=[b_pss])
                S.op("dve", lambda e, src=src, fc=fc, i=i, seg=seg: e.tensor_scalar(
                    out=ut[i][:, fc, :], in0=src[:, fc, :], scalar1=G[:, fc, seg:seg + 1], scalar2=None, op0=ALU.mult),
                    reads=[b_src, b_G], writes=[b_ut[i]])
            S.op("dve", lambda e, c0=c0: e.tensor_copy(out=ssq_sb[:, c0:c0 + PT], in_=pss[0:1, 0:PT]),
                 reads=[b_pss], writes=[b_ssq])
            S.dma("sp", xS[:, :, c0:c0 + PT].rearrange("f p t -> p f t"), src[:], reads=[b_src])
            S.dma("sp", uloc[:, :, c0:c0 + PT].rearrange("f p t -> p f t"), ut[i][:], reads=[b_ut[i]])
        S.dma("sp", sloc[:, :], ssq_sb[:], reads=[b_ssq])
        free_all()

    emit_p(0, None, True)
    exchange(True)
    for l in range(depth):
        env.update({"uT": ufull, "ssq4": ssq4, "wp": wp[l], "biasT": biasT[l], "wd": wd[l], "gng": gng[l],
                    "zT": uloc})
        _build_a(rows, nctx, 9, env)
        exchange(False)
        emit_p(min(l + 1, depth - 1), l, False)
        exchange(True)

    TT = 512
    ones = alloc("ones_f", [128, 128], F32)
    g = alloc("g_f", [128, 8], F32)
    s4 = alloc("s4_f", [4, nt], F32)
    rstd = alloc("rstd_f", [128, nl], F32)
    epsb = alloc("epsb_f", [128, 1], F32)
    xt = [alloc("xtf%d" % i, [128, 8, TT], F32) for i in range(2)]
    ot = [alloc("otf%d" % i, [128, 8, TT], F32) for i in range(2)]
    b_ones, b_g, b_s4, b_rstd, b_eps = B("ones"), B("g"), B("s4"), B("rstd"), B("epsf")
    b_xt, b_ot = [B("x0"), B("x1")], [B("o0"), B("o1")]
    S.dma("sp", ones[:], cf[:, 0, :], writes=[b_ones])
    S.dma("sp", g[:], fg[:, :], writes=[b_g])
    S.dma("sp", s4[:], ssq4, writes=[b_s4])
    S.op("dve", lambda e: e.memset(epsb[:], EPS), writes=[b_eps])
    for tt in range(nl // TT):
        c0 = tt * TT
        i = tt % 2
        S.op("pe", lambda e, i=i, c0=c0: e.matmul(banks[i][:, :], lhsT=ones[0:4, :], rhs=s4[:, nctx + c0:nctx + c0 + TT],
                                                 start=True, stop=True), reads=[b_ones, b_s4], writes=[b_bank[i]])
        S.op("act", lambda e, i=i, c0=c0: e.activation(out=rstd[:, c0:c0 + TT], in_=banks[i][:, :], func=AF.Sqrt,
                                                     scale=1.0 / D, bias=epsb[:, 0:1]),
             reads=[b_bank[i], b_eps], writes=[b_rstd])
        S.op("dve", lambda e, c0=c0: e.reciprocal(out=rstd[:, c0:c0 + TT], in_=rstd[:, c0:c0 + TT]),
             reads=[b_rstd], writes=[b_rstd])
        S.dma("sp", xt[i][:], xS[:, :, nctx + c0:nctx + c0 + TT].rearrange("f p t -> p f t"), writes=[b_xt[i]])
        for fc in range(8):
            S.op("dve", lambda e, i=i, fc=fc, c0=c0: e.scalar_tensor_tensor(
                out=ot[i][:, fc, :], in0=xt[i][:, fc, :], scalar=g[:, fc:fc + 1], in1=rstd[:, c0:c0 + TT],
                op0=ALU.mult, op1=ALU.mult), reads=[b_xt[i], b_g, b_rstd], writes=[b_ot[i]])
        S.dma("sp", oT[:, :, c0:c0 + TT].rearrange("f p t -> p f t"), ot[i][:], reads=[b_ot[i]])
    free_all()
    S.emit()
    return nc, S


def z_perm():
    perm = np.empty(32, np.int64)
    for g in range(4):
        for i in range(8):
            perm[8 * g + i] = (4 * g + i) if i < 4 else (16 + 4 * g + (i - 4))
    return perm


def kernel_fused(x, c, ctx, c_ctx, ada_w, ada_b, norm_g, w_in, na_rpb, gla_w_decay, gla_b_decay, gla_norm_g, w_out,
                 final_norm_g):
    f32 = np.float32
    x, c, ctx, c_ctx = (np.asarray(a, f32) for a in (x, c, ctx, c_ctx))
    ada_w, ada_b, norm_g, w_in, na_rpb = (np.asarray(a, f32) for a in (ada_w, ada_b, norm_g, w_in, na_rpb))
    gla_w_decay, gla_b_decay, gla_norm_g, w_out, final_norm_g = (
        np.asarray(a, f32) for a in (gla_w_decay, gla_b_decay, gla_norm_g, w_out, final_norm_g))
    depth = ada_w.shape[0]
    nb, nl, _ = x.shape
    nctx = ctx.shape[1]
    rows = nl // GRID_W
    nt = nctx + nl
    nc = _prog(("fused", depth, rows, nctx), lambda: build_fused(depth, rows, nctx)[0])
    perm = z_perm()
    rR, rC = rope_tables(rows)
    cfa, cba = const_f32(), const_bf16()
    shared = {}
    ims = []
    for b in range(2):
        for g in range(4):
            fs = np.arange(1024 * g, 1024 * (g + 1))
            im = {}
            xb = np.concatenate([ctx[b][:, fs], x[b][:, fs]], 0)
            im["xT"] = np.ascontiguousarray(xb.T).reshape(8, 128, nt)
            cv = np.stack([c_ctx, c[b]], -1)
            im["cT2"] = pmaj(cv)
            if g not in shared:
                sh = {}
                acols = np.concatenate([fs, D + fs, 2 * D + fs])
                sh["adaw"] = np.ascontiguousarray(ada_w[:, :, acols]).reshape(depth, 32, 128, 3072)
                sh["adab"] = np.ascontiguousarray(np.broadcast_to(ada_b[:, None, acols], (depth, 2, 3072)))
                sh["gvec"] = np.ascontiguousarray(norm_g[:, fs].reshape(depth, 8, 128).transpose(2, 0, 1))
                sh["fg"] = pmaj(final_norm_g[fs])
                wo = w_out[:, :, fs].reshape(depth, 32, 128, 1024)
                sh["wout"] = np.ascontiguousarray(wo[:, perm])
                for l in range(depth):
                    aw = a_weight_inputs(w_in[l], na_rpb[l], gla_w_decay[l], gla_b_decay[l], gla_norm_g[l], g, rows)
                    for p in range(4):
                        sh["wp%d_%d" % (p, l)] = aw["wp%d" % p]
                    sh["biasT_%d" % l] = aw["biasT"]
                    sh["wd_%d" % l] = aw["wd"]
                    sh["gng_%d" % l] = aw["gng"]
                shared[g] = sh
            im.update(shared[g])
            im.update({"ropeR": rR, "ropeC": rC, "cf": cfa, "cb": cba})
            ims.append(im)
    res = _run(nc, ims)
    out = np.empty((2, nl, D), f32)
    for ci in range(8):
        b, g = ci // 4, ci % 4
        out[b][:, 1024 * g:1024 * (g + 1)] = res[ci]["oT"].reshape(1024, nl).T
    return out
```
